# Optimizing a Trainium2 kernel written in Bass

```python
import math
import jax, jax.numpy as jnp
from jax import lax
import numpy as np

D_MODEL = 2048
BATCH = 2
SEQ = 16384
DEPTH = 1
DEC_BATCH = 2
DEC_SEQ = 4096
PAST_LEN = 128

PLE_DIM = 256
ROPE_THETA = 10000.0
RMS_EPS = 1e-6
NEG = -1e30
DIFF_HEADS = 4
DIFF_QK_DIM = 64
DIFF_V_DIM = 128
DIFF_WIDTH = DIFF_HEADS * DIFF_V_DIM
DIFF_QK_WIDTH = DIFF_HEADS * 2 * DIFF_QK_DIM
DIFF_QBLOCK = 128
DIL_PATTERNS = ((128, 1), (512, 4), (2048, 16))
DIL_GROUPS = len(DIL_PATTERNS)
DIL_HEADS = 4
DIL_HEAD_DIM = 128
DIL_QKV_WIDTH = DIL_GROUPS * DIL_HEADS * DIL_HEAD_DIM
DIL_OUT_WIDTH = DIL_HEADS * DIL_HEAD_DIM
IN_SPLITS = (DIFF_QK_WIDTH, DIFF_QK_WIDTH, DIFF_WIDTH, DIFF_WIDTH,
             DIL_QKV_WIDTH, DIL_QKV_WIDTH, DIL_QKV_WIDTH, DIL_OUT_WIDTH)
IN_WIDTH = sum(IN_SPLITS)
MIX_OUT_WIDTH = DIFF_WIDTH + DIL_OUT_WIDTH

kernel_name = "hymba_diff_dilated_encoder"


def rmsnorm(x, g):
    xf = x.astype(jnp.float32)
    y = xf * lax.rsqrt(jnp.mean(xf * xf, axis=-1, keepdims=True) + RMS_EPS)
    return (y * g.astype(jnp.float32)).astype(x.dtype)


def rope(x):
    S, D = x.shape[1], x.shape[-1]
    inv = ROPE_THETA ** (-jnp.arange(0, D, 2, dtype=jnp.float32) / D)
    ang = jnp.arange(S, dtype=jnp.float32)[:, None] * inv[None, :]
    ang = jnp.concatenate([ang, ang], axis=-1).reshape((1, S) + (1,) * (x.ndim - 3) + (D,))
    xf = x.astype(jnp.float32)
    x1, x2 = jnp.split(xf, 2, axis=-1)
    rot = jnp.concatenate([-x2, x1], axis=-1)
    return (xf * jnp.cos(ang) + rot * jnp.sin(ang)).astype(x.dtype)


def diff_attention(q, k, v, lam):
    B, S, H = q.shape[0], q.shape[1], q.shape[2]
    nqb = S // DIFF_QBLOCK
    qb = q.reshape(B, nqb, DIFF_QBLOCK, H, 2, DIFF_QK_DIM).transpose(1, 0, 2, 3, 4, 5)
    scale = DIFF_QK_DIM ** -0.5

    def one_block(qblk):
        s = jnp.einsum('bqhcd,bkhcd->bhcqk', qblk, k).astype(jnp.float32) * scale
        pr = jax.nn.softmax(s, axis=-1)
        a = pr[:, :, 0] - lam * pr[:, :, 1]
        return jnp.einsum('bhqk,bkhd->bqhd', a.astype(v.dtype), v)

    o = lax.map(one_block, qb)
    return o.transpose(1, 0, 2, 3, 4).reshape(B, S, H, DIFF_V_DIM)


def dilated_window_attention(q, k, v, window, dilation):
    B, S, H, D = q.shape
    half = window // (2 * dilation)
    blk = half
    L = S // dilation
    nb = -(-L // blk)
    Lp = nb * blk

    def to_sub(t):
        return t.reshape(B, L, dilation, H, D).transpose(0, 2, 1, 3, 4)

    qs = jnp.pad(to_sub(q), ((0, 0), (0, 0), (0, Lp - L), (0, 0), (0, 0))).reshape(B, dilation, nb, blk, H, D)

    def windows(t):
        tp = jnp.pad(to_sub(t), ((0, 0), (0, 0), (half, Lp - L + half), (0, 0), (0, 0)))
        tp = tp.reshape(B, dilation, nb + 2, blk, H, D)
        return jnp.concatenate([tp[:, :, :-2], tp[:, :, 1:-1], tp[:, :, 2:]], axis=3)

    kw, vw = windows(k), windows(v)
    s = jnp.einsum('brnqhd,brnkhd->brnhqk', qs, kw).astype(jnp.float32) * (D ** -0.5)
    t = jnp.arange(blk)[:, None]
    kk = jnp.arange(3 * blk)[None, :]
    band = (kk >= t) & (kk <= t + 2 * half)
    idx = jnp.arange(nb)[:, None] * blk + jnp.arange(3 * blk)[None, :] - half
    valid = (idx >= 0) & (idx < L)
    mask = band[None] & valid[:, None, :]
    s = jnp.where(mask[None, None, :, None], s, NEG)
    m = jnp.max(s, axis=-1, keepdims=True)
    e = jnp.exp(s - m)
    den = jnp.sum(e, axis=-1, keepdims=True)
    o = jnp.einsum('brnhqk,brnkhd->brnqhd', (e / den).astype(v.dtype), vw)
    lse = (m + jnp.log(den))[..., 0]
    o = o.reshape(B, dilation, Lp, H, D)[:, :, :L].transpose(0, 2, 1, 3, 4).reshape(B, S, H, D)
    lse = lse.transpose(0, 1, 2, 4, 3).reshape(B, dilation, Lp, H)[:, :, :L]
    lse = lse.transpose(0, 2, 1, 3).reshape(B, S, H)
    return o, lse


def hybrid_layer(h, p_i, layer_idx, norm_mix, w_in, lam_q1, lam_k1, lam_q2, lam_k2,
                 subln, w_out, ple_norm, w_ple_gate, w_ple_proj):
    B, S, _ = h.shape
    u = rmsnorm(h, norm_mix)
    z = u @ w_in
    split_idx = np.cumsum(IN_SPLITS)[:-1].tolist()
    aq, ak, av, ag, bq, bk, bv, bg = jnp.split(z, split_idx, axis=-1)

    aq = rope(aq.reshape(B, S, DIFF_HEADS, 2, DIFF_QK_DIM))
    ak = rope(ak.reshape(B, S, DIFF_HEADS, 2, DIFF_QK_DIM))
    av = av.reshape(B, S, DIFF_HEADS, DIFF_V_DIM)
    lam_init = 0.8 - 0.6 * math.exp(-0.3 * layer_idx)
    lam = (jnp.exp(jnp.sum(lam_q1.astype(jnp.float32) * lam_k1.astype(jnp.float32)))
           - jnp.exp(jnp.sum(lam_q2.astype(jnp.float32) * lam_k2.astype(jnp.float32))) + lam_init)
    oa = diff_attention(aq, ak, av, lam)
    oa = rmsnorm(oa, subln) * (1.0 - lam_init)
    ya = oa.reshape(B, S, DIFF_WIDTH) * jax.nn.silu(ag)

    bq = rope(bq.reshape(B, S, DIL_GROUPS, DIL_HEADS, DIL_HEAD_DIM))
    bk = rope(bk.reshape(B, S, DIL_GROUPS, DIL_HEADS, DIL_HEAD_DIM))
    bv = bv.reshape(B, S, DIL_GROUPS, DIL_HEADS, DIL_HEAD_DIM)
    outs, lses = [], []
    for g, (window, dilation) in enumerate(DIL_PATTERNS):
        o_g, lse_g = dilated_window_attention(bq[:, :, g], bk[:, :, g], bv[:, :, g], window, dilation)
        outs.append(o_g)
        lses.append(lse_g)
    wts = jax.nn.softmax(jnp.stack(lses, axis=0), axis=0)
    ob = jnp.sum(wts[..., None] * jnp.stack(outs, axis=0).astype(jnp.float32), axis=0).astype(h.dtype)
    yb = ob.reshape(B, S, DIL_OUT_WIDTH) * jax.nn.silu(bg)

    h = h + jnp.concatenate([ya, yb], axis=-1) @ w_out

    gate = jax.nn.sigmoid(rmsnorm(h, ple_norm) @ w_ple_gate)
    h = h + gate * (p_i @ w_ple_proj)
    return h


def setup_inputs(seed: int = 0) -> dict:
    key = jax.random.key(seed)
    ks = jax.random.split(key, 16)
    f32 = jnp.float32
    nrm = lambda k, shape, s: jax.random.normal(k, shape, f32) * s
    return {
        "x_prompt": nrm(ks[0], (BATCH, SEQ, D_MODEL), 1.0),
        "x_sample": nrm(ks[1], (DEC_BATCH, DEC_SEQ, D_MODEL), 1.0),
        "p_prompt": nrm(ks[2], (DEPTH, BATCH, SEQ, PLE_DIM), 1.0),
        "p_sample": nrm(ks[3], (DEPTH, DEC_BATCH, DEC_SEQ, PLE_DIM), 1.0),
        "norm_mix": 1.0 + nrm(ks[4], (DEPTH, D_MODEL), 0.01),
        "w_in": nrm(ks[5], (DEPTH, D_MODEL, IN_WIDTH), D_MODEL ** -0.5),
        "lam_q1": nrm(ks[6], (DEPTH, DIFF_QK_DIM), 0.1),
        "lam_k1": nrm(ks[7], (DEPTH, DIFF_QK_DIM), 0.1),
        "lam_q2": nrm(ks[8], (DEPTH, DIFF_QK_DIM), 0.1),
        "lam_k2": nrm(ks[9], (DEPTH, DIFF_QK_DIM), 0.1),
        "subln": 1.0 + nrm(ks[10], (DEPTH, DIFF_V_DIM), 0.01),
        "w_out": nrm(ks[11], (DEPTH, MIX_OUT_WIDTH, D_MODEL), MIX_OUT_WIDTH ** -0.5),
        "ple_norm": 1.0 + nrm(ks[12], (DEPTH, D_MODEL), 0.01),
        "w_ple_gate": nrm(ks[13], (DEPTH, D_MODEL, D_MODEL), D_MODEL ** -0.5),
        "w_ple_proj": nrm(ks[14], (DEPTH, PLE_DIM, D_MODEL), PLE_DIM ** -0.5),
        "final_norm": 1.0 + nrm(ks[15], (D_MODEL,), 0.01),
    }


def reference(x_prompt, x_sample, p_prompt, p_sample, norm_mix, w_in, lam_q1, lam_k1,
              lam_q2, lam_k2, subln, w_out, ple_norm, w_ple_gate, w_ple_proj, final_norm):
    def trunk(x, p):
        h = x
        for i in range(DEPTH):
            h = hybrid_layer(h, p[i], i, norm_mix[i], w_in[i], lam_q1[i], lam_k1[i],
                             lam_q2[i], lam_k2[i], subln[i], w_out[i], ple_norm[i],
                             w_ple_gate[i], w_ple_proj[i])
        return rmsnorm(h, final_norm)

    y_prompt = trunk(x_prompt, p_prompt)
    y_sample = trunk(x_sample, p_sample)
    return (y_prompt, y_sample)
```

```python
import numpy as np
import ml_dtypes
from contextlib import ExitStack
import concourse.bass as bass
import concourse.mybir as mybir
from concourse.bass_utils import run_bass_kernel_spmd

F32 = mybir.dt.float32
BF16 = mybir.dt.bfloat16
AF = mybir.ActivationFunctionType
ALU = mybir.AluOpType
AX = mybir.AxisListType
bf16 = ml_dtypes.bfloat16

D = 2048
NCH = 14
PAD = 1024
EPS = 1e-6
DILS = (1, 4, 16)
C_AQ, C_AK, C_AV, C_AG = 0, 1, 2, 3
C_BQ, C_BK, C_BV, C_BG = 4, 7, 10, 13
ROPE_DIFF = (C_AQ, C_AK)
ROPE_DIL = (4, 5, 6, 7, 8, 9)
COPY_CH = (C_AV, 10, 11, 12)
GATE_CH = (C_AG, C_BG)
NEGM = -30000.0


class Res:
    __slots__ = ("lw", "rd")

    def __init__(self):
        self.lw = None
        self.rd = {}


class Op:
    __slots__ = ("eng", "fn", "deps", "dma", "need", "sem", "sigval", "inc", "dwaits", "done")


class Sched:
    ENGS = ("pe", "act", "dve", "pool", "sp")

    def __init__(self, nc, es):
        self.nc = nc
        self.es = es
        self.sem = {e: es.enter_context(nc.semaphore("s_" + e)) for e in self.ENGS}
        self.cnt = {e: 0 for e in self.ENGS}
        self.gsem = {}
        self.gcnt = {}
        self.ops = []
        self.waited = {e: {} for e in self.ENGS}
        self.nres = 0

    def res(self, n=None):
        if n is None:
            return Res()
        return [Res() for _ in range(n)]

    def _dep(self, o, p, kind):
        if p is o or p.done:
            return
        if p.dma is None and o.dma is None and p.eng == o.eng:
            if o.eng == "pe":
                return
        o.deps.add(p)

    def op(self, eng, fn, reads=(), writes=(), dma=None, inc=None):
        o = Op()
        o.eng, o.fn, o.dma, o.deps, o.need = eng, fn, dma, set(), False
        o.sem = o.sigval = None
        o.inc = inc
        o.done = False
        for r in reads:
            if r.lw is not None:
                self._dep(o, r.lw, "raw")
        for w in writes:
            if w.lw is not None:
                self._dep(o, w.lw, "waw")
            for rr in w.rd.values():
                self._dep(o, rr, "war")
        for w in writes:
            w.lw = o
            w.rd = {}
        for r in reads:
            r.rd[eng if dma is None else ("d", dma)] = o
        o.dwaits = {}
        for p in o.deps:
            if p.dma is not None:
                o.dwaits[p.dma] = self.gcnt[p.dma]
        if dma is not None:
            if dma not in self.gsem:
                self.gsem[dma] = self.es.enter_context(self.nc.semaphore("g_" + dma))
                self.gcnt[dma] = 0
            o.inc = 16 if inc is None else inc
            self.gcnt[dma] += o.inc
            o.sem, o.sigval = self.gsem[dma], self.gcnt[dma]
        self.ops.append(o)
        return o

    def flush(self, final=False, drain_cc=True):
        nc = self.nc
        ops = self.ops
        self.ops = []
        for o in ops:
            o.done = True
            for p in o.deps:
                p.need = True
        for o in ops:
            if o.dma is None and o.need:
                self.cnt[o.eng] += 1
                o.sem, o.sigval, o.inc = self.sem[o.eng], self.cnt[o.eng], 1
        per = {e: [] for e in self.ENGS}
        for o in ops:
            per[o.eng].append(o)
        gs = [(self.gsem[g], self.gcnt[g]) for g in self.gsem]

        def mk(ename):
            def body(e):
                wd = self.waited[ename]
                for o in per[ename]:
                    ws = {}
                    for p in o.deps:
                        if p.dma is None:
                            k = id(p.sem)
                            if k not in ws or ws[k][1] < p.sigval:
                                ws[k] = (p.sem, p.sigval)
                    for g, v in o.dwaits.items():
                        ws[id(self.gsem[g])] = (self.gsem[g], v)
                    for k, (s, v) in ws.items():
                        if wd.get(k, 0) < v:
                            e.wait_ge(s, v)
                            wd[k] = v
                    ins = o.fn(e)
                    if o.sigval is not None:
                        ins.then_inc(o.sem, o.inc)
                if ename in ("sp", "pool"):
                    for g_, (s, v) in zip(list(self.gsem), gs):
                        if (g_ == "cc") != (ename == "pool"):
                            continue
                        if g_ == "cc" and not drain_cc:
                            continue
                        if v > 0 and wd.get(id(s), 0) < v:
                            e.wait_ge(s, v)
                            wd[id(s)] = v
            return body

        with nc.Block() as block:
            block.tensor(mk("pe"))
            block.scalar(mk("act"))
            block.vector(mk("dve"))
            block.gpsimd(mk("pool"))
            block.sync(mk("sp"))


def _consts(smax):
    ident = np.eye(128, dtype=np.float32)
    rdil = np.zeros((128, 128), np.float32)
    for f in range(64):
        rdil[f + 64, f] = -1.0
        rdil[f, f + 64] = 1.0
    rdiff = np.zeros((128, 128), np.float32)
    for b0 in (0, 64):
        for f in range(32):
            rdiff[b0 + f + 32, b0 + f] = -1.0
            rdiff[b0 + f, b0 + f + 32] = 1.0
    kp = np.arange(128)[:, None]
    qf = np.arange(128)[None, :]
    lo = qf >= kp
    hi = qf <= kp
    masks = np.stack([lo, hi, lo & (kp < 64), hi & (kp >= 64)], 1)
    masks = np.where(masks, 0.0, NEGM).astype(np.float32)
    cm = np.concatenate([ident[:, None, :], rdil[:, None, :], rdiff[:, None, :],
                         np.ones((128, 1, 128), np.float32), masks], 1)
    pos = np.arange(smax, dtype=np.float32)
    inv_dil = (10000.0 ** (-np.arange(0, 128, 2, dtype=np.float32) / 128)).astype(np.float32)
    inv_dif = (10000.0 ** (-np.arange(0, 64, 2, dtype=np.float32) / 64)).astype(np.float32)
    a_dil = (pos[None, :] * inv_dil[np.arange(128) % 64][:, None]).astype(np.float32)
    a_dif = (pos[None, :] * inv_dif[(np.arange(128) % 64) % 32][:, None]).astype(np.float32)
    tabs = np.stack([np.cos(a_dif), np.sin(a_dif), np.cos(a_dil), np.sin(a_dil)], 1)
    return cm.astype(bf16), np.ascontiguousarray(tabs.astype(np.float32))


def build(SP, SS, dbg=False, upto=5):
    nc = bass.Bass("TRN2", target_bir_lowering=False)
    seqs = [("p", SP), ("s", SS)]
    SMAX = max(SP, SS)
    TT = SP // 4 + SS // 4
    TOT = SP + SS
    din = {}

    def inp(name, shape, dt=F32):
        din[name] = nc.dram_tensor(name, list(shape), dt, kind="ExternalInput").ap()
        return din[name]

    x_in = {"p": inp("xp", [SP, D]), "s": inp("xs", [SS, D])}
    xq_in = {"p": inp("xqp", [SP // 4, D]), "s": inp("xqs", [SS // 4, D])}
    wh = inp("wh", [D, NCH * 128])
    nmix = inp("nmix", [128, 16])
    lamv = inp("lamv", [4, 64])
    subln = inp("subln", [1, 128])
    wout = inp("wout", [1024, D])
    pnorm = inp("pnorm", [128, 16])
    wgate = inp("wgate", [D, D])
    wproj = inp("wproj", [256, D])
    fnorm = inp("fnorm", [1, D])
    pT = inp("pT", [256, TT])
    cmat = inp("cmat", [128, 8, 128], BF16)
    tabs = inp("tabs", [128, 4, SMAX])
    out_p = nc.dram_tensor("out_p", [SP // 4, D], F32, kind="ExternalOutput").ap()
    out_s = nc.dram_tensor("out_s", [SS // 4, D], F32, kind="ExternalOutput").ap()
    Z = {k: nc.dram_tensor("Z" + k, [NCH, 128, S + 2 * PAD], BF16).ap() for k, S in seqs}
    CW = 1024
    NCK = TOT // CW
    ysend = nc.dram_tensor("ysend", [NCK * 256, CW], BF16)
    yall = nc.dram_tensor("yall", [NCK * 1024, CW], BF16)
    if dbg:
        dbg_z = nc.dram_tensor("dbg_z", [NCH, 128, SS + 2 * PAD], BF16, kind="ExternalOutput").ap()
        dbg_y = nc.dram_tensor("dbg_y", [(TOT // 1024) * 256, 1024], BF16, kind="ExternalOutput").ap()

    es = ExitStack()
    with es:
        sc = Sched(nc, es)
        op = sc.op
        cm = es.enter_context(nc.sbuf_tensor("cm", [128, 8, 128], BF16))
        ident, rdil, rdiff, ones_b = cm[:, 0, :], cm[:, 1, :], cm[:, 2, :], cm[:, 3, :]
        lam_t = es.enter_context(nc.sbuf_tensor("lam_t", [128, 8], F32))
        r_cm = sc.res()
        r_lam = sc.res()

        with ExitStack() as ps:
            sb = lambda n, s, d: ps.enter_context(nc.sbuf_tensor(n, s, d))
            pp = lambda n, s, d: ps.enter_context(nc.psum_tensor(n, s, d))
            Wb = sb("Wb", [128, 16, NCH * 128], BF16)
            xs_t = [sb("xs%d" % i, [128, D], F32) for i in range(4)]
            junk = sb("junk", [128, D], BF16)
            st_t = [sb("st%d" % i, [128, 8], F32) for i in range(2)]
            xb_t = [sb("xb%d" % i, [128, D], BF16) for i in range(2)]
            uT_t = [sb("uT%d" % i, [128, 16, 512], BF16) for i in range(2)]
            tab_t = [sb("tab%d" % i, [128, 4, 512], F32) for i in range(2)]
            stg_t = [sb("stg%d" % i, [128, NCH, 512], BF16) for i in range(2)]
            zb_t = [sb("zb%d" % i, [128, 512], BF16) for i in range(2)]
            t1_t = [sb("t1%d" % i, [128, 512], F32) for i in range(2)]
            t2_t = [sb("t2%d" % i, [128, 512], F32) for i in range(2)]
            ge_t = [sb("ge%d" % i, [128, 512], F32) for i in range(2)]
            ones_f = sb("ones_f", [128, 512], F32)
            nm_t = sb("nm_t", [128, 16], F32)
            lv_t = sb("lv_t", [128, 4, 64], F32)
            zero_t = sb("zero_t", [128, PAD], BF16)
            pT_ps = [pp("pT%d" % i, [128, D], BF16) for i in range(2)]
            pZ = [pp("pZ%d" % i, [128, 512], F32) for i in range(2)]
            pR = [pp("pR%d" % i, [128, 512], F32) for i in range(2)]
            r_W, r_nm, r_ones, r_zero, r_lv = sc.res(), sc.res(), sc.res(), sc.res(), sc.res()
            r_xs, r_st, r_xb = sc.res(4), sc.res(2), sc.res(2)
            r_uT = [[[sc.res(), sc.res()] for _ in range(4)] for _ in range(2)]
            r_tab, r_zb, r_t1, r_t2, r_ge = sc.res(2), sc.res(2), sc.res(2), sc.res(2), sc.res(2)
            r_stg = [sc.res(NCH) for _ in range(2)]
            r_pT, r_pZ, r_pR = sc.res(2), sc.res(2), sc.res(2)

            op("sp", lambda e: e.dma_start(out=cm[:], in_=cmat), writes=[r_cm], dma="cm")
            op("sp", lambda e: e.dma_start(out=nm_t[:], in_=nmix), writes=[r_nm], dma="cm")
            op("sp", lambda e: e.dma_start(out=lv_t[:].rearrange("p a b -> p (a b)"),
                                           in_=lamv.rearrange("a b -> (a b)").partition_broadcast(128)),
               writes=[r_lv], dma="cm")
            op("dve", lambda e: e.memset(ones_f[:], 1.0), writes=[r_ones])
            op("dve", lambda e: e.memset(zero_t[:], 0.0), writes=[r_zero])
            for k, S in seqs:
                for c in range(7, 13):
                    for off in (0, PAD + S):
                        op("pool", lambda e, k=k, c=c, off=off: e.dma_start(
                            out=Z[k][c, :, off:off + PAD], in_=zero_t[:]), reads=[r_zero], dma="zp")
            op("dve", lambda e: e.tensor_tensor(out=lv_t[:, 0, :], in0=lv_t[:, 0, :], in1=lv_t[:, 1, :], op=ALU.mult),
               reads=[r_lv], writes=[r_lv])
            op("dve", lambda e: e.tensor_tensor(out=lv_t[:, 2, :], in0=lv_t[:, 2, :], in1=lv_t[:, 3, :], op=ALU.mult),
               reads=[r_lv], writes=[r_lv])
            op("dve", lambda e: e.reduce_sum(out=lam_t[:, 2:3], in_=lv_t[:, 0, :], axis=AX.X), reads=[r_lv], writes=[r_lam])
            op("dve", lambda e: e.reduce_sum(out=lam_t[:, 3:4], in_=lv_t[:, 2, :], axis=AX.X), reads=[r_lv, r_lam], writes=[r_lam])
            op("act", lambda e: e.activation(out=lam_t[:, 4:6], in_=lam_t[:, 2:4], func=AF.Exp), reads=[r_lam], writes=[r_lam])
            op("dve", lambda e: e.tensor_tensor(out=lam_t[:, 6:7], in0=lam_t[:, 4:5], in1=lam_t[:, 5:6], op=ALU.subtract),
               reads=[r_lam], writes=[r_lam])
            op("dve", lambda e: e.tensor_scalar(out=lam_t[:, 0:1], in0=lam_t[:, 6:7], scalar1=0.2, scalar2=None, op0=ALU.add),
               reads=[r_lam], writes=[r_lam])
            op("dve", lambda e: e.tensor_scalar(out=lam_t[:, 1:2], in0=lam_t[:, 0:1], scalar1=-1.0, scalar2=None, op0=ALU.mult),
               reads=[r_lam], writes=[r_lam])
            for kc in range(16):
                sl = kc % 3
                for hf in range(2):
                    cs = slice(hf * 896, (hf + 1) * 896)
                    op("sp", lambda e, kc=kc, sl=sl, cs=cs: e.dma_start(out=xs_t[sl][:, 0:896], in_=wh[kc * 128:(kc + 1) * 128, cs]),
                       writes=[r_xs[sl]], dma="x%d" % sl)
                    op("dve", lambda e, kc=kc, sl=sl, cs=cs: e.tensor_scalar(out=Wb[:, kc, cs], in0=xs_t[sl][:, 0:896],
                                                                            scalar1=nm_t[:, kc:kc + 1], scalar2=None, op0=ALU.mult),
                       reads=[r_xs[sl], r_nm], writes=[r_W])

            blocks = [(k, S, b) for k, S in seqs for b in range(S // 512)]
            import os as _os
            if _os.environ.get("KDBG_NBLK"):
                blocks = blocks[:int(_os.environ["KDBG_NBLK"])]
            gtile = [0]

            NX = 4
            NT = len(blocks) * 4

            def tile_info(g):
                bi, ti = g // 4, g % 4
                k, S, b = blocks[bi]
                return bi, ti, k, b * 512 + ti * 128

            def prep_load(g):
                if g >= NT:
                    return
                bi, ti, k, t0 = tile_info(g)
                xsl = g % NX
                op("sp", lambda e: e.dma_start(out=xs_t[xsl][:], in_=x_in[k][t0:t0 + 128, :]), writes=[r_xs[xsl]], dma="x%d" % xsl)

            def prep_A(g):
                if g >= NT:
                    return
                xsl, s2 = g % NX, g % 2
                op("act", lambda e: e.activation(out=junk[:], in_=xs_t[xsl][:], func=AF.Square, accum_out=st_t[s2][:, 0:1]),
                   reads=[r_xs[xsl]], writes=[r_st[s2]])
                op("dve", lambda e: e.tensor_scalar(out=st_t[s2][:, 1:2], in0=st_t[s2][:, 0:1], scalar1=1.0 / D, scalar2=EPS,
                                                    op0=ALU.mult, op1=ALU.add), reads=[r_st[s2]], writes=[r_st[s2]])
                op("act", lambda e: e.activation(out=st_t[s2][:, 2:3], in_=st_t[s2][:, 1:2], func=AF.Ln), reads=[r_st[s2]], writes=[r_st[s2]])
                op("act", lambda e: e.activation(out=st_t[s2][:, 3:4], in_=st_t[s2][:, 2:3], func=AF.Exp, scale=-0.5),
                   reads=[r_st[s2]], writes=[r_st[s2]])
                op("dve", lambda e: e.tensor_scalar(out=xb_t[s2][:], in0=xs_t[xsl][:], scalar1=st_t[s2][:, 3:4], scalar2=None, op0=ALU.mult),
                   reads=[r_xs[xsl], r_st[s2]], writes=[r_xb[s2]])
                prep_load(g + 2)

            def prep_B(g):
                if g >= NT:
                    return
                bi, ti, k, t0 = tile_info(g)
                s2 = g % 2
                us = bi % 2
                for kc in range(16):
                    op("pe", lambda e, kc=kc: e.transpose(pT_ps[s2][:, kc * 128:(kc + 1) * 128], xb_t[s2][:, kc * 128:(kc + 1) * 128], ident),
                       reads=[r_xb[s2], r_cm], writes=[r_pT[s2]])
                src = pT_ps[s2][:].rearrange("p (a b) -> p a b", b=128)
                op("act", lambda e: e.activation(out=uT_t[us][:, 0:8, ti * 128:(ti + 1) * 128], in_=src[:, 0:8, :], func=AF.Copy),
                   reads=[r_pT[s2]], writes=[r_uT[us][ti][0]])
                op("dve", lambda e: e.tensor_copy(out=uT_t[us][:, 8:16, ti * 128:(ti + 1) * 128], in_=src[:, 8:16, :]),
                   reads=[r_pT[s2]], writes=[r_uT[us][ti][1]])

            def rot_part(bi, c, zs):
                us = bi % 2
                isdil = c in ROPE_DIL
                rm = rdil if isdil else rdiff
                ti = 2 if isdil else 0
                op("pe", lambda e: e.matmul(pR[zs][:], lhsT=rm, rhs=zb_t[zs][:], start=True, stop=True),
                   reads=[r_zb[zs], r_cm], writes=[r_pR[zs]])
                op("dve", lambda e: e.tensor_tensor(out=t2_t[zs][:], in0=pR[zs][:], in1=tab_t[us][:, ti + 1, :], op=ALU.mult),
                   reads=[r_pR[zs], r_tab[us]], writes=[r_t2[zs]])
                op("dve", lambda e: e.tensor_tensor(out=stg_t[us][:, c, :], in0=t1_t[zs][:], in1=t2_t[zs][:], op=ALU.add),
                   reads=[r_t1[zs], r_t2[zs]], writes=[r_stg[us][c]])

            if blocks:
                prep_load(0)
                prep_load(1)
                prep_A(0)
                prep_A(1)
                prep_B(0)
                prep_A(2)
                prep_B(1)
                prep_A(3)
                prep_B(2)
                prep_B(3)
            zc = 0
            for bi, (k, S, b) in enumerate(blocks):
                us = bi % 2
                t0 = b * 512
                op("sp", lambda e, us=us, t0=t0: e.dma_start(out=tab_t[us][:], in_=tabs[:, :, t0:t0 + 512]), writes=[r_tab[us]], dma="tab%d" % us)
                pend = None
                _cut = _os.environ.get("KDBG_CUT", "")
                for c in range(NCH):
                    if _cut == "prep":
                        break
                    if _cut == "rope" and c not in (0,):
                        continue
                    if _cut == "copy" and c not in (2,):
                        continue
                    if _cut == "gate" and c not in (3,):
                        continue
                    zs = zc % 2
                    zc += 1
                    uall = [r for t in r_uT[us] for r in t]
                    for kc in range(16):
                        op("pe", lambda e, c=c, kc=kc, zs=zs, us=us: e.matmul(pZ[zs][:], lhsT=Wb[:, kc, c * 128:(c + 1) * 128],
                                                                              rhs=uT_t[us][:, kc, :], start=(kc == 0), stop=(kc == 15)),
                           reads=[r_W] + uall, writes=[r_pZ[zs]])
                    if c in ROPE_DIFF or c in ROPE_DIL:
                        ti_ = 2 if c in ROPE_DIL else 0
                        op("act", lambda e, zs=zs: e.activation(out=zb_t[zs][:], in_=pZ[zs][:], func=AF.Copy),
                           reads=[r_pZ[zs]], writes=[r_zb[zs], r_pZ[zs]])
                        op("dve", lambda e, zs=zs, us=us, ti_=ti_: e.tensor_tensor(out=t1_t[zs][:], in0=pZ[zs][:], in1=tab_t[us][:, ti_, :], op=ALU.mult),
                           reads=[r_pZ[zs], r_tab[us]], writes=[r_t1[zs], r_pZ[zs]])
                    elif c in COPY_CH:
                        op("act", lambda e, zs=zs, us=us, c=c: e.activation(out=stg_t[us][:, c, :], in_=pZ[zs][:], func=AF.Copy),
                           reads=[r_pZ[zs]], writes=[r_stg[us][c]])
                    else:
                        op("act", lambda e, zs=zs: e.activation(out=ge_t[zs][:], in_=pZ[zs][:], func=AF.Exp, scale=-1.0),
                           reads=[r_pZ[zs]], writes=[r_ge[zs]])
                        op("pool", lambda e, zs=zs: e.tensor_tensor(out=ge_t[zs][:], in0=ge_t[zs][:], in1=ones_f[:], op=ALU.add),
                           reads=[r_ge[zs], r_ones], writes=[r_ge[zs]])
                        op("dve", lambda e, zs=zs: e.reciprocal(out=ge_t[zs][:], in_=ge_t[zs][:]), reads=[r_ge[zs]], writes=[r_ge[zs]])
                        op("dve", lambda e, zs=zs, us=us, c=c: e.tensor_tensor(out=stg_t[us][:, c, :], in0=pZ[zs][:], in1=ge_t[zs][:], op=ALU.mult),
                           reads=[r_pZ[zs], r_ge[zs]], writes=[r_stg[us][c]])
                    if pend is not None:
                        rot_part(bi, *pend)
                        pend = None
                    if c in ROPE_DIFF or c in ROPE_DIL:
                        pend = (c, zs)
                    if c in (0, 3, 6, 9):
                        prep_A((bi + 1) * 4 + c // 3)
                    if c in (2, 5, 8, 11):
                        prep_B((bi + 1) * 4 + (c - 2) // 3)
                if pend is not None:
                    rot_part(bi, *pend)
                    pend = None
                if _cut:
                    continue
                op("pool", lambda e, k=k, us=us, t0=t0: e.dma_start(
                    out=Z[k][:, :, PAD + t0:PAD + t0 + 512].rearrange("c p t -> p c t"), in_=stg_t[us][:]),
                   reads=r_stg[us], dma="stg%d" % us)
            sc.flush()
            if dbg:
                op("sp", lambda e: e.dma_start(out=dbg_z, in_=Z["s"]), dma="dbg")
                sc.flush()

        if upto < 2:
            return nc
        T2 = 2048
        scale_b = 128.0 ** -0.5
        with ExitStack() as ps:
            sb = lambda n, s, d: ps.enter_context(nc.sbuf_tensor(n, s, d))
            pp = lambda n, s, d: ps.enter_context(nc.psum_tensor(n, s, d))
            qb_t = [[sb("q%d_%d" % (i, g), [128, T2], BF16) for g in range(3)] for i in range(2)]
            kb_t = [[sb("k%d_%d" % (i, g), [128, T2 + 128 * DILS[g]], BF16) for g in range(3)] for i in range(2)]
            vb_t = [[sb("v%d_%d" % (i, g), [128, T2 + 128 * DILS[g]], BF16) for g in range(3)] for i in range(2)]
            gb_t = [sb("gb%d" % i, [128, T2], BF16) for i in range(2)]
            accO = [sb("accO%d" % i, [128, T2], F32) for i in range(2)]
            accD = [sb("accD%d" % i, [128, T2], F32) for i in range(2)]
            PT_t = [sb("PT%d" % i, [128, 256], BF16) for i in range(3)]
            Vm_t = [sb("Vm%d" % i, [128, 128], BF16) for i in range(3)]
            yst = [sb("yst%d" % i, [128, T2], BF16) for i in range(2)]
            pS = [pp("pS%d" % i, [128, 512], F32) for i in range(2)]
            pO = [pp("pO%d" % i, [128, 512], F32) for i in range(2)]
            pD = [pp("pD%d" % i, [128, 512], F32) for i in range(2)]
            pV_ = [pp("pV%d" % i, [128, 1024], BF16) for i in range(2)]
            r_q = [sc.res(3) for _ in range(2)]
            r_k = [sc.res(3) for _ in range(2)]
            r_v = [sc.res(3) for _ in range(2)]
            r_gb, r_aO, r_aD, r_PT, r_Vm, r_yst = sc.res(2), sc.res(2), sc.res(2), sc.res(3), sc.res(3), sc.res(2)
            r_pS, r_pO, r_pD, r_pV = sc.res(2), sc.res(2), sc.res(2), sc.res(2)
            masks = cm[:, 4:8, :]

            def strided(t, start, n, r):
                if r == 1:
                    return t[:, start:start + n]
                return t[:, start:start + (n - 1) * r + 1:r]

            units = [(k, S, u) for k, S in seqs for u in range(S // T2)]
            tcnt = [0]
            sucnt = [0]
            tokoff = {"p": 0, "s": SP}
            for ui, (k, S, u) in enumerate(units):
                us = ui % 2
                t0 = u * T2
                for g in range(3):
                    r = DILS[g]
                    op("sp", lambda e, g=g, us=us, k=k, t0=t0: e.dma_start(out=qb_t[us][g][:], in_=Z[k][C_BQ + g, :, PAD + t0:PAD + t0 + T2]),
                       writes=[r_q[us][g]], dma="q%d_%d" % (us, g))
                    op("sp", lambda e, g=g, r=r, us=us, k=k, t0=t0: e.dma_start(out=kb_t[us][g][:], in_=Z[k][C_BK + g, :, PAD + t0 - 64 * r:PAD + t0 + T2 + 64 * r]),
                       writes=[r_k[us][g]], dma="k%d_%d" % (us, g))
                    op("sp", lambda e, g=g, r=r, us=us, k=k, t0=t0: e.dma_start(out=vb_t[us][g][:], in_=Z[k][C_BV + g, :, PAD + t0 - 64 * r:PAD + t0 + T2 + 64 * r]),
                       writes=[r_v[us][g]], dma="v%d_%d" % (us, g))
                op("sp", lambda e, us=us, k=k, t0=t0: e.dma_start(out=gb_t[us][:], in_=Z[k][C_BG, :, PAD + t0:PAD + t0 + T2]), writes=[r_gb[us]], dma="gb%d" % us)
                tasks = []
                for g in range(3):
                    r = DILS[g]
                    Lu = T2 // r
                    nq = min(4, Lu // 128)
                    nsub = Lu // (nq * 128)
                    for rho in range(r):
                        for su in range(nsub):
                            l0 = su * nq * 128
                            first = (u == 0 and l0 == 0)
                            last = (u == S // T2 - 1 and l0 + nq * 128 == Lu)
                            sid = sucnt[0]
                            sucnt[0] += 1
                            for m in range(nq + 1):
                                tasks.append(dict(g=g, r=r, rho=rho, l0=l0, m=m, nq=nq, first=first, last=last, sid=sid))

                def front(t):
                    i = tcnt[0]
                    tcnt[0] += 1
                    t["i"] = i
                    g, r, rho, l0, m, nq = t["g"], t["r"], t["rho"], t["l0"], t["m"], t["nq"]
                    s2, s3, vh = i % 2, i % 3, i % 2
                    kcol = rho + r * (l0 + 128 * m)
                    ktile = strided(kb_t[us][g], kcol, 128, r)
                    vtile = strided(vb_t[us][g], kcol, 128, r)
                    op("pe", lambda e: e.transpose(pV_[vh][:, 0:128], vtile, ident), reads=[r_v[us][g], r_cm], writes=[r_pV[vh]])
                    if i % 2 == 0:
                        op("dve", lambda e: e.tensor_copy(out=Vm_t[s3][:], in_=pV_[vh][:, 0:128]), reads=[r_pV[vh]], writes=[r_Vm[s3]])
                    else:
                        op("act", lambda e: e.activation(out=Vm_t[s3][:], in_=pV_[vh][:, 0:128], func=AF.Copy), reads=[r_pV[vh]], writes=[r_Vm[s3]])
                    jlo = m - 1 if m >= 1 else None
                    jhi = m if m < nq else None
                    if jlo is not None and jhi is not None:
                        qap = strided(qb_t[us][g], rho + r * (l0 + 128 * jlo), 256, r)
                        mk_ = masks[:, 0:2, :].rearrange("p a b -> p (a b)")
                        N = 256
                    elif jhi is not None:
                        qap = strided(qb_t[us][g], rho + r * (l0 + 128 * jhi), 128, r)
                        mk_ = masks[:, 3, :] if t["first"] else masks[:, 1, :]
                        N = 128
                    else:
                        qap = strided(qb_t[us][g], rho + r * (l0 + 128 * jlo), 128, r)
                        mk_ = masks[:, 2, :] if t["last"] else masks[:, 0, :]
                        N = 128
                    t["N"], t["jlo"], t["jhi"] = N, jlo, jhi
                    op("pe", lambda e: e.matmul(pS[s2][:, 0:N], lhsT=ktile, rhs=qap, start=True, stop=False),
                       reads=[r_k[us][g], r_q[us][g]], writes=[r_pS[s2]])
                    op("pe", lambda e: e.matmul(pS[s2][:, 0:N], lhsT=ident, rhs=mk_, start=False, stop=True),
                       reads=[r_cm], writes=[r_pS[s2]])
                    op("act", lambda e: e.activation(out=PT_t[s3][:, 0:N], in_=pS[s2][:, 0:N], func=AF.Exp, scale=scale_b),
                       reads=[r_pS[s2]], writes=[r_PT[s3]])

                def back(t):
                    i = t["i"]
                    g, r, rho, l0, m, nq = t["g"], t["r"], t["rho"], t["l0"], t["m"], t["nq"]
                    s3 = i % 3
                    so = t["sid"] % 2
                    halves = []
                    if t["jlo"] is not None:
                        halves.append((t["jlo"], 0, False, True))
                    if t["jhi"] is not None:
                        halves.append((t["jhi"], t["N"] - 128, True, False))
                    for (j, c0, st_, sp_) in halves:
                        op("pe", lambda e, j=j, c0=c0, st_=st_, sp_=sp_: e.matmul(pO[so][:, j * 128:(j + 1) * 128], lhsT=Vm_t[s3][:],
                                                                                    rhs=PT_t[s3][:, c0:c0 + 128], start=st_, stop=sp_),
                           reads=[r_Vm[s3], r_PT[s3]], writes=[r_pO[so]])
                        op("pe", lambda e, j=j, c0=c0, st_=st_, sp_=sp_: e.matmul(pD[so][:, j * 128:(j + 1) * 128], lhsT=ones_b,
                                                                                    rhs=PT_t[s3][:, c0:c0 + 128], start=st_, stop=sp_),
                           reads=[r_cm, r_PT[s3]], writes=[r_pD[so]])
                    if m == nq:
                        n = nq * 128
                        dO = strided(accO[us], rho + r * l0, n, r)
                        dD = strided(accD[us], rho + r * l0, n, r)
                        if g == 0:
                            op("dve", lambda e: e.tensor_copy(out=dO, in_=pO[so][:, 0:n]), reads=[r_pO[so]], writes=[r_aO[us]])
                            op("act", lambda e: e.activation(out=dD, in_=pD[so][:, 0:n], func=AF.Copy), reads=[r_pD[so]], writes=[r_aD[us]])
                        else:
                            op("dve", lambda e: e.tensor_tensor(out=dO, in0=dO, in1=pO[so][:, 0:n], op=ALU.add),
                               reads=[r_pO[so], r_aO[us]], writes=[r_aO[us]])
                            op("dve", lambda e: e.tensor_tensor(out=dD, in0=dD, in1=pD[so][:, 0:n], op=ALU.add),
                               reads=[r_pD[so], r_aD[us]], writes=[r_aD[us]])

                front(tasks[0])
                for i_ in range(len(tasks)):
                    if i_ + 1 < len(tasks):
                        front(tasks[i_ + 1])
                    back(tasks[i_])
                op("dve", lambda e, us=us: e.reciprocal(out=accD[us][:], in_=accD[us][:]), reads=[r_aD[us]], writes=[r_aD[us]])
                op("pool", lambda e, us=us: e.tensor_tensor(out=accO[us][:], in0=accO[us][:], in1=accD[us][:], op=ALU.mult),
                   reads=[r_aD[us], r_aO[us]], writes=[r_aO[us]])
                op("dve", lambda e, us=us: e.tensor_tensor(out=yst[us][:], in0=accO[us][:], in1=gb_t[us][:], op=ALU.mult),
                   reads=[r_aO[us], r_gb[us]], writes=[r_yst[us]])
                c0_ = tokoff[k] + t0
                op("pool", lambda e, c0_=c0_, us=us: e.dma_start(out=ysend.ap()[(c0_ // CW) * 256:(c0_ // CW + T2 // CW) * 256, :].rearrange("(j r) t -> r j t", r=256)[128:256, :, :],
                                                                 in_=yst[us][:].rearrange("p (j t) -> p j t", t=CW)),
                   reads=[r_yst[us]], dma="yst%d" % us)
            sc.flush()

        if upto < 3:
            return nc
        sb = lambda n, s, d: es.enter_context(nc.sbuf_tensor(n, s, d))
        Wo = sb("Wo", [128, 8, D], BF16)
        Wg = sb("Wg", [128, 16, D], BF16)
        pn_t = sb("pn_t", [128, 16], F32)
        wst = [sb("wst%d" % i, [128, D], F32) for i in range(1)]
        r_Wo, r_Wg, r_Wp, r_fn, r_pn = sc.res(), sc.res(), sc.res(), sc.res(), sc.res()
        r_wst = sc.res(1)
        with ExitStack() as ps:
            sb = lambda n, s, d: ps.enter_context(nc.sbuf_tensor(n, s, d))
            pp = lambda n, s, d: ps.enter_context(nc.psum_tensor(n, s, d))
            Kt = sb("Kt", [128, SMAX], BF16)
            Vt = sb("Vt", [128, SMAX // 128, 130], BF16)
            vld = [sb("vld0", [128, 2048], BF16)] * 2
            Qb = [sb("Qb%d" % i, [128, 512], BF16) for i in range(2)]
            Gb = [sb("Gb%d" % i, [128, 512], BF16) for i in range(2)]
            PT3 = [sb("PT3_%d" % i, [128, 1024], BF16) for i in range(3)]
            Osb = [sb("Osb0", [128, 3, 512], F32)] * 2
            fin = [sb("fin%d" % i, [128, 16], F32) for i in range(2)]
            o_t = [sb("o_t%d" % i, [128, 4, 128], F32) for i in range(2)]
            sq_t = sb("sq_t", [128, 4, 128], F32)
            on_t = [sb("on_t%d" % i, [128, 4, 128], BF16) for i in range(2)]
            yst3 = [sb("yst3_%d" % i, [128, 512], BF16) for i in range(2)]
            sl_t = sb("sl_t", [128, 128], F32)
            pS3 = [pp("pS3_%d" % i, [128, 1024], F32) for i in range(2)]
            pO3 = pp("pO3", [128, 3, 512], F32)
            pY = pp("pY", [128, 512], BF16)
            r_Kt, r_Vt, r_sl = sc.res(), sc.res(), sc.res()
            r_vld, r_Qb, r_Gb, r_PT3, r_Osb, r_fin, r_o, r_on, r_yst3 = ([sc.res()] * 2, sc.res(2), sc.res(2), sc.res(3), [sc.res()] * 2,
                                                                          sc.res(2), sc.res(2), sc.res(2), sc.res(2))
            r_sq = sc.res()
            r_pS3, r_pO3, r_pY = sc.res(2), sc.res(), sc.res()

            def oacc(c, j):
                idx = c * 4 + j
                return idx // 3, (idx % 3) * 129

            op("sp", lambda e: e.dma_start(out=sl_t[:], in_=subln.rearrange("a b -> (a b)").partition_broadcast(128)), writes=[r_sl], dma="sl")
            op("dve", lambda e: e.tensor_scalar(out=sl_t[:], in0=sl_t[:], scalar1=0.8, scalar2=None, op0=ALU.mult), reads=[r_sl], writes=[r_sl])
            def load_tail_weights():
                op("sp", lambda e: e.dma_start(out=pn_t[:], in_=pnorm), writes=[r_pn], dma="c5")
                wl = 0
                for kc in range(8):
                    hh, ab = kc // 2, kc % 2
                    r0 = ab * 512 + hh * 128
                    sl = 0
                    wl += 1
                    op("sp", lambda e, r0=r0, sl=sl: e.dma_start(out=wst[sl][:], in_=wout[r0:r0 + 128, :]), writes=[r_wst[sl]], dma="wst%d" % sl)
                    op("dve", lambda e, kc=kc, sl=sl: e.tensor_copy(out=Wo[:, kc, :], in_=wst[sl][:]), reads=[r_wst[sl]], writes=[r_Wo])
                for kc in range(16):
                    sl = 0
                    wl += 1
                    op("sp", lambda e, kc=kc, sl=sl: e.dma_start(out=wst[sl][:], in_=wgate[kc * 128:(kc + 1) * 128, :]), writes=[r_wst[sl]], dma="wst%d" % sl)
                    op("dve", lambda e, kc=kc, sl=sl: e.tensor_scalar(out=Wg[:, kc, :], in0=wst[sl][:], scalar1=pn_t[:, kc:kc + 1], scalar2=None, op0=ALU.mult),
                       reads=[r_wst[sl], r_pn], writes=[r_Wg])
            vcnt = 0
            qcnt = 0
            pend_fin = []
            r_ys = sc.res(NCK)
            r_cc = sc.res()
            for k, S in seqs:
                nkt = S // 128
                op("sp", lambda e, k=k, S=S: e.dma_start(out=Kt[:, 0:S], in_=Z[k][C_AK, :, PAD:PAD + S]), writes=[r_Kt], dma="kt")
                op("dve", lambda e: e.memset(Vt[:, :, 128:130], 1.0), writes=[r_Vt])
                for vb in range(S // 2048):
                    vs = vcnt % 2
                    vcnt += 1
                    op("sp", lambda e, k=k, vb=vb, vs=vs: e.dma_start(out=vld[vs][:], in_=Z[k][C_AV, :, PAD + vb * 2048:PAD + (vb + 1) * 2048]),
                       writes=[r_vld[vs]], dma="vld0")
                    for q4 in range(4):
                        for jj in range(4):
                            tt = q4 * 4 + jj
                            op("pe", lambda e, vs=vs, tt=tt, jj=jj: e.transpose(pY[:, jj * 128:(jj + 1) * 128], vld[vs][:, tt * 128:(tt + 1) * 128], ident),
                               reads=[r_vld[vs], r_cm], writes=[r_pY])
                        kt0 = vb * 16 + q4 * 4
                        op("dve", lambda e, kt0=kt0: e.tensor_copy(out=Vt[:, kt0:kt0 + 4, 0:128], in_=pY[:].rearrange("p (a b) -> p a b", b=128)),
                           reads=[r_pY], writes=[r_Vt])
                for qb in range(S // 512):
                    qs = qcnt % 2
                    qcnt += 1
                    q0 = qb * 512
                    op("sp", lambda e, k=k, q0=q0, qs=qs: e.dma_start(out=Qb[qs][:], in_=Z[k][C_AQ, :, PAD + q0:PAD + q0 + 512]),
                       writes=[r_Qb[qs]], dma="qb%d" % qs)
                    op("sp", lambda e, k=k, q0=q0, qs=qs: e.dma_start(out=Gb[qs][:], in_=Z[k][C_AG, :, PAD + q0:PAD + q0 + 512]),
                       writes=[r_Gb[qs]], dma="qb%d" % qs)

                    def front3(kt, qs=qs):
                        s2, s3 = kt % 2, kt % 3
                        for c in range(2):
                            op("pe", lambda e, c=c: e.matmul(pS3[s2][:, c * 512:(c + 1) * 512], lhsT=Kt[c * 64:(c + 1) * 64, kt * 128:(kt + 1) * 128],
                                                             rhs=Qb[qs][c * 64:(c + 1) * 64, :], start=True, stop=True),
                               reads=[r_Kt, r_Qb[qs]], writes=[r_pS3[s2]])
                        op("act", lambda e: e.activation(out=PT3[s3][:], in_=pS3[s2][:], func=AF.Exp, scale=0.125),
                           reads=[r_pS3[s2]], writes=[r_PT3[s3]])

                    def back3(kt, nkt=nkt):
                        s3 = kt % 3
                        for c in range(2):
                            for j in range(4):
                                bk, off = oacc(c, j)
                                op("pe", lambda e, c=c, j=j, bk=bk, off=off: e.matmul(pO3[:, bk, off:off + 129],
                                                                                       lhsT=PT3[s3][:, c * 512 + j * 128:c * 512 + (j + 1) * 128],
                                                                                       rhs=Vt[:, kt, 0:129], start=(kt == 0 and off == 0), stop=(kt == nkt - 1),
                                                                                       skip_group_check=True),
                                   reads=[r_PT3[s3], r_Vt], writes=[r_pO3])

                    front3(0)
                    front3(1)
                    for kt in range(nkt):
                        if kt + 2 < nkt:
                            front3(kt + 2)
                        back3(kt)
                        if kt == 1 and pend_fin and len(pend_fin[0]) == 2:
                            pend_fin[0].pop(0)()
                        if kt == 14 and pend_fin:
                            for f_ in pend_fin.pop():
                                f_()
                    if pend_fin:
                        for f_ in pend_fin.pop():
                            f_()
                    if qcnt == 1:
                        load_tail_weights()
                    fs = qs
                    O = Osb[fs]
                    op("dve", lambda e, O=O: e.tensor_copy(out=O[:, 0:2, 0:387], in_=pO3[:, 0:2, 0:387]), reads=[r_pO3], writes=[r_Osb[fs]])
                    op("dve", lambda e, O=O: e.tensor_copy(out=O[:, 2, 0:258], in_=pO3[:, 2, 0:258]), reads=[r_pO3], writes=[r_Osb[fs]])
                    c0_ = tokoff[k] + q0

                    def finB(fs=fs, O=O):
                        f = fin[fs]
                        for c in range(2):
                            for j in range(4):
                                bk, off = oacc(c, j)
                                op("dve", lambda e, c=c, j=j, bk=bk, off=off: e.reciprocal(out=f[:, c * 4 + j:c * 4 + j + 1], in_=O[:, bk, off + 128:off + 129]),
                                   reads=[r_Osb[fs]], writes=[r_fin[fs]])
                        op("dve", lambda e: e.tensor_scalar(out=f[:, 4:8], in0=f[:, 4:8], scalar1=lam_t[:, 1:2], scalar2=None, op0=ALU.mult),
                           reads=[r_fin[fs], r_lam], writes=[r_fin[fs]])
                        ot = o_t[fs]
                        for j in range(4):
                            b0, o0 = oacc(0, j)
                            b1, o1 = oacc(1, j)
                            op("dve", lambda e, j=j, b0=b0, o0=o0: e.tensor_scalar(out=ot[:, j, :], in0=O[:, b0, o0:o0 + 128], scalar1=f[:, j:j + 1],
                                                                                 scalar2=None, op0=ALU.mult),
                               reads=[r_Osb[fs], r_fin[fs]], writes=[r_o[fs]])
                            op("dve", lambda e, j=j, b1=b1, o1=o1: e.scalar_tensor_tensor(out=ot[:, j, :], in0=O[:, b1, o1:o1 + 128], scalar=f[:, 4 + j:5 + j],
                                                                                        in1=ot[:, j, :], op0=ALU.mult, op1=ALU.add),
                               reads=[r_Osb[fs], r_fin[fs], r_o[fs]], writes=[r_o[fs]])
                        op("pool", lambda e: e.tensor_tensor(out=sq_t[:], in0=ot[:], in1=ot[:], op=ALU.mult), reads=[r_o[fs]], writes=[r_sq])
                        op("dve", lambda e: e.reduce_sum(out=f[:, 8:12], in_=sq_t[:], axis=AX.X), reads=[r_sq], writes=[r_fin[fs]])
                        op("dve", lambda e: e.tensor_scalar(out=f[:, 8:12], in0=f[:, 8:12], scalar1=1.0 / 128, scalar2=EPS, op0=ALU.mult, op1=ALU.add),
                           reads=[r_fin[fs]], writes=[r_fin[fs]])

                    def finC(fs=fs, qs=qs, c0_=c0_):
                        f = fin[fs]
                        ot = o_t[fs]
                        ont = on_t[fs]
                        op("act", lambda e: e.activation(out=f[:, 8:12], in_=f[:, 8:12], func=AF.Ln), reads=[r_fin[fs]], writes=[r_fin[fs]])
                        op("act", lambda e: e.activation(out=f[:, 12:16], in_=f[:, 8:12], func=AF.Exp, scale=-0.5), reads=[r_fin[fs]], writes=[r_fin[fs]])
                        for j in range(4):
                            op("dve", lambda e, j=j: e.scalar_tensor_tensor(out=ont[:, j, :], in0=ot[:, j, :], scalar=f[:, 12 + j:13 + j], in1=sl_t[:],
                                                                          op0=ALU.mult, op1=ALU.mult),
                               reads=[r_o[fs], r_fin[fs], r_sl], writes=[r_on[fs]])
                        for j in range(4):
                            op("pe", lambda e, j=j: e.transpose(pY[:, j * 128:(j + 1) * 128], ont[:, j, :], ident),
                               reads=[r_on[fs], r_cm], writes=[r_pY])
                        op("dve", lambda e: e.tensor_tensor(out=yst3[fs][:], in0=pY[:], in1=Gb[qs][:], op=ALU.mult),
                           reads=[r_pY, r_Gb[qs]], writes=[r_yst3[fs]])
                        jc = c0_ // CW
                        op("pool", lambda e: e.dma_start(out=ysend.ap()[(c0_ // CW) * 256:(c0_ // CW) * 256 + 128, c0_ % CW:c0_ % CW + 512], in_=yst3[fs][:]),
                           reads=[r_yst3[fs]], writes=[r_ys[jc]], dma="y3_%d" % fs)
                        if (c0_ + 512) % CW == 0:
                            op("pool", lambda e, j=jc: e.collective_compute("AllGather", ALU.bypass, replica_groups=[[0, 1, 2, 3], [4, 5, 6, 7]],
                                                                            ins=[ysend.ap()[j * 256:(j + 1) * 256, :].opt()],
                                                                            outs=[yall.ap()[j * 1024:(j + 1) * 1024, :].opt()]),
                               reads=[r_ys[jc]], writes=[r_cc], dma="cc", inc=1)

                    pend_fin.append([finB, finC])
            if pend_fin:
                for f_ in pend_fin.pop():
                    f_()
            sc.flush()

        if upto < 4:
            return nc
        if dbg:
            op("sp", lambda e: e.dma_start(out=dbg_y, in_=ysend.ap()), dma="dbg")
            sc.flush()

        if upto < 5:
            return nc
        with ExitStack() as ps:
            sb = lambda n, s, d: ps.enter_context(nc.sbuf_tensor(n, s, d))
            pp = lambda n, s, d: ps.enter_context(nc.psum_tensor(n, s, d))
            Wp = sb("Wp", [128, 2, D], BF16)
            fn_t = sb("fn_t", [128, D], F32)
            x5 = [wst[0]] + [sb("x5_%d" % i, [128, D], F32) for i in range(1, 3)]
            h1 = [sb("h1_%d" % i, [128, D], F32) for i in range(2)]
            hb = [sb("hb%d" % i, [128, D], BF16) for i in range(2)]
            hT = [sb("hT%d" % i, [128, 16, 128], BF16) for i in range(2)]
            yT = [sb("yT%d" % i, [128, 8, 512], BF16) for i in range(2)]
            pf = [sb("pf%d" % i, [128, 2, 512], F32) for i in range(2)]
            pb = [sb("pb%d" % i, [128, 2, 512], BF16) for i in range(2)]
            e5 = [sb("e5_%d" % i, [128, 512], F32) for i in range(2)]
            st5 = [sb("st5_%d" % i, [128, 8], F32) for i in range(2)]
            st5b = [sb("st5b_%d" % i, [128, 8], F32) for i in range(2)]
            r_st5b = sc.res(2)
            junk5 = sb("junk5", [128, D], BF16)
            pH = [pp("pH%d" % i, [128, 512], F32) for i in range(2)]
            pT5 = pp("pT5", [128, D], BF16)
            pG = [pp("pG%d" % i, [128, 512], F32) for i in range(2)]
            pP = [pp("pP%d" % i, [128, 512], F32) for i in range(2)]
            r_x5, r_h1, r_hb, r_yT, r_pf, r_pb, r_e5, r_st5 = (sc.res(3), sc.res(2), sc.res(2), sc.res(2), sc.res(2), sc.res(2), sc.res(2), sc.res(2))
            r_hT = [sc.res(2) for _ in range(2)]
            r_pH, r_pT5, r_pG, r_pP = sc.res(2), sc.res(), sc.res(2), sc.res(2)

            op("sp", lambda e: e.dma_start(out=fn_t[:], in_=fnorm.rearrange("a b -> (a b)").partition_broadcast(128)), writes=[r_fn], dma="c5")
            wl = 0
            for kc in range(2):
                sl = wl % 2
                wl += 1
                op("sp", lambda e, kc=kc, sl=sl: e.dma_start(out=x5[1 + sl][:], in_=wproj[kc * 128:(kc + 1) * 128, :]), writes=[r_x5[1 + sl]], dma="x5_%d" % (1 + sl))
                op("dve", lambda e, kc=kc, sl=sl: e.tensor_copy(out=Wp[:, kc, :], in_=x5[1 + sl][:]), reads=[r_x5[1 + sl]], writes=[r_Wp])

            rk = {}

            def rank_of(e):
                if "r" not in rk:
                    rk["r"] = e.partition_id() % 4
                return rk["r"]

            def ycols(e, nq4, cst):
                assert nq4 % CW == 0
                ck = rank_of(e) * ((nq4 // CW) * 1024) + (cst // CW) * 1024
                off = cst % CW
                return yall.ap()[bass.ds(ck, 1024), off:off + 512].rearrange("(c p) t -> p c t", p=128)

            tails = [("p", SP, out_p, 0), ("s", SS, out_s, SP // 4)]
            cnt = dict(g=0, h=0)
            tiles = []
            blks = []
            for k, S, outp, poff in tails:
                nq4 = S // 4
                ybase = 0 if k == "p" else SP
                for b4 in range(nq4 // 512):
                    lc0 = b4 * 512
                    bidx = len(blks)
                    blks.append(dict(ys=bidx % 2, lc0=lc0, nq4=nq4, ybase=ybase, poff=poff))
                    for ti in range(4):
                        t = len(tiles)
                        tiles.append(dict(k=k, outp=outp, l0=lc0 + ti * 128, ys=bidx % 2, ti=ti, s2=t % 2, xsl=t % 3, bidx=bidx))

            def blockload(bidx):
                if bidx >= len(blks):
                    return
                B_ = blks[bidx]
                ys, lc0, nq4, ybase, poff = B_["ys"], B_["lc0"], B_["nq4"], B_["ybase"], B_["poff"]
                need_cc = (SP // CW) if ybase == 0 else NCK

                def ld_y(e):
                    return e.dma_start(out=yT[ys][:], in_=ycols(e, nq4, ybase + lc0))

                op("sp", ld_y, writes=[r_yT[ys]], dma="yT%d" % ys)
                op("sp", lambda e: e.dma_start(out=pf[ys][:], in_=pT[:, poff + lc0:poff + lc0 + 512].rearrange("(c p) t -> p c t", p=128)),
                   writes=[r_pf[ys]], dma="yT%d" % ys)
                op("pool", lambda e: e.tensor_copy(out=pb[ys][:], in_=pf[ys][:]), reads=[r_pf[ys]], writes=[r_pb[ys]])

            def S1(t):
                T_ = tiles[t]
                k, l0, ys, ti, s2, xsl = T_["k"], T_["l0"], T_["ys"], T_["ti"], T_["s2"], T_["xsl"]
                if ti == 1:
                    blockload(T_["bidx"] + 1)
                op("sp", lambda e: e.dma_start(out=x5[xsl][:], in_=xq_in[k][l0:l0 + 128, :]), writes=[r_x5[xsl]], dma="x5_%d" % xsl)
                for cc in range(4):
                    hs = cnt["h"] % 2
                    cnt["h"] += 1
                    for kc in range(8):
                        op("pe", lambda e, kc=kc, cc=cc, hs=hs: e.matmul(pH[hs][:], lhsT=yT[ys][:, kc, ti * 128:(ti + 1) * 128],
                                                                        rhs=Wo[:, kc, cc * 512:(cc + 1) * 512], start=(kc == 0), stop=(kc == 7)),
                           reads=[r_yT[ys], r_Wo], writes=[r_pH[hs]])
                    op("dve", lambda e, cc=cc, hs=hs: e.tensor_tensor(out=h1[s2][:, cc * 512:(cc + 1) * 512], in0=pH[hs][:],
                                                                      in1=x5[xsl][:, cc * 512:(cc + 1) * 512], op=ALU.add),
                       reads=[r_pH[hs], r_x5[xsl]], writes=[r_h1[s2]])

            def S2(t):
                s2 = tiles[t]["s2"]
                st = st5[s2]
                op("act", lambda e: e.activation(out=junk5[:], in_=h1[s2][:], func=AF.Square, accum_out=st[:, 0:1]),
                   reads=[r_h1[s2]], writes=[r_st5[s2]])
                op("dve", lambda e: e.tensor_scalar(out=st[:, 1:2], in0=st[:, 0:1], scalar1=1.0 / D, scalar2=EPS, op0=ALU.mult, op1=ALU.add),
                   reads=[r_st5[s2]], writes=[r_st5[s2]])
                op("act", lambda e: e.activation(out=st[:, 2:3], in_=st[:, 1:2], func=AF.Ln), reads=[r_st5[s2]], writes=[r_st5[s2]])
                op("act", lambda e: e.activation(out=st[:, 3:4], in_=st[:, 2:3], func=AF.Exp, scale=-0.5), reads=[r_st5[s2]], writes=[r_st5[s2]])
                op("act", lambda e: e.activation(out=hb[s2][:], in_=h1[s2][:], func=AF.Copy, scale=st[:, 3:4]),
                   reads=[r_h1[s2], r_st5[s2]], writes=[r_hb[s2]])

            def S3(t):
                s2 = tiles[t]["s2"]
                for kc in range(16):
                    op("pe", lambda e, kc=kc: e.transpose(pT5[:, kc * 128:(kc + 1) * 128], hb[s2][:, kc * 128:(kc + 1) * 128], ident),
                       reads=[r_hb[s2], r_cm], writes=[r_pT5])
                src5 = pT5[:].rearrange("p (a b) -> p a b", b=128)
                op("dve", lambda e: e.tensor_copy(out=hT[s2][:, 0:8, :], in_=src5[:, 0:8, :]), reads=[r_pT5], writes=[r_hT[s2][0]])
                op("act", lambda e: e.activation(out=hT[s2][:, 8:16, :], in_=src5[:, 8:16, :], func=AF.Copy), reads=[r_pT5], writes=[r_hT[s2][1]])

            def S4(t):
                T_ = tiles[t]
                ys, ti, s2, xsl = T_["ys"], T_["ti"], T_["s2"], T_["xsl"]
                for cc in range(4):
                    gs = cnt["g"] % 2
                    cnt["g"] += 1
                    cs = slice(cc * 512, (cc + 1) * 512)
                    for kc in range(16):
                        op("pe", lambda e, kc=kc, cs=cs, gs=gs: e.matmul(pG[gs][:], lhsT=hT[s2][:, kc, :], rhs=Wg[:, kc, cs],
                                                                        start=(kc == 0), stop=(kc == 15)),
                           reads=r_hT[s2] + [r_Wg], writes=[r_pG[gs]])
                    for kc in range(2):
                        op("pe", lambda e, kc=kc, cs=cs, gs=gs: e.matmul(pP[gs][:], lhsT=pb[ys][:, kc, ti * 128:(ti + 1) * 128], rhs=Wp[:, kc, cs],
                                                                        start=(kc == 0), stop=(kc == 1)),
                           reads=[r_pb[ys], r_Wp], writes=[r_pP[gs]])
                    op("act", lambda e, gs=gs: e.activation(out=e5[gs][:], in_=pG[gs][:], func=AF.Exp, scale=-1.0), reads=[r_pG[gs]], writes=[r_e5[gs]])
                    op("dve", lambda e, gs=gs: e.tensor_scalar(out=e5[gs][:], in0=e5[gs][:], scalar1=1.0, scalar2=None, op0=ALU.add),
                       reads=[r_e5[gs]], writes=[r_e5[gs]])
                    op("dve", lambda e, gs=gs: e.reciprocal(out=e5[gs][:], in_=e5[gs][:]), reads=[r_e5[gs]], writes=[r_e5[gs]])
                    op("dve", lambda e, gs=gs: e.tensor_tensor(out=e5[gs][:], in0=e5[gs][:], in1=pP[gs][:], op=ALU.mult),
                       reads=[r_e5[gs], r_pP[gs]], writes=[r_e5[gs]])
                    op("pool", lambda e, gs=gs, cs=cs: e.tensor_tensor(out=x5[xsl][:, cs], in0=h1[s2][:, cs], in1=e5[gs][:], op=ALU.add),
                       reads=[r_e5[gs], r_h1[s2], r_x5[xsl]], writes=[r_x5[xsl]])

            def S5(t):
                T_ = tiles[t]
                s2, xsl, outp, l0 = T_["s2"], T_["xsl"], T_["outp"], T_["l0"]
                st = st5b[s2]
                op("act", lambda e: e.activation(out=junk5[:], in_=x5[xsl][:], func=AF.Square, accum_out=st[:, 4:5]),
                   reads=[r_x5[xsl]], writes=[r_st5b[s2]])
                op("dve", lambda e: e.tensor_scalar(out=st[:, 5:6], in0=st[:, 4:5], scalar1=1.0 / D, scalar2=EPS, op0=ALU.mult, op1=ALU.add),
                   reads=[r_st5b[s2]], writes=[r_st5b[s2]])
                op("act", lambda e: e.activation(out=st[:, 6:7], in_=st[:, 5:6], func=AF.Ln), reads=[r_st5b[s2]], writes=[r_st5b[s2]])
                op("act", lambda e: e.activation(out=st[:, 7:8], in_=st[:, 6:7], func=AF.Exp, scale=-0.5), reads=[r_st5b[s2]], writes=[r_st5b[s2]])
                op("dve", lambda e: e.scalar_tensor_tensor(out=x5[xsl][:], in0=x5[xsl][:], scalar=st[:, 7:8], in1=fn_t[:],
                                                           op0=ALU.mult, op1=ALU.mult),
                   reads=[r_st5b[s2], r_fn, r_x5[xsl]], writes=[r_x5[xsl]])
                op("pool", lambda e: e.dma_start(out=outp[l0:l0 + 128, :], in_=x5[xsl][:]),
                   reads=[r_x5[xsl]], writes=[r_x5[xsl]], dma="o5_%d" % xsl)

            NTL = len(tiles)
            blockload(0)
            S1(0)
            S2(0)
            for t in range(NTL):
                if t + 1 < NTL:
                    S1(t + 1)
                S3(t)
                if t >= 1:
                    S5(t - 1)
                if t + 1 < NTL:
                    S2(t + 1)
                S4(t)
            S5(NTL - 1)
            sc.flush(final=True)
    return nc


_CACHE = {}


def _prep_inputs(inputs, SP, SS):
    f = lambda a: np.ascontiguousarray(np.asarray(a, dtype=np.float32))
    xP, xS = f(inputs["x_prompt"]), f(inputs["x_sample"])
    pP, pS = f(inputs["p_prompt"])[0], f(inputs["p_sample"])[0]
    w_in = f(inputs["w_in"])[0]
    cm, tabs = _consts(max(SP, SS))
    to_pk = lambda v: np.ascontiguousarray(v.reshape(16, 128).T)
    lamv = np.stack([f(inputs[n])[0] for n in ("lam_q1", "lam_k1", "lam_q2", "lam_k2")], 0)
    common = dict(nmix=to_pk(f(inputs["norm_mix"])[0]), lamv=lamv, subln=f(inputs["subln"]).reshape(1, 128),
                  wout=f(inputs["w_out"])[0], pnorm=to_pk(f(inputs["ple_norm"])[0]), wgate=f(inputs["w_ple_gate"])[0],
                  wproj=f(inputs["w_ple_proj"])[0], fnorm=f(inputs["final_norm"]).reshape(1, D), cmat=cm, tabs=tabs)
    maps = []
    for c in range(8):
        b, h = c // 4, c % 4
        cols = []
        for base in (0, 512, 1024, 1536):
            cols.append(np.arange(base + h * 128, base + (h + 1) * 128))
        for base in (2048, 3584, 5120):
            for g in range(3):
                cols.append(np.arange(base + g * 512 + h * 128, base + g * 512 + (h + 1) * 128))
        cols.append(np.arange(6656 + h * 128, 6656 + (h + 1) * 128))
        cols = np.concatenate(cols)
        qp, qs = SP // 4, SS // 4
        pT = np.concatenate([pP[b, h * qp:(h + 1) * qp].T, pS[b, h * qs:(h + 1) * qs].T], 1)
        m = dict(common)
        m.update(xp=xP[b], xs=xS[b], xqp=xP[b, h * qp:(h + 1) * qp], xqs=xS[b, h * qs:(h + 1) * qs], wh=np.ascontiguousarray(w_in[:, cols]), pT=np.ascontiguousarray(pT))
        maps.append(m)
    return maps


def kernel(_dbg=False, _upto=5, **inputs):
    SP = inputs["x_prompt"].shape[1]
    SS = inputs["x_sample"].shape[1]
    key = (SP, SS, _dbg, _upto)
    if key not in _CACHE:
        _CACHE[key] = build(SP, SS, _dbg, _upto)
    nc = _CACHE[key]
    maps = _prep_inputs(inputs, SP, SS)
    res = run_bass_kernel_spmd(nc, maps, core_ids=list(range(8)))
    if _dbg:
        return res
    yp = np.zeros((2, SP, D), np.float32)
    ys = np.zeros((2, SS, D), np.float32)
    qp, qs = SP // 4, SS // 4
    for c in range(8):
        b, h = c // 4, c % 4
        yp[b, h * qp:(h + 1) * qp] = res.results[c]["out_p"]
        ys[b, h * qs:(h + 1) * qs] = res.results[c]["out_s"]
    return (yp, ys)
```

```python
import numpy as np
import ml_dtypes
from contextlib import ExitStack
import concourse.bass as bass
import concourse.mybir as mybir
from concourse.bass_utils import run_bass_kernel_spmd

F32 = mybir.dt.float32
BF16 = mybir.dt.bfloat16
AF = mybir.ActivationFunctionType
ALU = mybir.AluOpType
AX = mybir.AxisListType
bf16 = ml_dtypes.bfloat16

D = 2048
NCH = 14
PAD = 1024
EPS = 1e-6
DILS = (1, 4, 16)
C_AQ, C_AK, C_AV, C_AG = 0, 1, 2, 3
C_BQ, C_BK, C_BV, C_BG = 4, 7, 10, 13
ROPE_DIFF = (C_AQ, C_AK)
ROPE_DIL = (4, 5, 6, 7, 8, 9)
COPY_CH = (C_AV, 10, 11, 12)
GATE_CH = (C_AG, C_BG)
NEGM = -30000.0


class Res:
    __slots__ = ("lw", "rd")

    def __init__(self):
        self.lw = None
        self.rd = {}


class Op:
    __slots__ = ("eng", "fn", "deps", "dma", "need", "sem", "sigval", "inc", "dwaits", "done")


class Sched:
    ENGS = ("pe", "act", "dve", "pool", "sp")

    def __init__(self, nc, es):
        self.nc = nc
        self.es = es
        self.sem = {e: es.enter_context(nc.semaphore("s_" + e)) for e in self.ENGS}
        self.cnt = {e: 0 for e in self.ENGS}
        self.gsem = {}
        self.gcnt = {}
        self.ops = []
        self.waited = {e: {} for e in self.ENGS}
        self.nres = 0

    def res(self, n=None):
        if n is None:
            return Res()
        return [Res() for _ in range(n)]

    def _dep(self, o, p, kind):
        if p is o or p.done:
            return
        if p.dma is None and o.dma is None and p.eng == o.eng:
            if o.eng == "pe":
                return
        o.deps.add(p)

    def op(self, eng, fn, reads=(), writes=(), dma=None, inc=None):
        o = Op()
        o.eng, o.fn, o.dma, o.deps, o.need = eng, fn, dma, set(), False
        o.sem = o.sigval = None
        o.inc = inc
        o.done = False
        for r in reads:
            if r.lw is not None:
                self._dep(o, r.lw, "raw")
        for w in writes:
            if w.lw is not None:
                self._dep(o, w.lw, "waw")
            for rr in w.rd.values():
                self._dep(o, rr, "war")
        for w in writes:
            w.lw = o
            w.rd = {}
        for r in reads:
            r.rd[eng if dma is None else ("d", dma)] = o
        o.dwaits = {}
        for p in o.deps:
            if p.dma is not None:
                o.dwaits[p.dma] = self.gcnt[p.dma]
        if dma is not None:
            if dma not in self.gsem:
                self.gsem[dma] = self.es.enter_context(self.nc.semaphore("g_" + dma))
                self.gcnt[dma] = 0
            o.inc = 16 if inc is None else inc
            self.gcnt[dma] += o.inc
            o.sem, o.sigval = self.gsem[dma], self.gcnt[dma]
        self.ops.append(o)
        return o

    def flush(self, final=False, drain_cc=True):
        nc = self.nc
        ops = self.ops
        self.ops = []
        for o in ops:
            o.done = True
            for p in o.deps:
                p.need = True
        for o in ops:
            if o.dma is None and o.need:
                self.cnt[o.eng] += 1
                o.sem, o.sigval, o.inc = self.sem[o.eng], self.cnt[o.eng], 1
        per = {e: [] for e in self.ENGS}
        for o in ops:
            per[o.eng].append(o)
        gs = [(self.gsem[g], self.gcnt[g]) for g in self.gsem]

        def mk(ename):
            def body(e):
                wd = self.waited[ename]
                for o in per[ename]:
                    ws = {}
                    for p in o.deps:
                        if p.dma is None:
                            k = id(p.sem)
                            if k not in ws or ws[k][1] < p.sigval:
                                ws[k] = (p.sem, p.sigval)
                    for g, v in o.dwaits.items():
                        ws[id(self.gsem[g])] = (self.gsem[g], v)
                    for k, (s, v) in ws.items():
                        if wd.get(k, 0) < v:
                            e.wait_ge(s, v)
                            wd[k] = v
                    ins = o.fn(e)
                    if o.sigval is not None:
                        ins.then_inc(o.sem, o.inc)
                if ename in ("sp", "pool"):
                    for g_, (s, v) in zip(list(self.gsem), gs):
                        if (g_ == "cc") != (ename == "pool"):
                            continue
                        if g_ == "cc" and not drain_cc:
                            continue
                        if v > 0 and wd.get(id(s), 0) < v:
                            e.wait_ge(s, v)
                            wd[id(s)] = v
            return body

        with nc.Block() as block:
            block.tensor(mk("pe"))
            block.scalar(mk("act"))
            block.vector(mk("dve"))
            block.gpsimd(mk("pool"))
            block.sync(mk("sp"))


def _consts(smax):
    ident = np.eye(128, dtype=np.float32)
    rdil = np.zeros((128, 128), np.float32)
    for f in range(64):
        rdil[f + 64, f] = -1.0
        rdil[f, f + 64] = 1.0
    rdiff = np.zeros((128, 128), np.float32)
    for b0 in (0, 64):
        for f in range(32):
            rdiff[b0 + f + 32, b0 + f] = -1.0
            rdiff[b0 + f, b0 + f + 32] = 1.0
    kp = np.arange(128)[:, None]
    qf = np.arange(128)[None, :]
    lo = qf >= kp
    hi = qf <= kp
    masks = np.stack([lo, hi, lo & (kp < 64), hi & (kp >= 64)], 1)
    masks = np.where(masks, 0.0, NEGM).astype(np.float32)
    cm = np.concatenate([ident[:, None, :], rdil[:, None, :], rdiff[:, None, :],
                         np.ones((128, 1, 128), np.float32), masks], 1)
    pos = np.arange(smax, dtype=np.float32)
    inv_dil = (10000.0 ** (-np.arange(0, 128, 2, dtype=np.float32) / 128)).astype(np.float32)
    inv_dif = (10000.0 ** (-np.arange(0, 64, 2, dtype=np.float32) / 64)).astype(np.float32)
    a_dil = (pos[None, :] * inv_dil[np.arange(128) % 64][:, None]).astype(np.float32)
    a_dif = (pos[None, :] * inv_dif[(np.arange(128) % 64) % 32][:, None]).astype(np.float32)
    tabs = np.stack([np.cos(a_dif), np.sin(a_dif), np.cos(a_dil), np.sin(a_dil)], 1)
    return cm.astype(bf16), np.ascontiguousarray(tabs.astype(np.float32))


def build(SP, SS, dbg=False, upto=5):
    nc = bass.Bass("TRN2", target_bir_lowering=False)
    seqs = [("p", SP), ("s", SS)]
    SMAX = max(SP, SS)
    TT = SP // 4 + SS // 4
    TOT = SP + SS
    din = {}

    def inp(name, shape, dt=F32):
        din[name] = nc.dram_tensor(name, list(shape), dt, kind="ExternalInput").ap()
        return din[name]

    x_in = {"p": inp("xp", [SP, D]), "s": inp("xs", [SS, D])}
    xq_in = {"p": inp("xqp", [SP // 4, D]), "s": inp("xqs", [SS // 4, D])}
    wh = inp("wh", [D, NCH * 128])
    nmix = inp("nmix", [128, 16])
    lamv = inp("lamv", [4, 64])
    subln = inp("subln", [1, 128])
    wout = inp("wout", [1024, D])
    pnorm = inp("pnorm", [128, 16])
    wgate = inp("wgate", [D, D])
    wproj = inp("wproj", [256, D])
    fnorm = inp("fnorm", [1, D])
    pT = inp("pT", [256, TT])
    cmat = inp("cmat", [128, 8, 128], BF16)
    tabs = inp("tabs", [128, 4, SMAX])
    out_p = nc.dram_tensor("out_p", [SP // 4, D], F32, kind="ExternalOutput").ap()
    out_s = nc.dram_tensor("out_s", [SS // 4, D], F32, kind="ExternalOutput").ap()
    Z = {k: nc.dram_tensor("Z" + k, [NCH, 128, S + 2 * PAD], BF16).ap() for k, S in seqs}
    CW = 1024
    NCK = TOT // CW
    ysend = nc.dram_tensor("ysend", [NCK * 256, CW], BF16)
    yall = nc.dram_tensor("yall", [NCK * 1024, CW], BF16)
    if dbg:
        dbg_z = nc.dram_tensor("dbg_z", [NCH, 128, SS + 2 * PAD], BF16, kind="ExternalOutput").ap()
        dbg_y = nc.dram_tensor("dbg_y", [(TOT // 1024) * 256, 1024], BF16, kind="ExternalOutput").ap()

    es = ExitStack()
    with es:
        sc = Sched(nc, es)
        op = sc.op
        cm = es.enter_context(nc.sbuf_tensor("cm", [128, 8, 128], BF16))
        ident, rdil, rdiff, ones_b = cm[:, 0, :], cm[:, 1, :], cm[:, 2, :], cm[:, 3, :]
        lam_t = es.enter_context(nc.sbuf_tensor("lam_t", [128, 8], F32))
        one_c = es.enter_context(nc.sbuf_tensor("one_c", [128, 1], F32))
        r_one = sc.res()
        r_cm = sc.res()
        r_lam = sc.res()

        with ExitStack() as ps:
            sb = lambda n, s, d: ps.enter_context(nc.sbuf_tensor(n, s, d))
            pp = lambda n, s, d: ps.enter_context(nc.psum_tensor(n, s, d))
            Wb = sb("Wb", [128, 16, NCH * 128], BF16)
            xs_t = [sb("xs%d" % i, [128, D], F32) for i in range(4)]
            junk = sb("junk", [128, D], BF16)
            st_t = [sb("st%d" % i, [128, 8], F32) for i in range(2)]
            xb_t = [sb("xb%d" % i, [128, D], BF16) for i in range(2)]
            uT_t = [sb("uT%d" % i, [128, 16, 512], BF16) for i in range(2)]
            tab_t = [sb("tab%d" % i, [128, 4, 512], F32) for i in range(2)]
            stg_t = [sb("stg%d" % i, [128, NCH, 512], BF16) for i in range(2)]
            zb_t = [sb("zb%d" % i, [128, 512], BF16) for i in range(2)]
            t1_t = [sb("t1%d" % i, [128, 512], F32) for i in range(2)]
            t2_t = [sb("t2%d" % i, [128, 512], F32) for i in range(2)]
            ge_t = [sb("ge%d" % i, [128, 512], F32) for i in range(2)]
            ones_f = sb("ones_f", [128, 512], F32)
            nm_t = sb("nm_t", [128, 16], F32)
            lv_t = sb("lv_t", [128, 4, 64], F32)
            zero_t = sb("zero_t", [128, PAD], BF16)
            pT_ps = [pp("pT%d" % i, [128, D], BF16) for i in range(2)]
            pZ = [pp("pZ%d" % i, [128, 512], F32) for i in range(2)]
            pR = [pp("pR%d" % i, [128, 512], F32) for i in range(2)]
            r_W, r_nm, r_ones, r_zero, r_lv = sc.res(), sc.res(), sc.res(), sc.res(), sc.res()
            r_xs, r_st, r_xb = sc.res(4), sc.res(2), sc.res(2)
            r_uT = [[[sc.res(), sc.res()] for _ in range(4)] for _ in range(2)]
            r_tab, r_zb, r_t1, r_t2, r_ge = sc.res(2), sc.res(2), sc.res(2), sc.res(2), sc.res(2)
            r_stg = [sc.res(NCH) for _ in range(2)]
            r_pT, r_pZ, r_pR = sc.res(2), sc.res(2), sc.res(2)

            op("sp", lambda e: e.dma_start(out=cm[:], in_=cmat), writes=[r_cm], dma="cm")
            op("sp", lambda e: e.dma_start(out=nm_t[:], in_=nmix), writes=[r_nm], dma="cm")
            op("sp", lambda e: e.dma_start(out=lv_t[:].rearrange("p a b -> p (a b)"),
                                           in_=lamv.rearrange("a b -> (a b)").partition_broadcast(128)),
               writes=[r_lv], dma="cm")
            op("dve", lambda e: e.memset(ones_f[:], 1.0), writes=[r_ones])
            op("dve", lambda e: e.memset(one_c[:], 1.0), writes=[r_one])
            op("dve", lambda e: e.memset(zero_t[:], 0.0), writes=[r_zero])
            for k, S in seqs:
                for c in range(7, 13):
                    for off in (0, PAD + S):
                        op("pool", lambda e, k=k, c=c, off=off: e.dma_start(
                            out=Z[k][c, :, off:off + PAD], in_=zero_t[:]), reads=[r_zero], dma="zp")
            op("dve", lambda e: e.tensor_tensor(out=lv_t[:, 0, :], in0=lv_t[:, 0, :], in1=lv_t[:, 1, :], op=ALU.mult),
               reads=[r_lv], writes=[r_lv])
            op("dve", lambda e: e.tensor_tensor(out=lv_t[:, 2, :], in0=lv_t[:, 2, :], in1=lv_t[:, 3, :], op=ALU.mult),
               reads=[r_lv], writes=[r_lv])
            op("dve", lambda e: e.reduce_sum(out=lam_t[:, 2:3], in_=lv_t[:, 0, :], axis=AX.X), reads=[r_lv], writes=[r_lam])
            op("dve", lambda e: e.reduce_sum(out=lam_t[:, 3:4], in_=lv_t[:, 2, :], axis=AX.X), reads=[r_lv, r_lam], writes=[r_lam])
            op("act", lambda e: e.activation(out=lam_t[:, 4:6], in_=lam_t[:, 2:4], func=AF.Exp), reads=[r_lam], writes=[r_lam])
            op("dve", lambda e: e.tensor_tensor(out=lam_t[:, 6:7], in0=lam_t[:, 4:5], in1=lam_t[:, 5:6], op=ALU.subtract),
               reads=[r_lam], writes=[r_lam])
            op("dve", lambda e: e.tensor_scalar(out=lam_t[:, 0:1], in0=lam_t[:, 6:7], scalar1=0.2, scalar2=None, op0=ALU.add),
               reads=[r_lam], writes=[r_lam])
            op("dve", lambda e: e.tensor_scalar(out=lam_t[:, 1:2], in0=lam_t[:, 0:1], scalar1=-1.0, scalar2=None, op0=ALU.mult),
               reads=[r_lam], writes=[r_lam])
            for kc in range(16):
                sl = kc % 3
                for hf in range(2):
                    cs = slice(hf * 896, (hf + 1) * 896)
                    op("sp", lambda e, kc=kc, sl=sl, cs=cs: e.dma_start(out=xs_t[sl][:, 0:896], in_=wh[kc * 128:(kc + 1) * 128, cs]),
                       writes=[r_xs[sl]], dma="x%d" % sl)
                    op("dve", lambda e, kc=kc, sl=sl, cs=cs: e.tensor_scalar(out=Wb[:, kc, cs], in0=xs_t[sl][:, 0:896],
                                                                            scalar1=nm_t[:, kc:kc + 1], scalar2=None, op0=ALU.mult),
                       reads=[r_xs[sl], r_nm], writes=[r_W])

            blocks = [(k, S, b) for k, S in seqs for b in range(S // 512)]
            import os as _os
            if _os.environ.get("KDBG_NBLK"):
                blocks = blocks[:int(_os.environ["KDBG_NBLK"])]
            gtile = [0]

            NX = 4
            NT = len(blocks) * 4

            def tile_info(g):
                bi, ti = g // 4, g % 4
                k, S, b = blocks[bi]
                return bi, ti, k, b * 512 + ti * 128

            def prep_load(g):
                if g >= NT:
                    return
                bi, ti, k, t0 = tile_info(g)
                xsl = g % NX
                op("sp", lambda e: e.dma_start(out=xs_t[xsl][:], in_=x_in[k][t0:t0 + 128, :]), writes=[r_xs[xsl]], dma="x%d" % xsl)

            def prep_A(g):
                if g >= NT:
                    return
                xsl, s2 = g % NX, g % 2
                op("act", lambda e: e.activation(out=junk[:], in_=xs_t[xsl][:], func=AF.Square, accum_out=st_t[s2][:, 0:1]),
                   reads=[r_xs[xsl]], writes=[r_st[s2]])
                op("dve", lambda e: e.tensor_scalar(out=st_t[s2][:, 1:2], in0=st_t[s2][:, 0:1], scalar1=1.0 / D, scalar2=EPS,
                                                    op0=ALU.mult, op1=ALU.add), reads=[r_st[s2]], writes=[r_st[s2]])
                op("act", lambda e: e.activation(out=st_t[s2][:, 2:3], in_=st_t[s2][:, 1:2], func=AF.Ln), reads=[r_st[s2]], writes=[r_st[s2]])
                op("act", lambda e: e.activation(out=st_t[s2][:, 3:4], in_=st_t[s2][:, 2:3], func=AF.Exp, scale=-0.5),
                   reads=[r_st[s2]], writes=[r_st[s2]])
                op("dve", lambda e: e.tensor_scalar(out=xb_t[s2][:], in0=xs_t[xsl][:], scalar1=st_t[s2][:, 3:4], scalar2=None, op0=ALU.mult),
                   reads=[r_xs[xsl], r_st[s2]], writes=[r_xb[s2]])
                prep_load(g + 2)

            def prep_B(g):
                if g >= NT:
                    return
                bi, ti, k, t0 = tile_info(g)
                s2 = g % 2
                us = bi % 2
                for kc in range(16):
                    op("pe", lambda e, kc=kc: e.transpose(pT_ps[s2][:, kc * 128:(kc + 1) * 128], xb_t[s2][:, kc * 128:(kc + 1) * 128], ident),
                       reads=[r_xb[s2], r_cm], writes=[r_pT[s2]])
                src = pT_ps[s2][:].rearrange("p (a b) -> p a b", b=128)
                op("act", lambda e: e.activation(out=uT_t[us][:, 0:8, ti * 128:(ti + 1) * 128], in_=src[:, 0:8, :], func=AF.Copy),
                   reads=[r_pT[s2]], writes=[r_uT[us][ti][0]])
                op("dve", lambda e: e.tensor_copy(out=uT_t[us][:, 8:16, ti * 128:(ti + 1) * 128], in_=src[:, 8:16, :]),
                   reads=[r_pT[s2]], writes=[r_uT[us][ti][1]])

            def rot_part(bi, c, zs):
                us = bi % 2
                isdil = c in ROPE_DIL
                rm = rdil if isdil else rdiff
                ti = 2 if isdil else 0
                op("pe", lambda e: e.matmul(pR[zs][:], lhsT=rm, rhs=zb_t[zs][:], start=True, stop=True),
                   reads=[r_zb[zs], r_cm], writes=[r_pR[zs]])
                op("dve", lambda e: e.tensor_tensor(out=t2_t[zs][:], in0=pR[zs][:], in1=tab_t[us][:, ti + 1, :], op=ALU.mult),
                   reads=[r_pR[zs], r_tab[us]], writes=[r_t2[zs]])
                op("dve", lambda e: e.tensor_tensor(out=stg_t[us][:, c, :], in0=t1_t[zs][:], in1=t2_t[zs][:], op=ALU.add),
                   reads=[r_t1[zs], r_t2[zs]], writes=[r_stg[us][c]])

            if blocks:
                prep_load(0)
                prep_load(1)
                prep_A(0)
                prep_A(1)
                prep_B(0)
                prep_A(2)
                prep_B(1)
                prep_A(3)
                prep_B(2)
                prep_B(3)
            zc = 0
            for bi, (k, S, b) in enumerate(blocks):
                us = bi % 2
                t0 = b * 512
                op("sp", lambda e, us=us, t0=t0: e.dma_start(out=tab_t[us][:], in_=tabs[:, :, t0:t0 + 512]), writes=[r_tab[us]], dma="tab%d" % us)
                pend = None
                _cut = _os.environ.get("KDBG_CUT", "")
                for c in range(NCH):
                    if _cut == "prep":
                        break
                    if _cut == "rope" and c not in (0,):
                        continue
                    if _cut == "copy" and c not in (2,):
                        continue
                    if _cut == "gate" and c not in (3,):
                        continue
                    zs = zc % 2
                    zc += 1
                    uall = [r for t in r_uT[us] for r in t]
                    for kc in range(16):
                        op("pe", lambda e, c=c, kc=kc, zs=zs, us=us: e.matmul(pZ[zs][:], lhsT=Wb[:, kc, c * 128:(c + 1) * 128],
                                                                              rhs=uT_t[us][:, kc, :], start=(kc == 0), stop=(kc == 15)),
                           reads=[r_W] + uall, writes=[r_pZ[zs]])
                    if c in ROPE_DIFF or c in ROPE_DIL:
                        ti_ = 2 if c in ROPE_DIL else 0
                        op("act", lambda e, zs=zs: e.activation(out=zb_t[zs][:], in_=pZ[zs][:], func=AF.Copy),
                           reads=[r_pZ[zs]], writes=[r_zb[zs], r_pZ[zs]])
                        op("dve", lambda e, zs=zs, us=us, ti_=ti_: e.tensor_tensor(out=t1_t[zs][:], in0=pZ[zs][:], in1=tab_t[us][:, ti_, :], op=ALU.mult),
                           reads=[r_pZ[zs], r_tab[us]], writes=[r_t1[zs], r_pZ[zs]])
                    elif c in COPY_CH:
                        op("act", lambda e, zs=zs, us=us, c=c: e.activation(out=stg_t[us][:, c, :], in_=pZ[zs][:], func=AF.Copy),
                           reads=[r_pZ[zs]], writes=[r_stg[us][c]])
                    else:
                        op("act", lambda e, zs=zs: e.activation(out=ge_t[zs][:], in_=pZ[zs][:], func=AF.Exp, scale=-1.0),
                           reads=[r_pZ[zs]], writes=[r_ge[zs]])
                        op("act", lambda e, zs=zs: e.activation(out=ge_t[zs][:], in_=ge_t[zs][:], func=AF.Ln, bias=one_c[:, 0:1]),
                           reads=[r_ge[zs], r_one], writes=[r_ge[zs]])
                        op("act", lambda e, zs=zs: e.activation(out=ge_t[zs][:], in_=ge_t[zs][:], func=AF.Exp, scale=-1.0),
                           reads=[r_ge[zs]], writes=[r_ge[zs]])
                        op("dve", lambda e, zs=zs, us=us, c=c: e.tensor_tensor(out=stg_t[us][:, c, :], in0=pZ[zs][:], in1=ge_t[zs][:], op=ALU.mult),
                           reads=[r_pZ[zs], r_ge[zs]], writes=[r_stg[us][c]])
                    if pend is not None:
                        rot_part(bi, *pend)
                        pend = None
                    if c in ROPE_DIFF or c in ROPE_DIL:
                        pend = (c, zs)
                    if c in (0, 3, 6, 9):
                        prep_A((bi + 1) * 4 + c // 3)
                    if c in (2, 5, 8, 11):
                        prep_B((bi + 1) * 4 + (c - 2) // 3)
                if pend is not None:
                    rot_part(bi, *pend)
                    pend = None
                if _cut:
                    continue
                op("pool", lambda e, k=k, us=us, t0=t0: e.dma_start(
                    out=Z[k][:, :, PAD + t0:PAD + t0 + 512].rearrange("c p t -> p c t"), in_=stg_t[us][:]),
                   reads=r_stg[us], dma="stg%d" % us)
            sc.flush()
            if dbg:
                op("sp", lambda e: e.dma_start(out=dbg_z, in_=Z["s"]), dma="dbg")
                sc.flush()

        if upto < 2:
            return nc
        T2 = 2048
        scale_b = 128.0 ** -0.5
        with ExitStack() as ps:
            sb = lambda n, s, d: ps.enter_context(nc.sbuf_tensor(n, s, d))
            pp = lambda n, s, d: ps.enter_context(nc.psum_tensor(n, s, d))
            qb_t = [[sb("q%d_%d" % (i, g), [128, T2], BF16) for g in range(3)] for i in range(2)]
            kb_t = [[sb("k%d_%d" % (i, g), [128, T2 + 128 * DILS[g]], BF16) for g in range(3)] for i in range(2)]
            vb_t = [[sb("v%d_%d" % (i, g), [128, T2 + 128 * DILS[g]], BF16) for g in range(3)] for i in range(2)]
            gb_t = [sb("gb%d" % i, [128, T2], BF16) for i in range(2)]
            accO = [sb("accO%d" % i, [128, T2], F32) for i in range(2)]
            accD = [sb("accD%d" % i, [128, T2], F32) for i in range(2)]
            PT_t = [sb("PT%d" % i, [128, 256], BF16) for i in range(3)]
            Vm_t = [sb("Vm%d" % i, [128, 128], BF16) for i in range(3)]
            yst = [sb("yst%d" % i, [128, T2], BF16) for i in range(2)]
            pS = [pp("pS%d" % i, [128, 512], F32) for i in range(2)]
            pO = [pp("pO%d" % i, [128, 512], F32) for i in range(2)]
            pD = [pp("pD%d" % i, [128, 512], F32) for i in range(2)]
            pV_ = [pp("pV%d" % i, [128, 1024], BF16) for i in range(2)]
            r_q = [sc.res(3) for _ in range(2)]
            r_k = [sc.res(3) for _ in range(2)]
            r_v = [sc.res(3) for _ in range(2)]
            r_gb, r_aO, r_aD, r_PT, r_Vm, r_yst = sc.res(2), sc.res(2), sc.res(2), sc.res(3), sc.res(3), sc.res(2)
            r_pS, r_pO, r_pD, r_pV = sc.res(2), sc.res(2), sc.res(2), sc.res(2)
            masks = cm[:, 4:8, :]

            def strided(t, start, n, r):
                if r == 1:
                    return t[:, start:start + n]
                return t[:, start:start + (n - 1) * r + 1:r]

            units = [(k, S, u) for k, S in seqs for u in range(S // T2)]
            tcnt = [0]
            sucnt = [0]
            tokoff = {"p": 0, "s": SP}
            for ui, (k, S, u) in enumerate(units):
                us = ui % 2
                t0 = u * T2
                for g in range(3):
                    r = DILS[g]
                    op("sp", lambda e, g=g, us=us, k=k, t0=t0: e.dma_start(out=qb_t[us][g][:], in_=Z[k][C_BQ + g, :, PAD + t0:PAD + t0 + T2]),
                       writes=[r_q[us][g]], dma="q%d_%d" % (us, g))
                    op("sp", lambda e, g=g, r=r, us=us, k=k, t0=t0: e.dma_start(out=kb_t[us][g][:], in_=Z[k][C_BK + g, :, PAD + t0 - 64 * r:PAD + t0 + T2 + 64 * r]),
                       writes=[r_k[us][g]], dma="k%d_%d" % (us, g))
                    op("sp", lambda e, g=g, r=r, us=us, k=k, t0=t0: e.dma_start(out=vb_t[us][g][:], in_=Z[k][C_BV + g, :, PAD + t0 - 64 * r:PAD + t0 + T2 + 64 * r]),
                       writes=[r_v[us][g]], dma="v%d_%d" % (us, g))
                op("sp", lambda e, us=us, k=k, t0=t0: e.dma_start(out=gb_t[us][:], in_=Z[k][C_BG, :, PAD + t0:PAD + t0 + T2]), writes=[r_gb[us]], dma="gb%d" % us)
                tasks = []
                for g in range(3):
                    r = DILS[g]
                    Lu = T2 // r
                    nq = min(4, Lu // 128)
                    nsub = Lu // (nq * 128)
                    for rho in range(r):
                        for su in range(nsub):
                            l0 = su * nq * 128
                            first = (u == 0 and l0 == 0)
                            last = (u == S // T2 - 1 and l0 + nq * 128 == Lu)
                            sid = sucnt[0]
                            sucnt[0] += 1
                            for m in range(nq + 1):
                                tasks.append(dict(g=g, r=r, rho=rho, l0=l0, m=m, nq=nq, first=first, last=last, sid=sid))

                def front(t):
                    i = tcnt[0]
                    tcnt[0] += 1
                    t["i"] = i
                    g, r, rho, l0, m, nq = t["g"], t["r"], t["rho"], t["l0"], t["m"], t["nq"]
                    s2, s3, vh = i % 2, i % 3, i % 2
                    kcol = rho + r * (l0 + 128 * m)
                    ktile = strided(kb_t[us][g], kcol, 128, r)
                    vtile = strided(vb_t[us][g], kcol, 128, r)
                    op("pe", lambda e: e.transpose(pV_[vh][:, 0:128], vtile, ident), reads=[r_v[us][g], r_cm], writes=[r_pV[vh]])
                    if i % 2 == 0:
                        op("dve", lambda e: e.tensor_copy(out=Vm_t[s3][:], in_=pV_[vh][:, 0:128]), reads=[r_pV[vh]], writes=[r_Vm[s3]])
                    else:
                        op("act", lambda e: e.activation(out=Vm_t[s3][:], in_=pV_[vh][:, 0:128], func=AF.Copy), reads=[r_pV[vh]], writes=[r_Vm[s3]])
                    jlo = m - 1 if m >= 1 else None
                    jhi = m if m < nq else None
                    if jlo is not None and jhi is not None:
                        qap = strided(qb_t[us][g], rho + r * (l0 + 128 * jlo), 256, r)
                        mk_ = masks[:, 0:2, :].rearrange("p a b -> p (a b)")
                        N = 256
                    elif jhi is not None:
                        qap = strided(qb_t[us][g], rho + r * (l0 + 128 * jhi), 128, r)
                        mk_ = masks[:, 3, :] if t["first"] else masks[:, 1, :]
                        N = 128
                    else:
                        qap = strided(qb_t[us][g], rho + r * (l0 + 128 * jlo), 128, r)
                        mk_ = masks[:, 2, :] if t["last"] else masks[:, 0, :]
                        N = 128
                    t["N"], t["jlo"], t["jhi"] = N, jlo, jhi
                    op("pe", lambda e: e.matmul(pS[s2][:, 0:N], lhsT=ktile, rhs=qap, start=True, stop=False),
                       reads=[r_k[us][g], r_q[us][g]], writes=[r_pS[s2]])
                    op("pe", lambda e: e.matmul(pS[s2][:, 0:N], lhsT=ident, rhs=mk_, start=False, stop=True),
                       reads=[r_cm], writes=[r_pS[s2]])
                    op("act", lambda e: e.activation(out=PT_t[s3][:, 0:N], in_=pS[s2][:, 0:N], func=AF.Exp, scale=scale_b),
                       reads=[r_pS[s2]], writes=[r_PT[s3]])

                def back(t):
                    i = t["i"]
                    g, r, rho, l0, m, nq = t["g"], t["r"], t["rho"], t["l0"], t["m"], t["nq"]
                    s3 = i % 3
                    so = t["sid"] % 2
                    halves = []
                    if t["jlo"] is not None:
                        halves.append((t["jlo"], 0, False, True))
                    if t["jhi"] is not None:
                        halves.append((t["jhi"], t["N"] - 128, True, False))
                    for (j, c0, st_, sp_) in halves:
                        op("pe", lambda e, j=j, c0=c0, st_=st_, sp_=sp_: e.matmul(pO[so][:, j * 128:(j + 1) * 128], lhsT=Vm_t[s3][:],
                                                                                    rhs=PT_t[s3][:, c0:c0 + 128], start=st_, stop=sp_),
                           reads=[r_Vm[s3], r_PT[s3]], writes=[r_pO[so]])
                        op("pe", lambda e, j=j, c0=c0, st_=st_, sp_=sp_: e.matmul(pD[so][:, j * 128:(j + 1) * 128], lhsT=ones_b,
                                                                                    rhs=PT_t[s3][:, c0:c0 + 128], start=st_, stop=sp_),
                           reads=[r_cm, r_PT[s3]], writes=[r_pD[so]])
                    if m == nq:
                        n = nq * 128
                        dO = strided(accO[us], rho + r * l0, n, r)
                        dD = strided(accD[us], rho + r * l0, n, r)
                        if g == 0:
                            op("dve", lambda e: e.tensor_copy(out=dO, in_=pO[so][:, 0:n]), reads=[r_pO[so]], writes=[r_aO[us]])
                            op("act", lambda e: e.activation(out=dD, in_=pD[so][:, 0:n], func=AF.Copy), reads=[r_pD[so]], writes=[r_aD[us]])
                        else:
                            op("dve", lambda e: e.tensor_tensor(out=dO, in0=dO, in1=pO[so][:, 0:n], op=ALU.add),
                               reads=[r_pO[so], r_aO[us]], writes=[r_aO[us]])
                            op("dve", lambda e: e.tensor_tensor(out=dD, in0=dD, in1=pD[so][:, 0:n], op=ALU.add),
                               reads=[r_pD[so], r_aD[us]], writes=[r_aD[us]])

                front(tasks[0])
                for i_ in range(len(tasks)):
                    if i_ + 1 < len(tasks):
                        front(tasks[i_ + 1])
                    back(tasks[i_])
                op("act", lambda e, us=us: e.activation(out=accD[us][:], in_=accD[us][:], func=AF.Ln), reads=[r_aD[us]], writes=[r_aD[us]])
                op("act", lambda e, us=us: e.activation(out=accD[us][:], in_=accD[us][:], func=AF.Exp, scale=-1.0), reads=[r_aD[us]], writes=[r_aD[us]])
                op("pool", lambda e, us=us: e.tensor_tensor(out=accO[us][:], in0=accO[us][:], in1=accD[us][:], op=ALU.mult),
                   reads=[r_aD[us], r_aO[us]], writes=[r_aO[us]])
                op("dve", lambda e, us=us: e.tensor_tensor(out=yst[us][:], in0=accO[us][:], in1=gb_t[us][:], op=ALU.mult),
                   reads=[r_aO[us], r_gb[us]], writes=[r_yst[us]])
                c0_ = tokoff[k] + t0
                op("pool", lambda e, c0_=c0_, us=us: e.dma_start(out=ysend.ap()[(c0_ // CW) * 256:(c0_ // CW + T2 // CW) * 256, :].rearrange("(j r) t -> r j t", r=256)[128:256, :, :],
                                                                 in_=yst[us][:].rearrange("p (j t) -> p j t", t=CW)),
                   reads=[r_yst[us]], dma="yst%d" % us)
            sc.flush()

        if upto < 3:
            return nc
        sb = lambda n, s, d: es.enter_context(nc.sbuf_tensor(n, s, d))
        Wo = sb("Wo", [128, 8, D], BF16)
        Wg = sb("Wg", [128, 16, D], BF16)
        pn_t = sb("pn_t", [128, 16], F32)
        wst = [sb("wst%d" % i, [128, D], F32) for i in range(1)]
        r_Wo, r_Wg, r_Wp, r_fn, r_pn = sc.res(), sc.res(), sc.res(), sc.res(), sc.res()
        r_wst = sc.res(1)
        with ExitStack() as ps:
            sb = lambda n, s, d: ps.enter_context(nc.sbuf_tensor(n, s, d))
            pp = lambda n, s, d: ps.enter_context(nc.psum_tensor(n, s, d))
            Kt = sb("Kt", [128, SMAX], BF16)
            Vt = sb("Vt", [128, SMAX // 128, 130], BF16)
            vld = [sb("vld0", [128, 2048], BF16)] * 2
            Qb = [sb("Qb%d" % i, [128, 512], BF16) for i in range(2)]
            Gb = [sb("Gb%d" % i, [128, 512], BF16) for i in range(2)]
            PT3 = [sb("PT3_%d" % i, [128, 1024], BF16) for i in range(3)]
            Osb = [sb("Osb0", [128, 3, 512], F32)] * 2
            fin = [sb("fin%d" % i, [128, 16], F32) for i in range(2)]
            o_t = [sb("o_t%d" % i, [128, 4, 128], F32) for i in range(2)]
            sq_t = sb("sq_t", [128, 4, 128], F32)
            on_t = [sb("on_t%d" % i, [128, 4, 128], BF16) for i in range(2)]
            yst3 = [sb("yst3_%d" % i, [128, 512], BF16) for i in range(2)]
            sl_t = sb("sl_t", [128, 128], F32)
            pS3 = [pp("pS3_%d" % i, [128, 1024], F32) for i in range(2)]
            pO3 = pp("pO3", [128, 3, 512], F32)
            pY = pp("pY", [128, 512], BF16)
            r_Kt, r_Vt, r_sl = sc.res(), sc.res(), sc.res()
            r_vld, r_Qb, r_Gb, r_PT3, r_Osb, r_fin, r_o, r_on, r_yst3 = ([sc.res()] * 2, sc.res(2), sc.res(2), sc.res(3), [sc.res()] * 2,
                                                                          sc.res(2), sc.res(2), sc.res(2), sc.res(2))
            r_sq = sc.res()
            r_pS3, r_pO3, r_pY = sc.res(2), sc.res(), sc.res()

            def oacc(c, j):
                idx = c * 4 + j
                return idx // 3, (idx % 3) * 129

            op("sp", lambda e: e.dma_start(out=sl_t[:], in_=subln.rearrange("a b -> (a b)").partition_broadcast(128)), writes=[r_sl], dma="sl")
            op("dve", lambda e: e.tensor_scalar(out=sl_t[:], in0=sl_t[:], scalar1=0.8, scalar2=None, op0=ALU.mult), reads=[r_sl], writes=[r_sl])
            def load_tail_weights():
                op("sp", lambda e: e.dma_start(out=pn_t[:], in_=pnorm), writes=[r_pn], dma="c5")
                wl = 0
                for kc in range(8):
                    hh, ab = kc // 2, kc % 2
                    r0 = ab * 512 + hh * 128
                    sl = 0
                    wl += 1
                    op("sp", lambda e, r0=r0, sl=sl: e.dma_start(out=wst[sl][:], in_=wout[r0:r0 + 128, :]), writes=[r_wst[sl]], dma="wst%d" % sl)
                    op("dve", lambda e, kc=kc, sl=sl: e.tensor_copy(out=Wo[:, kc, :], in_=wst[sl][:]), reads=[r_wst[sl]], writes=[r_Wo])
                for kc in range(16):
                    sl = 0
                    wl += 1
                    op("sp", lambda e, kc=kc, sl=sl: e.dma_start(out=wst[sl][:], in_=wgate[kc * 128:(kc + 1) * 128, :]), writes=[r_wst[sl]], dma="wst%d" % sl)
                    op("dve", lambda e, kc=kc, sl=sl: e.tensor_scalar(out=Wg[:, kc, :], in0=wst[sl][:], scalar1=pn_t[:, kc:kc + 1], scalar2=None, op0=ALU.mult),
                       reads=[r_wst[sl], r_pn], writes=[r_Wg])
            vcnt = 0
            qcnt = 0
            pend_fin = []
            r_ys = sc.res(NCK)
            r_cc = sc.res()
            for k, S in seqs:
                nkt = S // 128
                op("sp", lambda e, k=k, S=S: e.dma_start(out=Kt[:, 0:S], in_=Z[k][C_AK, :, PAD:PAD + S]), writes=[r_Kt], dma="kt")
                op("dve", lambda e: e.memset(Vt[:, :, 128:130], 1.0), writes=[r_Vt])
                for vb in range(S // 2048):
                    vs = vcnt % 2
                    vcnt += 1
                    op("sp", lambda e, k=k, vb=vb, vs=vs: e.dma_start(out=vld[vs][:], in_=Z[k][C_AV, :, PAD + vb * 2048:PAD + (vb + 1) * 2048]),
                       writes=[r_vld[vs]], dma="vld0")
                    for q4 in range(4):
                        for jj in range(4):
                            tt = q4 * 4 + jj
                            op("pe", lambda e, vs=vs, tt=tt, jj=jj: e.transpose(pY[:, jj * 128:(jj + 1) * 128], vld[vs][:, tt * 128:(tt + 1) * 128], ident),
                               reads=[r_vld[vs], r_cm], writes=[r_pY])
                        kt0 = vb * 16 + q4 * 4
                        op("dve", lambda e, kt0=kt0: e.tensor_copy(out=Vt[:, kt0:kt0 + 4, 0:128], in_=pY[:].rearrange("p (a b) -> p a b", b=128)),
                           reads=[r_pY], writes=[r_Vt])
                for qb in range(S // 512):
                    qs = qcnt % 2
                    qcnt += 1
                    q0 = qb * 512
                    op("sp", lambda e, k=k, q0=q0, qs=qs: e.dma_start(out=Qb[qs][:], in_=Z[k][C_AQ, :, PAD + q0:PAD + q0 + 512]),
                       writes=[r_Qb[qs]], dma="qb%d" % qs)
                    op("sp", lambda e, k=k, q0=q0, qs=qs: e.dma_start(out=Gb[qs][:], in_=Z[k][C_AG, :, PAD + q0:PAD + q0 + 512]),
                       writes=[r_Gb[qs]], dma="qb%d" % qs)

                    def front3(kt, qs=qs):
                        s2, s3 = kt % 2, kt % 3
                        for c in range(2):
                            op("pe", lambda e, c=c: e.matmul(pS3[s2][:, c * 512:(c + 1) * 512], lhsT=Kt[c * 64:(c + 1) * 64, kt * 128:(kt + 1) * 128],
                                                             rhs=Qb[qs][c * 64:(c + 1) * 64, :], start=True, stop=True),
                               reads=[r_Kt, r_Qb[qs]], writes=[r_pS3[s2]])
                        op("act", lambda e: e.activation(out=PT3[s3][:], in_=pS3[s2][:], func=AF.Exp, scale=0.125),
                           reads=[r_pS3[s2]], writes=[r_PT3[s3]])

                    def back3(kt, nkt=nkt):
                        s3 = kt % 3
                        for c in range(2):
                            for j in range(4):
                                bk, off = oacc(c, j)
                                op("pe", lambda e, c=c, j=j, bk=bk, off=off: e.matmul(pO3[:, bk, off:off + 129],
                                                                                       lhsT=PT3[s3][:, c * 512 + j * 128:c * 512 + (j + 1) * 128],
                                                                                       rhs=Vt[:, kt, 0:129], start=(kt == 0 and off == 0), stop=(kt == nkt - 1),
                                                                                       skip_group_check=True),
                                   reads=[r_PT3[s3], r_Vt], writes=[r_pO3])

                    front3(0)
                    front3(1)
                    for kt in range(nkt):
                        if kt + 2 < nkt:
                            front3(kt + 2)
                        back3(kt)
                        if kt == 1 and pend_fin and len(pend_fin[0]) == 2:
                            pend_fin[0].pop(0)()
                        if kt == 14 and pend_fin:
                            for f_ in pend_fin.pop():
                                f_()
                    if pend_fin:
                        for f_ in pend_fin.pop():
                            f_()
                    if qcnt == 1:
                        load_tail_weights()
                    fs = qs
                    O = Osb[fs]
                    op("dve", lambda e, O=O: e.tensor_copy(out=O[:, 0:2, 0:387], in_=pO3[:, 0:2, 0:387]), reads=[r_pO3], writes=[r_Osb[fs]])
                    op("dve", lambda e, O=O: e.tensor_copy(out=O[:, 2, 0:258], in_=pO3[:, 2, 0:258]), reads=[r_pO3], writes=[r_Osb[fs]])
                    c0_ = tokoff[k] + q0

                    def finB(fs=fs, O=O):
                        f = fin[fs]
                        for c in range(2):
                            for j in range(4):
                                bk, off = oacc(c, j)
                                op("dve", lambda e, c=c, j=j, bk=bk, off=off: e.reciprocal(out=f[:, c * 4 + j:c * 4 + j + 1], in_=O[:, bk, off + 128:off + 129]),
                                   reads=[r_Osb[fs]], writes=[r_fin[fs]])
                        op("dve", lambda e: e.tensor_scalar(out=f[:, 4:8], in0=f[:, 4:8], scalar1=lam_t[:, 1:2], scalar2=None, op0=ALU.mult),
                           reads=[r_fin[fs], r_lam], writes=[r_fin[fs]])
                        ot = o_t[fs]
                        for j in range(4):
                            b0, o0 = oacc(0, j)
                            b1, o1 = oacc(1, j)
                            op("dve", lambda e, j=j, b0=b0, o0=o0: e.tensor_scalar(out=ot[:, j, :], in0=O[:, b0, o0:o0 + 128], scalar1=f[:, j:j + 1],
                                                                                 scalar2=None, op0=ALU.mult),
                               reads=[r_Osb[fs], r_fin[fs]], writes=[r_o[fs]])
                            op("dve", lambda e, j=j, b1=b1, o1=o1: e.scalar_tensor_tensor(out=ot[:, j, :], in0=O[:, b1, o1:o1 + 128], scalar=f[:, 4 + j:5 + j],
                                                                                        in1=ot[:, j, :], op0=ALU.mult, op1=ALU.add),
                               reads=[r_Osb[fs], r_fin[fs], r_o[fs]], writes=[r_o[fs]])
                        op("pool", lambda e: e.tensor_tensor(out=sq_t[:], in0=ot[:], in1=ot[:], op=ALU.mult), reads=[r_o[fs]], writes=[r_sq])
                        op("dve", lambda e: e.reduce_sum(out=f[:, 8:12], in_=sq_t[:], axis=AX.X), reads=[r_sq], writes=[r_fin[fs]])
                        op("dve", lambda e: e.tensor_scalar(out=f[:, 8:12], in0=f[:, 8:12], scalar1=1.0 / 128, scalar2=EPS, op0=ALU.mult, op1=ALU.add),
                           reads=[r_fin[fs]], writes=[r_fin[fs]])

                    def finC(fs=fs, qs=qs, c0_=c0_):
                        f = fin[fs]
                        ot = o_t[fs]
                        ont = on_t[fs]
                        op("act", lambda e: e.activation(out=f[:, 8:12], in_=f[:, 8:12], func=AF.Ln), reads=[r_fin[fs]], writes=[r_fin[fs]])
                        op("act", lambda e: e.activation(out=f[:, 12:16], in_=f[:, 8:12], func=AF.Exp, scale=-0.5), reads=[r_fin[fs]], writes=[r_fin[fs]])
                        for j in range(4):
                            op("dve", lambda e, j=j: e.scalar_tensor_tensor(out=ont[:, j, :], in0=ot[:, j, :], scalar=f[:, 12 + j:13 + j], in1=sl_t[:],
                                                                          op0=ALU.mult, op1=ALU.mult),
                               reads=[r_o[fs], r_fin[fs], r_sl], writes=[r_on[fs]])
                        for j in range(4):
                            op("pe", lambda e, j=j: e.transpose(pY[:, j * 128:(j + 1) * 128], ont[:, j, :], ident),
                               reads=[r_on[fs], r_cm], writes=[r_pY])
                        op("dve", lambda e: e.tensor_tensor(out=yst3[fs][:], in0=pY[:], in1=Gb[qs][:], op=ALU.mult),
                           reads=[r_pY, r_Gb[qs]], writes=[r_yst3[fs]])
                        jc = c0_ // CW
                        op("pool", lambda e: e.dma_start(out=ysend.ap()[(c0_ // CW) * 256:(c0_ // CW) * 256 + 128, c0_ % CW:c0_ % CW + 512], in_=yst3[fs][:]),
                           reads=[r_yst3[fs]], writes=[r_ys[jc]], dma="y3_%d" % fs)
                        if (c0_ + 512) % CW == 0:
                            op("pool", lambda e, j=jc: e.collective_compute("AllGather", ALU.bypass, replica_groups=[[0, 1, 2, 3], [4, 5, 6, 7]],
                                                                            ins=[ysend.ap()[j * 256:(j + 1) * 256, :].opt()],
                                                                            outs=[yall.ap()[j * 1024:(j + 1) * 1024, :].opt()]),
                               reads=[r_ys[jc]], writes=[r_cc], dma="cc", inc=1)

                    pend_fin.append([finB, finC])
            if pend_fin:
                for f_ in pend_fin.pop():
                    f_()
            sc.flush()

        if upto < 4:
            return nc
        if dbg:
            op("sp", lambda e: e.dma_start(out=dbg_y, in_=ysend.ap()), dma="dbg")
            sc.flush()

        if upto < 5:
            return nc
        with ExitStack() as ps:
            sb = lambda n, s, d: ps.enter_context(nc.sbuf_tensor(n, s, d))
            pp = lambda n, s, d: ps.enter_context(nc.psum_tensor(n, s, d))
            Wp = sb("Wp", [128, 2, D], BF16)
            fn_t = sb("fn_t", [128, D], F32)
            x5 = [wst[0]] + [sb("x5_%d" % i, [128, D], F32) for i in range(1, 3)]
            h1 = [sb("h1_%d" % i, [128, D], F32) for i in range(2)]
            hb = [sb("hb%d" % i, [128, D], BF16) for i in range(2)]
            hT = [sb("hT%d" % i, [128, 16, 128], BF16) for i in range(2)]
            yT = [sb("yT%d" % i, [128, 8, 512], BF16) for i in range(2)]
            pf = [sb("pf%d" % i, [128, 2, 512], F32) for i in range(2)]
            pb = [sb("pb%d" % i, [128, 2, 512], BF16) for i in range(2)]
            e5 = [sb("e5_%d" % i, [128, 512], F32) for i in range(2)]
            st5 = [sb("st5_%d" % i, [128, 8], F32) for i in range(2)]
            st5b = [sb("st5b_%d" % i, [128, 8], F32) for i in range(2)]
            r_st5b = sc.res(2)
            junk5 = sb("junk5", [128, D], BF16)
            pH = [pp("pH%d" % i, [128, 512], F32) for i in range(2)]
            pT5 = pp("pT5", [128, D], BF16)
            pG = [pp("pG%d" % i, [128, 512], F32) for i in range(2)]
            pP = [pp("pP%d" % i, [128, 512], F32) for i in range(2)]
            r_x5, r_h1, r_hb, r_yT, r_pf, r_pb, r_e5, r_st5 = (sc.res(3), sc.res(2), sc.res(2), sc.res(2), sc.res(2), sc.res(2), sc.res(2), sc.res(2))
            r_hT = [sc.res(2) for _ in range(2)]
            r_pH, r_pT5, r_pG, r_pP = sc.res(2), sc.res(), sc.res(2), sc.res(2)

            op("sp", lambda e: e.dma_start(out=fn_t[:], in_=fnorm.rearrange("a b -> (a b)").partition_broadcast(128)), writes=[r_fn], dma="c5")
            wl = 0
            for kc in range(2):
                sl = wl % 2
                wl += 1
                op("sp", lambda e, kc=kc, sl=sl: e.dma_start(out=x5[1 + sl][:], in_=wproj[kc * 128:(kc + 1) * 128, :]), writes=[r_x5[1 + sl]], dma="x5_%d" % (1 + sl))
                op("dve", lambda e, kc=kc, sl=sl: e.tensor_copy(out=Wp[:, kc, :], in_=x5[1 + sl][:]), reads=[r_x5[1 + sl]], writes=[r_Wp])

            rk = {}

            def rank_of(e):
                if "r" not in rk:
                    rk["r"] = e.partition_id() % 4
                return rk["r"]

            def ycols(e, nq4, cst):
                assert nq4 % CW == 0
                ck = rank_of(e) * ((nq4 // CW) * 1024) + (cst // CW) * 1024
                off = cst % CW
                return yall.ap()[bass.ds(ck, 1024), off:off + 512].rearrange("(c p) t -> p c t", p=128)

            tails = [("p", SP, out_p, 0), ("s", SS, out_s, SP // 4)]
            cnt = dict(g=0, h=0)
            tiles = []
            blks = []
            for k, S, outp, poff in tails:
                nq4 = S // 4
                ybase = 0 if k == "p" else SP
                for b4 in range(nq4 // 512):
                    lc0 = b4 * 512
                    bidx = len(blks)
                    blks.append(dict(ys=bidx % 2, lc0=lc0, nq4=nq4, ybase=ybase, poff=poff))
                    for ti in range(4):
                        t = len(tiles)
                        tiles.append(dict(k=k, outp=outp, l0=lc0 + ti * 128, ys=bidx % 2, ti=ti, s2=t % 2, xsl=t % 3, bidx=bidx))

            def blockload(bidx):
                if bidx >= len(blks):
                    return
                B_ = blks[bidx]
                ys, lc0, nq4, ybase, poff = B_["ys"], B_["lc0"], B_["nq4"], B_["ybase"], B_["poff"]
                need_cc = (SP // CW) if ybase == 0 else NCK

                def ld_y(e):
                    return e.dma_start(out=yT[ys][:], in_=ycols(e, nq4, ybase + lc0))

                op("sp", ld_y, writes=[r_yT[ys]], dma="yT%d" % ys)
                op("sp", lambda e: e.dma_start(out=pf[ys][:], in_=pT[:, poff + lc0:poff + lc0 + 512].rearrange("(c p) t -> p c t", p=128)),
                   writes=[r_pf[ys]], dma="yT%d" % ys)
                op("pool", lambda e: e.tensor_copy(out=pb[ys][:], in_=pf[ys][:]), reads=[r_pf[ys]], writes=[r_pb[ys]])

            def S1(t):
                T_ = tiles[t]
                k, l0, ys, ti, s2, xsl = T_["k"], T_["l0"], T_["ys"], T_["ti"], T_["s2"], T_["xsl"]
                if ti == 1:
                    blockload(T_["bidx"] + 1)
                op("sp", lambda e: e.dma_start(out=x5[xsl][:], in_=xq_in[k][l0:l0 + 128, :]), writes=[r_x5[xsl]], dma="x5_%d" % xsl)
                for cc in range(4):
                    hs = cnt["h"] % 2
                    cnt["h"] += 1
                    for kc in range(8):
                        op("pe", lambda e, kc=kc, cc=cc, hs=hs: e.matmul(pH[hs][:], lhsT=yT[ys][:, kc, ti * 128:(ti + 1) * 128],
                                                                        rhs=Wo[:, kc, cc * 512:(cc + 1) * 512], start=(kc == 0), stop=(kc == 7)),
                           reads=[r_yT[ys], r_Wo], writes=[r_pH[hs]])
                    op("dve", lambda e, cc=cc, hs=hs: e.tensor_tensor(out=h1[s2][:, cc * 512:(cc + 1) * 512], in0=pH[hs][:],
                                                                      in1=x5[xsl][:, cc * 512:(cc + 1) * 512], op=ALU.add),
                       reads=[r_pH[hs], r_x5[xsl]], writes=[r_h1[s2]])

            def S2(t):
                s2 = tiles[t]["s2"]
                st = st5[s2]
                op("act", lambda e: e.activation(out=junk5[:], in_=h1[s2][:], func=AF.Square, accum_out=st[:, 0:1]),
                   reads=[r_h1[s2]], writes=[r_st5[s2]])
                op("dve", lambda e: e.tensor_scalar(out=st[:, 1:2], in0=st[:, 0:1], scalar1=1.0 / D, scalar2=EPS, op0=ALU.mult, op1=ALU.add),
                   reads=[r_st5[s2]], writes=[r_st5[s2]])
                op("act", lambda e: e.activation(out=st[:, 2:3], in_=st[:, 1:2], func=AF.Ln), reads=[r_st5[s2]], writes=[r_st5[s2]])
                op("act", lambda e: e.activation(out=st[:, 3:4], in_=st[:, 2:3], func=AF.Exp, scale=-0.5), reads=[r_st5[s2]], writes=[r_st5[s2]])
                op("act", lambda e: e.activation(out=hb[s2][:], in_=h1[s2][:], func=AF.Copy, scale=st[:, 3:4]),
                   reads=[r_h1[s2], r_st5[s2]], writes=[r_hb[s2]])

            def S3(t):
                s2 = tiles[t]["s2"]
                for kc in range(16):
                    op("pe", lambda e, kc=kc: e.transpose(pT5[:, kc * 128:(kc + 1) * 128], hb[s2][:, kc * 128:(kc + 1) * 128], ident),
                       reads=[r_hb[s2], r_cm], writes=[r_pT5])
                src5 = pT5[:].rearrange("p (a b) -> p a b", b=128)
                op("dve", lambda e: e.tensor_copy(out=hT[s2][:, 0:8, :], in_=src5[:, 0:8, :]), reads=[r_pT5], writes=[r_hT[s2][0]])
                op("act", lambda e: e.activation(out=hT[s2][:, 8:16, :], in_=src5[:, 8:16, :], func=AF.Copy), reads=[r_pT5], writes=[r_hT[s2][1]])

            def S4(t):
                T_ = tiles[t]
                ys, ti, s2, xsl = T_["ys"], T_["ti"], T_["s2"], T_["xsl"]
                for cc in range(4):
                    gs = cnt["g"] % 2
                    cnt["g"] += 1
                    cs = slice(cc * 512, (cc + 1) * 512)
                    for kc in range(16):
                        op("pe", lambda e, kc=kc, cs=cs, gs=gs: e.matmul(pG[gs][:], lhsT=hT[s2][:, kc, :], rhs=Wg[:, kc, cs],
                                                                        start=(kc == 0), stop=(kc == 15)),
                           reads=r_hT[s2] + [r_Wg], writes=[r_pG[gs]])
                    for kc in range(2):
                        op("pe", lambda e, kc=kc, cs=cs, gs=gs: e.matmul(pP[gs][:], lhsT=pb[ys][:, kc, ti * 128:(ti + 1) * 128], rhs=Wp[:, kc, cs],
                                                                        start=(kc == 0), stop=(kc == 1)),
                           reads=[r_pb[ys], r_Wp], writes=[r_pP[gs]])
                    op("act", lambda e, gs=gs: e.activation(out=e5[gs][:], in_=pG[gs][:], func=AF.Exp, scale=-1.0), reads=[r_pG[gs]], writes=[r_e5[gs]])
                    op("act", lambda e, gs=gs: e.activation(out=e5[gs][:], in_=e5[gs][:], func=AF.Ln, bias=one_c[:, 0:1]),
                       reads=[r_e5[gs], r_one], writes=[r_e5[gs]])
                    op("act", lambda e, gs=gs: e.activation(out=e5[gs][:], in_=e5[gs][:], func=AF.Exp, scale=-1.0),
                       reads=[r_e5[gs]], writes=[r_e5[gs]])
                    op("dve", lambda e, gs=gs: e.tensor_tensor(out=e5[gs][:], in0=e5[gs][:], in1=pP[gs][:], op=ALU.mult),
                       reads=[r_e5[gs], r_pP[gs]], writes=[r_e5[gs]])
                    op("pool", lambda e, gs=gs, cs=cs: e.tensor_tensor(out=x5[xsl][:, cs], in0=h1[s2][:, cs], in1=e5[gs][:], op=ALU.add),
                       reads=[r_e5[gs], r_h1[s2], r_x5[xsl]], writes=[r_x5[xsl]])

            def S5(t):
                T_ = tiles[t]
                s2, xsl, outp, l0 = T_["s2"], T_["xsl"], T_["outp"], T_["l0"]
                st = st5b[s2]
                op("act", lambda e: e.activation(out=junk5[:], in_=x5[xsl][:], func=AF.Square, accum_out=st[:, 4:5]),
                   reads=[r_x5[xsl]], writes=[r_st5b[s2]])
                op("dve", lambda e: e.tensor_scalar(out=st[:, 5:6], in0=st[:, 4:5], scalar1=1.0 / D, scalar2=EPS, op0=ALU.mult, op1=ALU.add),
                   reads=[r_st5b[s2]], writes=[r_st5b[s2]])
                op("act", lambda e: e.activation(out=st[:, 6:7], in_=st[:, 5:6], func=AF.Ln), reads=[r_st5b[s2]], writes=[r_st5b[s2]])
                op("act", lambda e: e.activation(out=st[:, 7:8], in_=st[:, 6:7], func=AF.Exp, scale=-0.5), reads=[r_st5b[s2]], writes=[r_st5b[s2]])
                op("dve", lambda e: e.scalar_tensor_tensor(out=x5[xsl][:], in0=x5[xsl][:], scalar=st[:, 7:8], in1=fn_t[:],
                                                           op0=ALU.mult, op1=ALU.mult),
                   reads=[r_st5b[s2], r_fn, r_x5[xsl]], writes=[r_x5[xsl]])
                op("pool", lambda e: e.dma_start(out=outp[l0:l0 + 128, :], in_=x5[xsl][:]),
                   reads=[r_x5[xsl]], writes=[r_x5[xsl]], dma="o5_%d" % xsl)

            NTL = len(tiles)
            blockload(0)
            S1(0)
            S2(0)
            for t in range(NTL):
                if t + 1 < NTL:
                    S1(t + 1)
                S3(t)
                if t >= 1:
                    S5(t - 1)
                if t + 1 < NTL:
                    S2(t + 1)
                S4(t)
            S5(NTL - 1)
            sc.flush(final=True)
    return nc


_CACHE = {}


def _prep_inputs(inputs, SP, SS):
    f = lambda a: np.ascontiguousarray(np.asarray(a, dtype=np.float32))
    xP, xS = f(inputs["x_prompt"]), f(inputs["x_sample"])
    pP, pS = f(inputs["p_prompt"])[0], f(inputs["p_sample"])[0]
    w_in = f(inputs["w_in"])[0]
    cm, tabs = _consts(max(SP, SS))
    to_pk = lambda v: np.ascontiguousarray(v.reshape(16, 128).T)
    lamv = np.stack([f(inputs[n])[0] for n in ("lam_q1", "lam_k1", "lam_q2", "lam_k2")], 0)
    common = dict(nmix=to_pk(f(inputs["norm_mix"])[0]), lamv=lamv, subln=f(inputs["subln"]).reshape(1, 128),
                  wout=f(inputs["w_out"])[0], pnorm=to_pk(f(inputs["ple_norm"])[0]), wgate=f(inputs["w_ple_gate"])[0],
                  wproj=f(inputs["w_ple_proj"])[0], fnorm=f(inputs["final_norm"]).reshape(1, D), cmat=cm, tabs=tabs)
    maps = []
    for c in range(8):
        b, h = c // 4, c % 4
        cols = []
        for base in (0, 512, 1024, 1536):
            cols.append(np.arange(base + h * 128, base + (h + 1) * 128))
        for base in (2048, 3584, 5120):
            for g in range(3):
                cols.append(np.arange(base + g * 512 + h * 128, base + g * 512 + (h + 1) * 128))
        cols.append(np.arange(6656 + h * 128, 6656 + (h + 1) * 128))
        cols = np.concatenate(cols)
        qp, qs = SP // 4, SS // 4
        pT = np.concatenate([pP[b, h * qp:(h + 1) * qp].T, pS[b, h * qs:(h + 1) * qs].T], 1)
        m = dict(common)
        m.update(xp=xP[b], xs=xS[b], xqp=xP[b, h * qp:(h + 1) * qp], xqs=xS[b, h * qs:(h + 1) * qs], wh=np.ascontiguousarray(w_in[:, cols]), pT=np.ascontiguousarray(pT))
        maps.append(m)
    return maps


def kernel(_dbg=False, _upto=5, **inputs):
    SP = inputs["x_prompt"].shape[1]
    SS = inputs["x_sample"].shape[1]
    key = (SP, SS, _dbg, _upto)
    if key not in _CACHE:
        _CACHE[key] = build(SP, SS, _dbg, _upto)
    nc = _CACHE[key]
    maps = _prep_inputs(inputs, SP, SS)
    res = run_bass_kernel_spmd(nc, maps, core_ids=list(range(8)))
    if _dbg:
        return res
    yp = np.zeros((2, SP, D), np.float32)
    ys = np.zeros((2, SS, D), np.float32)
    qp, qs = SP // 4, SS // 4
    for c in range(8):
        b, h = c // 4, c % 4
        yp[b, h * qp:(h + 1) * qp] = res.results[c]["out_p"]
        ys[b, h * qs:(h + 1) * qs] = res.results[c]["out_s"]
    return (yp, ys)
```

```python
import numpy as np
import ml_dtypes
from contextlib import ExitStack
import concourse.bass as bass
import concourse.mybir as mybir
from concourse.bass_utils import run_bass_kernel_spmd

F32 = mybir.dt.float32
BF16 = mybir.dt.bfloat16
AF = mybir.ActivationFunctionType
ALU = mybir.AluOpType
AX = mybir.AxisListType
bf16 = ml_dtypes.bfloat16

D = 2048
NCH = 14
PAD = 1024
EPS = 1e-6
DILS = (1, 4, 16)
C_AQ, C_AK, C_AV, C_AG = 0, 1, 2, 3
C_BQ, C_BK, C_BV, C_BG = 4, 7, 10, 13
ROPE_DIFF = (C_AQ, C_AK)
ROPE_DIL = (4, 5, 6, 7, 8, 9)
COPY_CH = (C_AV, 10, 11, 12)
GATE_CH = (C_AG, C_BG)
NEGM = -30000.0


class Res:
    __slots__ = ("lw", "rd")

    def __init__(self):
        self.lw = None
        self.rd = {}


class Op:
    __slots__ = ("eng", "fn", "deps", "dma", "need", "sem", "sigval", "inc", "dwaits", "done")


class Sched:
    ENGS = ("pe", "act", "dve", "pool", "sp")

    def __init__(self, nc, es):
        self.nc = nc
        self.es = es
        self.sem = {e: es.enter_context(nc.semaphore("s_" + e)) for e in self.ENGS}
        self.cnt = {e: 0 for e in self.ENGS}
        self.gsem = {}
        self.gcnt = {}
        self.ops = []
        self.waited = {e: {} for e in self.ENGS}
        self.nres = 0

    def res(self, n=None):
        if n is None:
            return Res()
        return [Res() for _ in range(n)]

    def _dep(self, o, p, kind):
        if p is o or p.done:
            return
        if p.dma is None and o.dma is None and p.eng == o.eng:
            if o.eng == "pe":
                return
        o.deps.add(p)

    def op(self, eng, fn, reads=(), writes=(), dma=None, inc=None):
        o = Op()
        o.eng, o.fn, o.dma, o.deps, o.need = eng, fn, dma, set(), False
        o.sem = o.sigval = None
        o.inc = inc
        o.done = False
        for r in reads:
            if r.lw is not None:
                self._dep(o, r.lw, "raw")
        for w in writes:
            if w.lw is not None:
                self._dep(o, w.lw, "waw")
            for rr in w.rd.values():
                self._dep(o, rr, "war")
        for w in writes:
            w.lw = o
            w.rd = {}
        for r in reads:
            r.rd[eng if dma is None else ("d", dma)] = o
        o.dwaits = {}
        for p in o.deps:
            if p.dma is not None:
                o.dwaits[p.dma] = self.gcnt[p.dma]
        if dma is not None:
            if dma not in self.gsem:
                self.gsem[dma] = self.es.enter_context(self.nc.semaphore("g_" + dma))
                self.gcnt[dma] = 0
            o.inc = 16 if inc is None else inc
            self.gcnt[dma] += o.inc
            o.sem, o.sigval = self.gsem[dma], self.gcnt[dma]
        self.ops.append(o)
        return o

    def flush(self, final=False, drain_cc=True):
        nc = self.nc
        ops = self.ops
        self.ops = []
        for o in ops:
            o.done = True
            for p in o.deps:
                p.need = True
        for o in ops:
            if o.dma is None and o.need:
                self.cnt[o.eng] += 1
                o.sem, o.sigval, o.inc = self.sem[o.eng], self.cnt[o.eng], 1
        per = {e: [] for e in self.ENGS}
        for o in ops:
            per[o.eng].append(o)
        gs = [(self.gsem[g], self.gcnt[g]) for g in self.gsem]

        def mk(ename):
            def body(e):
                wd = self.waited[ename]
                for o in per[ename]:
                    ws = {}
                    for p in o.deps:
                        if p.dma is None:
                            k = id(p.sem)
                            if k not in ws or ws[k][1] < p.sigval:
                                ws[k] = (p.sem, p.sigval)
                    for g, v in o.dwaits.items():
                        ws[id(self.gsem[g])] = (self.gsem[g], v)
                    for k, (s, v) in ws.items():
                        if wd.get(k, 0) < v:
                            e.wait_ge(s, v)
                            wd[k] = v
                    ins = o.fn(e)
                    if o.sigval is not None:
                        ins.then_inc(o.sem, o.inc)
                if ename in ("sp", "pool"):
                    for g_, (s, v) in zip(list(self.gsem), gs):
                        if (g_ == "cc") != (ename == "pool"):
                            continue
                        if g_ == "cc" and not drain_cc:
                            continue
                        if v > 0 and wd.get(id(s), 0) < v:
                            e.wait_ge(s, v)
                            wd[id(s)] = v
            return body

        with nc.Block() as block:
            block.tensor(mk("pe"))
            block.scalar(mk("act"))
            block.vector(mk("dve"))
            block.gpsimd(mk("pool"))
            block.sync(mk("sp"))


def _consts(smax):
    ident = np.eye(128, dtype=np.float32)
    rdil = np.zeros((128, 128), np.float32)
    for f in range(64):
        rdil[f + 64, f] = -1.0
        rdil[f, f + 64] = 1.0
    rdiff = np.zeros((128, 128), np.float32)
    for b0 in (0, 64):
        for f in range(32):
            rdiff[b0 + f + 32, b0 + f] = -1.0
            rdiff[b0 + f, b0 + f + 32] = 1.0
    kp = np.arange(128)[:, None]
    qf = np.arange(128)[None, :]
    lo = qf >= kp
    hi = qf <= kp
    masks = np.stack([lo, hi, lo & (kp < 64), hi & (kp >= 64)], 1)
    masks = np.where(masks, 0.0, NEGM).astype(np.float32)
    cm = np.concatenate([ident[:, None, :], rdil[:, None, :], rdiff[:, None, :],
                         np.ones((128, 1, 128), np.float32), masks], 1)
    pos = np.arange(smax, dtype=np.float32)
    inv_dil = (10000.0 ** (-np.arange(0, 128, 2, dtype=np.float32) / 128)).astype(np.float32)
    inv_dif = (10000.0 ** (-np.arange(0, 64, 2, dtype=np.float32) / 64)).astype(np.float32)
    a_dil = (pos[None, :] * inv_dil[np.arange(128) % 64][:, None]).astype(np.float32)
    a_dif = (pos[None, :] * inv_dif[(np.arange(128) % 64) % 32][:, None]).astype(np.float32)
    tabs = np.stack([np.cos(a_dif), np.sin(a_dif), np.cos(a_dil), np.sin(a_dil)], 1)
    return cm.astype(bf16), np.ascontiguousarray(tabs.astype(np.float32))


def build(SP, SS, dbg=False, upto=5):
    nc = bass.Bass("TRN2", target_bir_lowering=False)
    seqs = [("p", SP), ("s", SS)]
    SMAX = max(SP, SS)
    TT = SP // 4 + SS // 4
    TOT = SP + SS
    din = {}

    def inp(name, shape, dt=F32):
        din[name] = nc.dram_tensor(name, list(shape), dt, kind="ExternalInput").ap()
        return din[name]

    x_in = {"p": inp("xp", [SP, D]), "s": inp("xs", [SS, D])}
    xq_in = {"p": inp("xqp", [SP // 4, D]), "s": inp("xqs", [SS // 4, D])}
    wh = inp("wh", [D, NCH * 128])
    nmix = inp("nmix", [128, 16])
    lamv = inp("lamv", [4, 64])
    subln = inp("subln", [1, 128])
    wout = inp("wout", [1024, D])
    pnorm = inp("pnorm", [128, 16])
    wgate = inp("wgate", [D, D])
    wproj = inp("wproj", [256, D])
    fnorm = inp("fnorm", [1, D])
    pT = inp("pT", [256, TT])
    cmat = inp("cmat", [128, 8, 128], BF16)
    tabs = inp("tabs", [128, 4, SMAX])
    out_p = nc.dram_tensor("out_p", [SP // 4, D], F32, kind="ExternalOutput").ap()
    out_s = nc.dram_tensor("out_s", [SS // 4, D], F32, kind="ExternalOutput").ap()
    Z = {k: nc.dram_tensor("Z" + k, [NCH, 128, S + 2 * PAD], BF16).ap() for k, S in seqs}
    CW = 1024
    NCK = TOT // CW
    ysend = nc.dram_tensor("ysend", [NCK * 256, CW], BF16)
    yall = nc.dram_tensor("yall", [NCK * 1024, CW], BF16)
    if dbg:
        dbg_z = nc.dram_tensor("dbg_z", [NCH, 128, SS + 2 * PAD], BF16, kind="ExternalOutput").ap()
        dbg_y = nc.dram_tensor("dbg_y", [(TOT // 1024) * 256, 1024], BF16, kind="ExternalOutput").ap()

    es = ExitStack()
    with es:
        sc = Sched(nc, es)
        op = sc.op
        cm = es.enter_context(nc.sbuf_tensor("cm", [128, 8, 128], BF16))
        ident, rdil, rdiff, ones_b = cm[:, 0, :], cm[:, 1, :], cm[:, 2, :], cm[:, 3, :]
        lam_t = es.enter_context(nc.sbuf_tensor("lam_t", [128, 8], F32))
        one_c = es.enter_context(nc.sbuf_tensor("one_c", [128, 1], F32))
        r_one = sc.res()
        r_cm = sc.res()
        r_lam = sc.res()

        with ExitStack() as ps:
            sb = lambda n, s, d: ps.enter_context(nc.sbuf_tensor(n, s, d))
            pp = lambda n, s, d: ps.enter_context(nc.psum_tensor(n, s, d))
            Wb = sb("Wb", [128, 16, NCH * 128], BF16)
            xs_t = [sb("xs%d" % i, [128, D], F32) for i in range(4)]
            junk = sb("junk", [128, D], BF16)
            st_t = [sb("st%d" % i, [128, 8], F32) for i in range(2)]
            xb_t = [sb("xb%d" % i, [128, D], BF16) for i in range(2)]
            uT_t = [sb("uT%d" % i, [128, 16, 512], BF16) for i in range(2)]
            tab_t = [sb("tab%d" % i, [128, 4, 512], F32) for i in range(2)]
            stg_t = [sb("stg%d" % i, [128, NCH, 512], BF16) for i in range(2)]
            zb_t = [sb("zb%d" % i, [128, 512], BF16) for i in range(2)]
            t1_t = [sb("t1%d" % i, [128, 512], F32) for i in range(2)]
            t2_t = [sb("t2%d" % i, [128, 512], F32) for i in range(2)]
            ge_t = [sb("ge%d" % i, [128, 512], F32) for i in range(2)]
            ones_f = sb("ones_f", [128, 512], F32)
            nm_t = sb("nm_t", [128, 16], F32)
            lv_t = sb("lv_t", [128, 4, 64], F32)
            zero_t = sb("zero_t", [128, PAD], BF16)
            pT_ps = [pp("pT%d" % i, [128, D], BF16) for i in range(2)]
            pZ = [pp("pZ%d" % i, [128, 512], F32) for i in range(2)]
            pR = [pp("pR%d" % i, [128, 512], F32) for i in range(2)]
            r_W, r_nm, r_ones, r_zero, r_lv = sc.res(), sc.res(), sc.res(), sc.res(), sc.res()
            r_xs, r_st, r_xb = sc.res(4), sc.res(2), sc.res(2)
            r_uT = [[[sc.res(), sc.res()] for _ in range(4)] for _ in range(2)]
            r_tab, r_zb, r_t1, r_t2, r_ge = sc.res(2), sc.res(2), sc.res(2), sc.res(2), sc.res(2)
            r_stg = [sc.res(NCH) for _ in range(2)]
            r_pT, r_pZ, r_pR = sc.res(2), sc.res(2), sc.res(2)

            op("sp", lambda e: e.dma_start(out=cm[:], in_=cmat), writes=[r_cm], dma="cm")
            op("sp", lambda e: e.dma_start(out=nm_t[:], in_=nmix), writes=[r_nm], dma="cm")
            op("sp", lambda e: e.dma_start(out=lv_t[:].rearrange("p a b -> p (a b)"),
                                           in_=lamv.rearrange("a b -> (a b)").partition_broadcast(128)),
               writes=[r_lv], dma="cm")
            op("dve", lambda e: e.memset(ones_f[:], 1.0), writes=[r_ones])
            op("dve", lambda e: e.memset(one_c[:], 1.0), writes=[r_one])
            op("dve", lambda e: e.memset(zero_t[:], 0.0), writes=[r_zero])
            for k, S in seqs:
                for c in range(7, 13):
                    for off in (0, PAD + S):
                        op("pool", lambda e, k=k, c=c, off=off: e.dma_start(
                            out=Z[k][c, :, off:off + PAD], in_=zero_t[:]), reads=[r_zero], dma="zp")
            op("dve", lambda e: e.tensor_tensor(out=lv_t[:, 0, :], in0=lv_t[:, 0, :], in1=lv_t[:, 1, :], op=ALU.mult),
               reads=[r_lv], writes=[r_lv])
            op("dve", lambda e: e.tensor_tensor(out=lv_t[:, 2, :], in0=lv_t[:, 2, :], in1=lv_t[:, 3, :], op=ALU.mult),
               reads=[r_lv], writes=[r_lv])
            op("dve", lambda e: e.reduce_sum(out=lam_t[:, 2:3], in_=lv_t[:, 0, :], axis=AX.X), reads=[r_lv], writes=[r_lam])
            op("dve", lambda e: e.reduce_sum(out=lam_t[:, 3:4], in_=lv_t[:, 2, :], axis=AX.X), reads=[r_lv, r_lam], writes=[r_lam])
            op("act", lambda e: e.activation(out=lam_t[:, 4:6], in_=lam_t[:, 2:4], func=AF.Exp), reads=[r_lam], writes=[r_lam])
            op("dve", lambda e: e.tensor_tensor(out=lam_t[:, 6:7], in0=lam_t[:, 4:5], in1=lam_t[:, 5:6], op=ALU.subtract),
               reads=[r_lam], writes=[r_lam])
            op("dve", lambda e: e.tensor_scalar(out=lam_t[:, 0:1], in0=lam_t[:, 6:7], scalar1=0.2, scalar2=None, op0=ALU.add),
               reads=[r_lam], writes=[r_lam])
            op("dve", lambda e: e.tensor_scalar(out=lam_t[:, 1:2], in0=lam_t[:, 0:1], scalar1=-1.0, scalar2=None, op0=ALU.mult),
               reads=[r_lam], writes=[r_lam])
            for kc in range(16):
                sl = kc % 3
                for hf in range(2):
                    cs = slice(hf * 896, (hf + 1) * 896)
                    op("sp", lambda e, kc=kc, sl=sl, cs=cs: e.dma_start(out=xs_t[sl][:, 0:896], in_=wh[kc * 128:(kc + 1) * 128, cs]),
                       writes=[r_xs[sl]], dma="x%d" % sl)
                    op("dve", lambda e, kc=kc, sl=sl, cs=cs: e.tensor_scalar(out=Wb[:, kc, cs], in0=xs_t[sl][:, 0:896],
                                                                            scalar1=nm_t[:, kc:kc + 1], scalar2=None, op0=ALU.mult),
                       reads=[r_xs[sl], r_nm], writes=[r_W])

            blocks = [(k, S, b) for k, S in seqs for b in range(S // 512)]
            import os as _os
            if _os.environ.get("KDBG_NBLK"):
                blocks = blocks[:int(_os.environ["KDBG_NBLK"])]
            gtile = [0]

            NX = 4
            NT = len(blocks) * 4

            def tile_info(g):
                bi, ti = g // 4, g % 4
                k, S, b = blocks[bi]
                return bi, ti, k, b * 512 + ti * 128

            def prep_load(g):
                if g >= NT:
                    return
                bi, ti, k, t0 = tile_info(g)
                xsl = g % NX
                op("sp", lambda e: e.dma_start(out=xs_t[xsl][:], in_=x_in[k][t0:t0 + 128, :]), writes=[r_xs[xsl]], dma="x%d" % xsl)

            def prep_A(g):
                if g >= NT:
                    return
                xsl, s2 = g % NX, g % 2
                op("act", lambda e: e.activation(out=junk[:], in_=xs_t[xsl][:], func=AF.Square, accum_out=st_t[s2][:, 0:1]),
                   reads=[r_xs[xsl]], writes=[r_st[s2]])
                op("dve", lambda e: e.tensor_scalar(out=st_t[s2][:, 1:2], in0=st_t[s2][:, 0:1], scalar1=1.0 / D, scalar2=EPS,
                                                    op0=ALU.mult, op1=ALU.add), reads=[r_st[s2]], writes=[r_st[s2]])
                op("act", lambda e: e.activation(out=st_t[s2][:, 2:3], in_=st_t[s2][:, 1:2], func=AF.Ln), reads=[r_st[s2]], writes=[r_st[s2]])
                op("act", lambda e: e.activation(out=st_t[s2][:, 3:4], in_=st_t[s2][:, 2:3], func=AF.Exp, scale=-0.5),
                   reads=[r_st[s2]], writes=[r_st[s2]])
                op("dve", lambda e: e.tensor_scalar(out=xb_t[s2][:], in0=xs_t[xsl][:], scalar1=st_t[s2][:, 3:4], scalar2=None, op0=ALU.mult),
                   reads=[r_xs[xsl], r_st[s2]], writes=[r_xb[s2]])
                prep_load(g + 2)

            def prep_B(g):
                if g >= NT:
                    return
                bi, ti, k, t0 = tile_info(g)
                s2 = g % 2
                us = bi % 2
                for kc in range(16):
                    op("pe", lambda e, kc=kc: e.transpose(pT_ps[s2][:, kc * 128:(kc + 1) * 128], xb_t[s2][:, kc * 128:(kc + 1) * 128], ident),
                       reads=[r_xb[s2], r_cm], writes=[r_pT[s2]])
                src = pT_ps[s2][:].rearrange("p (a b) -> p a b", b=128)
                op("act", lambda e: e.activation(out=uT_t[us][:, 0:8, ti * 128:(ti + 1) * 128], in_=src[:, 0:8, :], func=AF.Copy),
                   reads=[r_pT[s2]], writes=[r_uT[us][ti][0]])
                op("dve", lambda e: e.tensor_copy(out=uT_t[us][:, 8:16, ti * 128:(ti + 1) * 128], in_=src[:, 8:16, :]),
                   reads=[r_pT[s2]], writes=[r_uT[us][ti][1]])

            def rot_part(bi, c, zs):
                us = bi % 2
                isdil = c in ROPE_DIL
                rm = rdil if isdil else rdiff
                ti = 2 if isdil else 0
                op("pe", lambda e: e.matmul(pR[zs][:], lhsT=rm, rhs=zb_t[zs][:], start=True, stop=True),
                   reads=[r_zb[zs], r_cm], writes=[r_pR[zs]])
                op("dve", lambda e: e.tensor_tensor(out=t2_t[zs][:], in0=pR[zs][:], in1=tab_t[us][:, ti + 1, :], op=ALU.mult),
                   reads=[r_pR[zs], r_tab[us]], writes=[r_t2[zs]])
                op("dve", lambda e: e.tensor_tensor(out=stg_t[us][:, c, :], in0=t1_t[zs][:], in1=t2_t[zs][:], op=ALU.add),
                   reads=[r_t1[zs], r_t2[zs]], writes=[r_stg[us][c]])

            if blocks:
                prep_load(0)
                prep_load(1)
                prep_A(0)
                prep_A(1)
                prep_B(0)
                prep_A(2)
                prep_B(1)
                prep_A(3)
                prep_B(2)
                prep_B(3)
            zc = 0
            for bi, (k, S, b) in enumerate(blocks):
                us = bi % 2
                t0 = b * 512
                op("sp", lambda e, us=us, t0=t0: e.dma_start(out=tab_t[us][:], in_=tabs[:, :, t0:t0 + 512]), writes=[r_tab[us]], dma="tab%d" % us)
                pend = None
                _cut = _os.environ.get("KDBG_CUT", "")
                for c in range(NCH):
                    if _cut == "prep":
                        break
                    if _cut == "rope" and c not in (0,):
                        continue
                    if _cut == "copy" and c not in (2,):
                        continue
                    if _cut == "gate" and c not in (3,):
                        continue
                    zs = zc % 2
                    zc += 1
                    uall = [r for t in r_uT[us] for r in t]
                    for kc in range(16):
                        op("pe", lambda e, c=c, kc=kc, zs=zs, us=us: e.matmul(pZ[zs][:], lhsT=Wb[:, kc, c * 128:(c + 1) * 128],
                                                                              rhs=uT_t[us][:, kc, :], start=(kc == 0), stop=(kc == 15)),
                           reads=[r_W] + uall, writes=[r_pZ[zs]])
                    if c in ROPE_DIFF or c in ROPE_DIL:
                        ti_ = 2 if c in ROPE_DIL else 0
                        op("act", lambda e, zs=zs: e.activation(out=zb_t[zs][:], in_=pZ[zs][:], func=AF.Copy),
                           reads=[r_pZ[zs]], writes=[r_zb[zs], r_pZ[zs]])
                        op("dve", lambda e, zs=zs, us=us, ti_=ti_: e.tensor_tensor(out=t1_t[zs][:], in0=pZ[zs][:], in1=tab_t[us][:, ti_, :], op=ALU.mult),
                           reads=[r_pZ[zs], r_tab[us]], writes=[r_t1[zs], r_pZ[zs]])
                    elif c in COPY_CH:
                        op("act", lambda e, zs=zs, us=us, c=c: e.activation(out=stg_t[us][:, c, :], in_=pZ[zs][:], func=AF.Copy),
                           reads=[r_pZ[zs]], writes=[r_stg[us][c]])
                    else:
                        op("act", lambda e, zs=zs: e.activation(out=ge_t[zs][:], in_=pZ[zs][:], func=AF.Exp, scale=-1.0),
                           reads=[r_pZ[zs]], writes=[r_ge[zs]])
                        op("act", lambda e, zs=zs: e.activation(out=ge_t[zs][:], in_=ge_t[zs][:], func=AF.Ln, bias=one_c[:, 0:1]),
                           reads=[r_ge[zs], r_one], writes=[r_ge[zs]])
                        op("act", lambda e, zs=zs: e.activation(out=ge_t[zs][:], in_=ge_t[zs][:], func=AF.Exp, scale=-1.0),
                           reads=[r_ge[zs]], writes=[r_ge[zs]])
                        op("dve", lambda e, zs=zs, us=us, c=c: e.tensor_tensor(out=stg_t[us][:, c, :], in0=pZ[zs][:], in1=ge_t[zs][:], op=ALU.mult),
                           reads=[r_pZ[zs], r_ge[zs]], writes=[r_stg[us][c]])
                    if pend is not None:
                        rot_part(bi, *pend)
                        pend = None
                    if c in ROPE_DIFF or c in ROPE_DIL:
                        pend = (c, zs)
                    if c in (0, 3, 6, 9):
                        prep_A((bi + 1) * 4 + c // 3)
                    if c in (2, 5, 8, 11):
                        prep_B((bi + 1) * 4 + (c - 2) // 3)
                if pend is not None:
                    rot_part(bi, *pend)
                    pend = None
                if _cut:
                    continue
                op("pool", lambda e, k=k, us=us, t0=t0: e.dma_start(
                    out=Z[k][:, :, PAD + t0:PAD + t0 + 512].rearrange("c p t -> p c t"), in_=stg_t[us][:]),
                   reads=r_stg[us], dma="stg%d" % us)
            sc.flush()
            if dbg:
                op("sp", lambda e: e.dma_start(out=dbg_z, in_=Z["s"]), dma="dbg")
                sc.flush()

        if upto < 2:
            return nc
        T2 = 2048
        scale_b = 128.0 ** -0.5
        with ExitStack() as ps:
            sb = lambda n, s, d: ps.enter_context(nc.sbuf_tensor(n, s, d))
            pp = lambda n, s, d: ps.enter_context(nc.psum_tensor(n, s, d))
            qb_t = [[sb("q%d_%d" % (i, g), [128, T2], BF16) for g in range(3)] for i in range(2)]
            kb_t = [[sb("k%d_%d" % (i, g), [128, T2 + 128 * DILS[g]], BF16) for g in range(3)] for i in range(2)]
            vb_t = [[sb("v%d_%d" % (i, g), [128, T2 + 128 * DILS[g]], BF16) for g in range(3)] for i in range(2)]
            gb_t = [sb("gb%d" % i, [128, T2], BF16) for i in range(2)]
            accO = [sb("accO%d" % i, [128, T2], F32) for i in range(2)]
            accD = [sb("accD%d" % i, [128, T2], F32) for i in range(2)]
            PT_t = [sb("PT%d" % i, [128, 256], BF16) for i in range(3)]
            Vm_t = [sb("Vm%d" % i, [128, 128], BF16) for i in range(3)]
            yst = [sb("yst%d" % i, [128, T2], BF16) for i in range(2)]
            pS = [pp("pS%d" % i, [128, 512], F32) for i in range(2)]
            pO = [pp("pO%d" % i, [128, 512], F32) for i in range(2)]
            pD = [pp("pD%d" % i, [128, 512], F32) for i in range(2)]
            pV_ = [pp("pV%d" % i, [128, 1024], BF16) for i in range(2)]
            r_q = [sc.res(3) for _ in range(2)]
            r_k = [sc.res(3) for _ in range(2)]
            r_v = [sc.res(3) for _ in range(2)]
            r_gb, r_aO, r_aD, r_PT, r_Vm, r_yst = sc.res(2), sc.res(2), sc.res(2), sc.res(3), sc.res(3), sc.res(2)
            r_pS, r_pO, r_pD, r_pV = sc.res(2), sc.res(2), sc.res(2), sc.res(2)
            masks = cm[:, 4:8, :]

            def strided(t, start, n, r):
                if r == 1:
                    return t[:, start:start + n]
                return t[:, start:start + (n - 1) * r + 1:r]

            units = [(k, S, u) for k, S in seqs for u in range(S // T2)]
            tcnt = [0]
            sucnt = [0]
            pend_norm = []
            tokoff = {"p": 0, "s": SP}
            for ui, (k, S, u) in enumerate(units):
                us = ui % 2
                t0 = u * T2
                for g in range(3):
                    r = DILS[g]
                    op("sp", lambda e, g=g, us=us, k=k, t0=t0: e.dma_start(out=qb_t[us][g][:], in_=Z[k][C_BQ + g, :, PAD + t0:PAD + t0 + T2]),
                       writes=[r_q[us][g]], dma="q%d_%d" % (us, g))
                    op("sp", lambda e, g=g, r=r, us=us, k=k, t0=t0: e.dma_start(out=kb_t[us][g][:], in_=Z[k][C_BK + g, :, PAD + t0 - 64 * r:PAD + t0 + T2 + 64 * r]),
                       writes=[r_k[us][g]], dma="k%d_%d" % (us, g))
                    op("sp", lambda e, g=g, r=r, us=us, k=k, t0=t0: e.dma_start(out=vb_t[us][g][:], in_=Z[k][C_BV + g, :, PAD + t0 - 64 * r:PAD + t0 + T2 + 64 * r]),
                       writes=[r_v[us][g]], dma="v%d_%d" % (us, g))
                op("sp", lambda e, us=us, k=k, t0=t0: e.dma_start(out=gb_t[us][:], in_=Z[k][C_BG, :, PAD + t0:PAD + t0 + T2]), writes=[r_gb[us]], dma="gb%d" % us)
                tasks = []
                for g in range(3):
                    r = DILS[g]
                    Lu = T2 // r
                    nq = min(4, Lu // 128)
                    nsub = Lu // (nq * 128)
                    for rho in range(r):
                        for su in range(nsub):
                            l0 = su * nq * 128
                            first = (u == 0 and l0 == 0)
                            last = (u == S // T2 - 1 and l0 + nq * 128 == Lu)
                            sid = sucnt[0]
                            sucnt[0] += 1
                            for m in range(nq + 1):
                                tasks.append(dict(g=g, r=r, rho=rho, l0=l0, m=m, nq=nq, first=first, last=last, sid=sid))

                def front(t):
                    i = tcnt[0]
                    tcnt[0] += 1
                    t["i"] = i
                    g, r, rho, l0, m, nq = t["g"], t["r"], t["rho"], t["l0"], t["m"], t["nq"]
                    s2, s3, vh = i % 2, i % 3, i % 2
                    kcol = rho + r * (l0 + 128 * m)
                    ktile = strided(kb_t[us][g], kcol, 128, r)
                    vtile = strided(vb_t[us][g], kcol, 128, r)
                    op("pe", lambda e: e.transpose(pV_[vh][:, 0:128], vtile, ident), reads=[r_v[us][g], r_cm], writes=[r_pV[vh]])
                    if i % 2 == 0:
                        op("dve", lambda e: e.tensor_copy(out=Vm_t[s3][:], in_=pV_[vh][:, 0:128]), reads=[r_pV[vh]], writes=[r_Vm[s3]])
                    else:
                        op("act", lambda e: e.activation(out=Vm_t[s3][:], in_=pV_[vh][:, 0:128], func=AF.Copy), reads=[r_pV[vh]], writes=[r_Vm[s3]])
                    jlo = m - 1 if m >= 1 else None
                    jhi = m if m < nq else None
                    if jlo is not None and jhi is not None:
                        qap = strided(qb_t[us][g], rho + r * (l0 + 128 * jlo), 256, r)
                        mk_ = masks[:, 0:2, :].rearrange("p a b -> p (a b)")
                        N = 256
                    elif jhi is not None:
                        qap = strided(qb_t[us][g], rho + r * (l0 + 128 * jhi), 128, r)
                        mk_ = masks[:, 3, :] if t["first"] else masks[:, 1, :]
                        N = 128
                    else:
                        qap = strided(qb_t[us][g], rho + r * (l0 + 128 * jlo), 128, r)
                        mk_ = masks[:, 2, :] if t["last"] else masks[:, 0, :]
                        N = 128
                    t["N"], t["jlo"], t["jhi"] = N, jlo, jhi
                    op("pe", lambda e: e.matmul(pS[s2][:, 0:N], lhsT=ktile, rhs=qap, start=True, stop=False),
                       reads=[r_k[us][g], r_q[us][g]], writes=[r_pS[s2]])
                    op("pe", lambda e: e.matmul(pS[s2][:, 0:N], lhsT=ident, rhs=mk_, start=False, stop=True),
                       reads=[r_cm], writes=[r_pS[s2]])
                    op("act", lambda e: e.activation(out=PT_t[s3][:, 0:N], in_=pS[s2][:, 0:N], func=AF.Exp, scale=scale_b),
                       reads=[r_pS[s2]], writes=[r_PT[s3]])

                def back(t):
                    i = t["i"]
                    g, r, rho, l0, m, nq = t["g"], t["r"], t["rho"], t["l0"], t["m"], t["nq"]
                    s3 = i % 3
                    so = t["sid"] % 2
                    halves = []
                    if t["jlo"] is not None:
                        halves.append((t["jlo"], 0, False, True))
                    if t["jhi"] is not None:
                        halves.append((t["jhi"], t["N"] - 128, True, False))
                    for (j, c0, st_, sp_) in halves:
                        op("pe", lambda e, j=j, c0=c0, st_=st_, sp_=sp_: e.matmul(pO[so][:, j * 128:(j + 1) * 128], lhsT=Vm_t[s3][:],
                                                                                    rhs=PT_t[s3][:, c0:c0 + 128], start=st_, stop=sp_),
                           reads=[r_Vm[s3], r_PT[s3]], writes=[r_pO[so]])
                        op("pe", lambda e, j=j, c0=c0, st_=st_, sp_=sp_: e.matmul(pD[so][:, j * 128:(j + 1) * 128], lhsT=ones_b,
                                                                                    rhs=PT_t[s3][:, c0:c0 + 128], start=st_, stop=sp_),
                           reads=[r_cm, r_PT[s3]], writes=[r_pD[so]])
                    if m == nq:
                        n = nq * 128
                        dO = strided(accO[us], rho + r * l0, n, r)
                        dD = strided(accD[us], rho + r * l0, n, r)
                        if g == 0:
                            op("dve", lambda e: e.tensor_copy(out=dO, in_=pO[so][:, 0:n]), reads=[r_pO[so]], writes=[r_aO[us]])
                            op("act", lambda e: e.activation(out=dD, in_=pD[so][:, 0:n], func=AF.Copy), reads=[r_pD[so]], writes=[r_aD[us]])
                        else:
                            op("dve", lambda e: e.tensor_tensor(out=dO, in0=dO, in1=pO[so][:, 0:n], op=ALU.add),
                               reads=[r_pO[so], r_aO[us]], writes=[r_aO[us]])
                            op("dve", lambda e: e.tensor_tensor(out=dD, in0=dD, in1=pD[so][:, 0:n], op=ALU.add),
                               reads=[r_pD[so], r_aD[us]], writes=[r_aD[us]])

                front(tasks[0])
                for i_ in range(len(tasks)):
                    if i_ + 1 < len(tasks):
                        front(tasks[i_ + 1])
                    back(tasks[i_])
                    if i_ == 24 and pend_norm:
                        pend_norm.pop()()
                if pend_norm:
                    pend_norm.pop()()
                c0_ = tokoff[k] + t0

                def norm(us=us, c0_=c0_):
                    op("act", lambda e: e.activation(out=accD[us][:], in_=accD[us][:], func=AF.Ln), reads=[r_aD[us]], writes=[r_aD[us]])
                    op("act", lambda e: e.activation(out=accD[us][:], in_=accD[us][:], func=AF.Exp, scale=-1.0), reads=[r_aD[us]], writes=[r_aD[us]])
                    op("pool", lambda e: e.tensor_tensor(out=accO[us][:], in0=accO[us][:], in1=accD[us][:], op=ALU.mult),
                       reads=[r_aD[us], r_aO[us]], writes=[r_aO[us]])
                    op("dve", lambda e: e.tensor_tensor(out=yst[us][:], in0=accO[us][:], in1=gb_t[us][:], op=ALU.mult),
                       reads=[r_aO[us], r_gb[us]], writes=[r_yst[us]])
                    op("pool", lambda e: e.dma_start(out=ysend.ap()[(c0_ // CW) * 256:(c0_ // CW + T2 // CW) * 256, :].rearrange("(j r) t -> r j t", r=256)[128:256, :, :],
                                                     in_=yst[us][:].rearrange("p (j t) -> p j t", t=CW)),
                       reads=[r_yst[us]], dma="yst%d" % us)

                pend_norm.append(norm)
            if pend_norm:
                pend_norm.pop()()
            sc.flush()

        if upto < 3:
            return nc
        sb = lambda n, s, d: es.enter_context(nc.sbuf_tensor(n, s, d))
        Wo = sb("Wo", [128, 8, D], BF16)
        Wg = sb("Wg", [128, 16, D], BF16)
        pn_t = sb("pn_t", [128, 16], F32)
        wst = [sb("wst%d" % i, [128, D], F32) for i in range(1)]
        r_Wo, r_Wg, r_Wp, r_fn, r_pn = sc.res(), sc.res(), sc.res(), sc.res(), sc.res()
        r_wst = sc.res(1)
        with ExitStack() as ps:
            sb = lambda n, s, d: ps.enter_context(nc.sbuf_tensor(n, s, d))
            pp = lambda n, s, d: ps.enter_context(nc.psum_tensor(n, s, d))
            Kt = sb("Kt", [128, SMAX], BF16)
            Vt = sb("Vt", [128, SMAX // 128, 130], BF16)
            vld = [sb("vld0", [128, 2048], BF16)] * 2
            Qb = [sb("Qb%d" % i, [128, 512], BF16) for i in range(2)]
            Gb = [sb("Gb%d" % i, [128, 512], BF16) for i in range(2)]
            PT3 = [sb("PT3_%d" % i, [128, 1024], BF16) for i in range(3)]
            Osb = [sb("Osb0", [128, 3, 512], F32)] * 2
            fin = [sb("fin%d" % i, [128, 16], F32) for i in range(2)]
            o_t = [sb("o_t%d" % i, [128, 4, 128], F32) for i in range(2)]
            sq_t = sb("sq_t", [128, 4, 128], F32)
            on_t = [sb("on_t%d" % i, [128, 4, 128], BF16) for i in range(2)]
            yst3 = [sb("yst3_%d" % i, [128, 512], BF16) for i in range(2)]
            sl_t = sb("sl_t", [128, 128], F32)
            pS3 = [pp("pS3_%d" % i, [128, 1024], F32) for i in range(2)]
            pO3 = pp("pO3", [128, 3, 512], F32)
            pY = pp("pY", [128, 512], BF16)
            r_Kt, r_Vt, r_sl = sc.res(), sc.res(), sc.res()
            r_vld, r_Qb, r_Gb, r_PT3, r_Osb, r_fin, r_o, r_on, r_yst3 = ([sc.res()] * 2, sc.res(2), sc.res(2), sc.res(3), [sc.res()] * 2,
                                                                          sc.res(2), sc.res(2), sc.res(2), sc.res(2))
            r_sq = sc.res()
            r_pS3, r_pO3, r_pY = sc.res(2), sc.res(), sc.res()

            def oacc(c, j):
                idx = c * 4 + j
                return idx // 3, (idx % 3) * 129

            op("sp", lambda e: e.dma_start(out=sl_t[:], in_=subln.rearrange("a b -> (a b)").partition_broadcast(128)), writes=[r_sl], dma="sl")
            op("dve", lambda e: e.tensor_scalar(out=sl_t[:], in0=sl_t[:], scalar1=0.8, scalar2=None, op0=ALU.mult), reads=[r_sl], writes=[r_sl])
            def load_tail_weights():
                op("sp", lambda e: e.dma_start(out=pn_t[:], in_=pnorm), writes=[r_pn], dma="c5")
                wl = 0
                for kc in range(8):
                    hh, ab = kc // 2, kc % 2
                    r0 = ab * 512 + hh * 128
                    sl = 0
                    wl += 1
                    op("sp", lambda e, r0=r0, sl=sl: e.dma_start(out=wst[sl][:], in_=wout[r0:r0 + 128, :]), writes=[r_wst[sl]], dma="wst%d" % sl)
                    op("dve", lambda e, kc=kc, sl=sl: e.tensor_copy(out=Wo[:, kc, :], in_=wst[sl][:]), reads=[r_wst[sl]], writes=[r_Wo])
                for kc in range(16):
                    sl = 0
                    wl += 1
                    op("sp", lambda e, kc=kc, sl=sl: e.dma_start(out=wst[sl][:], in_=wgate[kc * 128:(kc + 1) * 128, :]), writes=[r_wst[sl]], dma="wst%d" % sl)
                    op("dve", lambda e, kc=kc, sl=sl: e.tensor_scalar(out=Wg[:, kc, :], in0=wst[sl][:], scalar1=pn_t[:, kc:kc + 1], scalar2=None, op0=ALU.mult),
                       reads=[r_wst[sl], r_pn], writes=[r_Wg])
            vcnt = 0
            qcnt = 0
            pend_fin = []
            r_ys = sc.res(NCK)
            r_cc = sc.res()
            for k, S in seqs:
                nkt = S // 128
                op("sp", lambda e, k=k, S=S: e.dma_start(out=Kt[:, 0:S], in_=Z[k][C_AK, :, PAD:PAD + S]), writes=[r_Kt], dma="kt")
                op("dve", lambda e: e.memset(Vt[:, :, 128:130], 1.0), writes=[r_Vt])
                for vb in range(S // 2048):
                    vs = vcnt % 2
                    vcnt += 1
                    op("sp", lambda e, k=k, vb=vb, vs=vs: e.dma_start(out=vld[vs][:], in_=Z[k][C_AV, :, PAD + vb * 2048:PAD + (vb + 1) * 2048]),
                       writes=[r_vld[vs]], dma="vld0")
                    for q4 in range(4):
                        for jj in range(4):
                            tt = q4 * 4 + jj
                            op("pe", lambda e, vs=vs, tt=tt, jj=jj: e.transpose(pY[:, jj * 128:(jj + 1) * 128], vld[vs][:, tt * 128:(tt + 1) * 128], ident),
                               reads=[r_vld[vs], r_cm], writes=[r_pY])
                        kt0 = vb * 16 + q4 * 4
                        op("dve", lambda e, kt0=kt0: e.tensor_copy(out=Vt[:, kt0:kt0 + 4, 0:128], in_=pY[:].rearrange("p (a b) -> p a b", b=128)),
                           reads=[r_pY], writes=[r_Vt])
                for qb in range(S // 512):
                    qs = qcnt % 2
                    qcnt += 1
                    q0 = qb * 512
                    op("sp", lambda e, k=k, q0=q0, qs=qs: e.dma_start(out=Qb[qs][:], in_=Z[k][C_AQ, :, PAD + q0:PAD + q0 + 512]),
                       writes=[r_Qb[qs]], dma="qb%d" % qs)
                    op("sp", lambda e, k=k, q0=q0, qs=qs: e.dma_start(out=Gb[qs][:], in_=Z[k][C_AG, :, PAD + q0:PAD + q0 + 512]),
                       writes=[r_Gb[qs]], dma="qb%d" % qs)

                    def front3(kt, qs=qs):
                        s2, s3 = kt % 2, kt % 3
                        for c in range(2):
                            op("pe", lambda e, c=c: e.matmul(pS3[s2][:, c * 512:(c + 1) * 512], lhsT=Kt[c * 64:(c + 1) * 64, kt * 128:(kt + 1) * 128],
                                                             rhs=Qb[qs][c * 64:(c + 1) * 64, :], start=True, stop=True),
                               reads=[r_Kt, r_Qb[qs]], writes=[r_pS3[s2]])
                        op("act", lambda e: e.activation(out=PT3[s3][:], in_=pS3[s2][:], func=AF.Exp, scale=0.125),
                           reads=[r_pS3[s2]], writes=[r_PT3[s3]])

                    def back3(kt, nkt=nkt):
                        s3 = kt % 3
                        for c in range(2):
                            for j in range(4):
                                bk, off = oacc(c, j)
                                op("pe", lambda e, c=c, j=j, bk=bk, off=off: e.matmul(pO3[:, bk, off:off + 129],
                                                                                       lhsT=PT3[s3][:, c * 512 + j * 128:c * 512 + (j + 1) * 128],
                                                                                       rhs=Vt[:, kt, 0:129], start=(kt == 0 and off == 0), stop=(kt == nkt - 1),
                                                                                       skip_group_check=True),
                                   reads=[r_PT3[s3], r_Vt], writes=[r_pO3])

                    front3(0)
                    front3(1)
                    for kt in range(nkt):
                        if kt + 2 < nkt:
                            front3(kt + 2)
                        back3(kt)
                        if kt == 1 and pend_fin and len(pend_fin[0]) == 2:
                            pend_fin[0].pop(0)()
                        if kt == 14 and pend_fin:
                            for f_ in pend_fin.pop():
                                f_()
                    if pend_fin:
                        for f_ in pend_fin.pop():
                            f_()
                    if qcnt == 1:
                        load_tail_weights()
                    fs = qs
                    O = Osb[fs]
                    op("dve", lambda e, O=O: e.tensor_copy(out=O[:, 0:2, 0:387], in_=pO3[:, 0:2, 0:387]), reads=[r_pO3], writes=[r_Osb[fs]])
                    op("dve", lambda e, O=O: e.tensor_copy(out=O[:, 2, 0:258], in_=pO3[:, 2, 0:258]), reads=[r_pO3], writes=[r_Osb[fs]])
                    c0_ = tokoff[k] + q0

                    def finB(fs=fs, O=O):
                        f = fin[fs]
                        for c in range(2):
                            for j in range(4):
                                bk, off = oacc(c, j)
                                op("dve", lambda e, c=c, j=j, bk=bk, off=off: e.reciprocal(out=f[:, c * 4 + j:c * 4 + j + 1], in_=O[:, bk, off + 128:off + 129]),
                                   reads=[r_Osb[fs]], writes=[r_fin[fs]])
                        op("dve", lambda e: e.tensor_scalar(out=f[:, 4:8], in0=f[:, 4:8], scalar1=lam_t[:, 1:2], scalar2=None, op0=ALU.mult),
                           reads=[r_fin[fs], r_lam], writes=[r_fin[fs]])
                        ot = o_t[fs]
                        for j in range(4):
                            b0, o0 = oacc(0, j)
                            b1, o1 = oacc(1, j)
                            op("dve", lambda e, j=j, b0=b0, o0=o0: e.tensor_scalar(out=ot[:, j, :], in0=O[:, b0, o0:o0 + 128], scalar1=f[:, j:j + 1],
                                                                                 scalar2=None, op0=ALU.mult),
                               reads=[r_Osb[fs], r_fin[fs]], writes=[r_o[fs]])
                            op("dve", lambda e, j=j, b1=b1, o1=o1: e.scalar_tensor_tensor(out=ot[:, j, :], in0=O[:, b1, o1:o1 + 128], scalar=f[:, 4 + j:5 + j],
                                                                                        in1=ot[:, j, :], op0=ALU.mult, op1=ALU.add),
                               reads=[r_Osb[fs], r_fin[fs], r_o[fs]], writes=[r_o[fs]])
                        op("pool", lambda e: e.tensor_tensor(out=sq_t[:], in0=ot[:], in1=ot[:], op=ALU.mult), reads=[r_o[fs]], writes=[r_sq])
                        op("dve", lambda e: e.reduce_sum(out=f[:, 8:12], in_=sq_t[:], axis=AX.X), reads=[r_sq], writes=[r_fin[fs]])
                        op("dve", lambda e: e.tensor_scalar(out=f[:, 8:12], in0=f[:, 8:12], scalar1=1.0 / 128, scalar2=EPS, op0=ALU.mult, op1=ALU.add),
                           reads=[r_fin[fs]], writes=[r_fin[fs]])

                    def finC(fs=fs, qs=qs, c0_=c0_):
                        f = fin[fs]
                        ot = o_t[fs]
                        ont = on_t[fs]
                        op("act", lambda e: e.activation(out=f[:, 8:12], in_=f[:, 8:12], func=AF.Ln), reads=[r_fin[fs]], writes=[r_fin[fs]])
                        op("act", lambda e: e.activation(out=f[:, 12:16], in_=f[:, 8:12], func=AF.Exp, scale=-0.5), reads=[r_fin[fs]], writes=[r_fin[fs]])
                        for j in range(4):
                            op("dve", lambda e, j=j: e.scalar_tensor_tensor(out=ont[:, j, :], in0=ot[:, j, :], scalar=f[:, 12 + j:13 + j], in1=sl_t[:],
                                                                          op0=ALU.mult, op1=ALU.mult),
                               reads=[r_o[fs], r_fin[fs], r_sl], writes=[r_on[fs]])
                        for j in range(4):
                            op("pe", lambda e, j=j: e.transpose(pY[:, j * 128:(j + 1) * 128], ont[:, j, :], ident),
                               reads=[r_on[fs], r_cm], writes=[r_pY])
                        op("dve", lambda e: e.tensor_tensor(out=yst3[fs][:], in0=pY[:], in1=Gb[qs][:], op=ALU.mult),
                           reads=[r_pY, r_Gb[qs]], writes=[r_yst3[fs]])
                        jc = c0_ // CW
                        op("pool", lambda e: e.dma_start(out=ysend.ap()[(c0_ // CW) * 256:(c0_ // CW) * 256 + 128, c0_ % CW:c0_ % CW + 512], in_=yst3[fs][:]),
                           reads=[r_yst3[fs]], writes=[r_ys[jc]], dma="y3_%d" % fs)
                        if (c0_ + 512) % CW == 0:
                            op("pool", lambda e, j=jc: e.collective_compute("AllGather", ALU.bypass, replica_groups=[[0, 1, 2, 3], [4, 5, 6, 7]],
                                                                            ins=[ysend.ap()[j * 256:(j + 1) * 256, :].opt()],
                                                                            outs=[yall.ap()[j * 1024:(j + 1) * 1024, :].opt()]),
                               reads=[r_ys[jc]], writes=[r_cc], dma="cc", inc=1)

                    pend_fin.append([finB, finC])
            if pend_fin:
                for f_ in pend_fin.pop():
                    f_()
            sc.flush()

        if upto < 4:
            return nc
        if dbg:
            op("sp", lambda e: e.dma_start(out=dbg_y, in_=ysend.ap()), dma="dbg")
            sc.flush()

        if upto < 5:
            return nc
        with ExitStack() as ps:
            sb = lambda n, s, d: ps.enter_context(nc.sbuf_tensor(n, s, d))
            pp = lambda n, s, d: ps.enter_context(nc.psum_tensor(n, s, d))
            Wp = sb("Wp", [128, 2, D], BF16)
            fn_t = sb("fn_t", [128, D], F32)
            x5 = [wst[0]] + [sb("x5_%d" % i, [128, D], F32) for i in range(1, 3)]
            h1 = [sb("h1_%d" % i, [128, D], F32) for i in range(2)]
            hb = [sb("hb%d" % i, [128, D], BF16) for i in range(2)]
            hT = [sb("hT%d" % i, [128, 16, 128], BF16) for i in range(2)]
            yT = [sb("yT%d" % i, [128, 8, 512], BF16) for i in range(2)]
            pf = [sb("pf%d" % i, [128, 2, 512], F32) for i in range(2)]
            pb = [sb("pb%d" % i, [128, 2, 512], BF16) for i in range(2)]
            e5 = [sb("e5_%d" % i, [128, 512], F32) for i in range(2)]
            st5 = [sb("st5_%d" % i, [128, 8], F32) for i in range(2)]
            st5b = [sb("st5b_%d" % i, [128, 8], F32) for i in range(2)]
            r_st5b = sc.res(2)
            junk5 = sb("junk5", [128, D], BF16)
            pH = [pp("pH%d" % i, [128, 512], F32) for i in range(2)]
            pT5 = pp("pT5", [128, D], BF16)
            pG = [pp("pG%d" % i, [128, 512], F32) for i in range(2)]
            pP = [pp("pP%d" % i, [128, 512], F32) for i in range(2)]
            r_x5, r_h1, r_hb, r_yT, r_pf, r_pb, r_e5, r_st5 = (sc.res(3), sc.res(2), sc.res(2), sc.res(2), sc.res(2), sc.res(2), sc.res(2), sc.res(2))
            r_hT = [sc.res(2) for _ in range(2)]
            r_pH, r_pT5, r_pG, r_pP = sc.res(2), sc.res(), sc.res(2), sc.res(2)

            op("sp", lambda e: e.dma_start(out=fn_t[:], in_=fnorm.rearrange("a b -> (a b)").partition_broadcast(128)), writes=[r_fn], dma="c5")
            wl = 0
            for kc in range(2):
                sl = wl % 2
                wl += 1
                op("sp", lambda e, kc=kc, sl=sl: e.dma_start(out=x5[1 + sl][:], in_=wproj[kc * 128:(kc + 1) * 128, :]), writes=[r_x5[1 + sl]], dma="x5_%d" % (1 + sl))
                op("dve", lambda e, kc=kc, sl=sl: e.tensor_copy(out=Wp[:, kc, :], in_=x5[1 + sl][:]), reads=[r_x5[1 + sl]], writes=[r_Wp])

            rk = {}

            def rank_of(e):
                if "r" not in rk:
                    rk["r"] = e.partition_id() % 4
                return rk["r"]

            def ycols(e, nq4, cst):
                assert nq4 % CW == 0
                ck = rank_of(e) * ((nq4 // CW) * 1024) + (cst // CW) * 1024
                off = cst % CW
                return yall.ap()[bass.ds(ck, 1024), off:off + 512].rearrange("(c p) t -> p c t", p=128)

            tails = [("p", SP, out_p, 0), ("s", SS, out_s, SP // 4)]
            cnt = dict(g=0, h=0)
            tiles = []
            blks = []
            for k, S, outp, poff in tails:
                nq4 = S // 4
                ybase = 0 if k == "p" else SP
                for b4 in range(nq4 // 512):
                    lc0 = b4 * 512
                    bidx = len(blks)
                    blks.append(dict(ys=bidx % 2, lc0=lc0, nq4=nq4, ybase=ybase, poff=poff))
                    for ti in range(4):
                        t = len(tiles)
                        tiles.append(dict(k=k, outp=outp, l0=lc0 + ti * 128, ys=bidx % 2, ti=ti, s2=t % 2, xsl=t % 3, bidx=bidx))

            def blockload(bidx):
                if bidx >= len(blks):
                    return
                B_ = blks[bidx]
                ys, lc0, nq4, ybase, poff = B_["ys"], B_["lc0"], B_["nq4"], B_["ybase"], B_["poff"]
                need_cc = (SP // CW) if ybase == 0 else NCK

                def ld_y(e):
                    return e.dma_start(out=yT[ys][:], in_=ycols(e, nq4, ybase + lc0))

                op("sp", ld_y, writes=[r_yT[ys]], dma="yT%d" % ys)
                op("sp", lambda e: e.dma_start(out=pf[ys][:], in_=pT[:, poff + lc0:poff + lc0 + 512].rearrange("(c p) t -> p c t", p=128)),
                   writes=[r_pf[ys]], dma="yT%d" % ys)
                op("pool", lambda e: e.tensor_copy(out=pb[ys][:], in_=pf[ys][:]), reads=[r_pf[ys]], writes=[r_pb[ys]])

            def S1(t):
                T_ = tiles[t]
                k, l0, ys, ti, s2, xsl = T_["k"], T_["l0"], T_["ys"], T_["ti"], T_["s2"], T_["xsl"]
                if ti == 1:
                    blockload(T_["bidx"] + 1)
                op("sp", lambda e: e.dma_start(out=x5[xsl][:], in_=xq_in[k][l0:l0 + 128, :]), writes=[r_x5[xsl]], dma="x5_%d" % xsl)
                for cc in range(4):
                    hs = cnt["h"] % 2
                    cnt["h"] += 1
                    for kc in range(8):
                        op("pe", lambda e, kc=kc, cc=cc, hs=hs: e.matmul(pH[hs][:], lhsT=yT[ys][:, kc, ti * 128:(ti + 1) * 128],
                                                                        rhs=Wo[:, kc, cc * 512:(cc + 1) * 512], start=(kc == 0), stop=(kc == 7)),
                           reads=[r_yT[ys], r_Wo], writes=[r_pH[hs]])
                    op("dve", lambda e, cc=cc, hs=hs: e.tensor_tensor(out=h1[s2][:, cc * 512:(cc + 1) * 512], in0=pH[hs][:],
                                                                      in1=x5[xsl][:, cc * 512:(cc + 1) * 512], op=ALU.add),
                       reads=[r_pH[hs], r_x5[xsl]], writes=[r_h1[s2]])

            def S2(t):
                s2 = tiles[t]["s2"]
                st = st5[s2]
                op("act", lambda e: e.activation(out=junk5[:], in_=h1[s2][:], func=AF.Square, accum_out=st[:, 0:1]),
                   reads=[r_h1[s2]], writes=[r_st5[s2]])
                op("dve", lambda e: e.tensor_scalar(out=st[:, 1:2], in0=st[:, 0:1], scalar1=1.0 / D, scalar2=EPS, op0=ALU.mult, op1=ALU.add),
                   reads=[r_st5[s2]], writes=[r_st5[s2]])
                op("act", lambda e: e.activation(out=st[:, 2:3], in_=st[:, 1:2], func=AF.Ln), reads=[r_st5[s2]], writes=[r_st5[s2]])
                op("act", lambda e: e.activation(out=st[:, 3:4], in_=st[:, 2:3], func=AF.Exp, scale=-0.5), reads=[r_st5[s2]], writes=[r_st5[s2]])
                op("act", lambda e: e.activation(out=hb[s2][:], in_=h1[s2][:], func=AF.Copy, scale=st[:, 3:4]),
                   reads=[r_h1[s2], r_st5[s2]], writes=[r_hb[s2]])

            def S3(t):
                s2 = tiles[t]["s2"]
                for kc in range(16):
                    op("pe", lambda e, kc=kc: e.transpose(pT5[:, kc * 128:(kc + 1) * 128], hb[s2][:, kc * 128:(kc + 1) * 128], ident),
                       reads=[r_hb[s2], r_cm], writes=[r_pT5])
                src5 = pT5[:].rearrange("p (a b) -> p a b", b=128)
                op("dve", lambda e: e.tensor_copy(out=hT[s2][:, 0:8, :], in_=src5[:, 0:8, :]), reads=[r_pT5], writes=[r_hT[s2][0]])
                op("act", lambda e: e.activation(out=hT[s2][:, 8:16, :], in_=src5[:, 8:16, :], func=AF.Copy), reads=[r_pT5], writes=[r_hT[s2][1]])

            def S4(t):
                T_ = tiles[t]
                ys, ti, s2, xsl = T_["ys"], T_["ti"], T_["s2"], T_["xsl"]
                for cc in range(4):
                    gs = cnt["g"] % 2
                    cnt["g"] += 1
                    cs = slice(cc * 512, (cc + 1) * 512)
                    for kc in range(16):
                        op("pe", lambda e, kc=kc, cs=cs, gs=gs: e.matmul(pG[gs][:], lhsT=hT[s2][:, kc, :], rhs=Wg[:, kc, cs],
                                                                        start=(kc == 0), stop=(kc == 15)),
                           reads=r_hT[s2] + [r_Wg], writes=[r_pG[gs]])
                    for kc in range(2):
                        op("pe", lambda e, kc=kc, cs=cs, gs=gs: e.matmul(pP[gs][:], lhsT=pb[ys][:, kc, ti * 128:(ti + 1) * 128], rhs=Wp[:, kc, cs],
                                                                        start=(kc == 0), stop=(kc == 1)),
                           reads=[r_pb[ys], r_Wp], writes=[r_pP[gs]])
                    op("act", lambda e, gs=gs: e.activation(out=e5[gs][:], in_=pG[gs][:], func=AF.Exp, scale=-1.0), reads=[r_pG[gs]], writes=[r_e5[gs]])
                    op("act", lambda e, gs=gs: e.activation(out=e5[gs][:], in_=e5[gs][:], func=AF.Ln, bias=one_c[:, 0:1]),
                       reads=[r_e5[gs], r_one], writes=[r_e5[gs]])
                    op("act", lambda e, gs=gs: e.activation(out=e5[gs][:], in_=e5[gs][:], func=AF.Exp, scale=-1.0),
                       reads=[r_e5[gs]], writes=[r_e5[gs]])
                    op("dve", lambda e, gs=gs: e.tensor_tensor(out=e5[gs][:], in0=e5[gs][:], in1=pP[gs][:], op=ALU.mult),
                       reads=[r_e5[gs], r_pP[gs]], writes=[r_e5[gs]])
                    op("pool", lambda e, gs=gs, cs=cs: e.tensor_tensor(out=x5[xsl][:, cs], in0=h1[s2][:, cs], in1=e5[gs][:], op=ALU.add),
                       reads=[r_e5[gs], r_h1[s2], r_x5[xsl]], writes=[r_x5[xsl]])

            def S5(t):
                T_ = tiles[t]
                s2, xsl, outp, l0 = T_["s2"], T_["xsl"], T_["outp"], T_["l0"]
                st = st5b[s2]
                op("act", lambda e: e.activation(out=junk5[:], in_=x5[xsl][:], func=AF.Square, accum_out=st[:, 4:5]),
                   reads=[r_x5[xsl]], writes=[r_st5b[s2]])
                op("dve", lambda e: e.tensor_scalar(out=st[:, 5:6], in0=st[:, 4:5], scalar1=1.0 / D, scalar2=EPS, op0=ALU.mult, op1=ALU.add),
                   reads=[r_st5b[s2]], writes=[r_st5b[s2]])
                op("act", lambda e: e.activation(out=st[:, 6:7], in_=st[:, 5:6], func=AF.Ln), reads=[r_st5b[s2]], writes=[r_st5b[s2]])
                op("act", lambda e: e.activation(out=st[:, 7:8], in_=st[:, 6:7], func=AF.Exp, scale=-0.5), reads=[r_st5b[s2]], writes=[r_st5b[s2]])
                op("dve", lambda e: e.scalar_tensor_tensor(out=x5[xsl][:], in0=x5[xsl][:], scalar=st[:, 7:8], in1=fn_t[:],
                                                           op0=ALU.mult, op1=ALU.mult),
                   reads=[r_st5b[s2], r_fn, r_x5[xsl]], writes=[r_x5[xsl]])
                op("pool", lambda e: e.dma_start(out=outp[l0:l0 + 128, :], in_=x5[xsl][:]),
                   reads=[r_x5[xsl]], writes=[r_x5[xsl]], dma="o5_%d" % xsl)

            NTL = len(tiles)
            blockload(0)
            S1(0)
            S2(0)
            for t in range(NTL):
                if t + 1 < NTL:
                    S1(t + 1)
                S3(t)
                if t >= 1:
                    S5(t - 1)
                if t + 1 < NTL:
                    S2(t + 1)
                S4(t)
            S5(NTL - 1)
            sc.flush(final=True)
    return nc


_CACHE = {}


def _prep_inputs(inputs, SP, SS):
    f = lambda a: np.ascontiguousarray(np.asarray(a, dtype=np.float32))
    xP, xS = f(inputs["x_prompt"]), f(inputs["x_sample"])
    pP, pS = f(inputs["p_prompt"])[0], f(inputs["p_sample"])[0]
    w_in = f(inputs["w_in"])[0]
    cm, tabs = _consts(max(SP, SS))
    to_pk = lambda v: np.ascontiguousarray(v.reshape(16, 128).T)
    lamv = np.stack([f(inputs[n])[0] for n in ("lam_q1", "lam_k1", "lam_q2", "lam_k2")], 0)
    common = dict(nmix=to_pk(f(inputs["norm_mix"])[0]), lamv=lamv, subln=f(inputs["subln"]).reshape(1, 128),
                  wout=f(inputs["w_out"])[0], pnorm=to_pk(f(inputs["ple_norm"])[0]), wgate=f(inputs["w_ple_gate"])[0],
                  wproj=f(inputs["w_ple_proj"])[0], fnorm=f(inputs["final_norm"]).reshape(1, D), cmat=cm, tabs=tabs)
    maps = []
    for c in range(8):
        b, h = c // 4, c % 4
        cols = []
        for base in (0, 512, 1024, 1536):
            cols.append(np.arange(base + h * 128, base + (h + 1) * 128))
        for base in (2048, 3584, 5120):
            for g in range(3):
                cols.append(np.arange(base + g * 512 + h * 128, base + g * 512 + (h + 1) * 128))
        cols.append(np.arange(6656 + h * 128, 6656 + (h + 1) * 128))
        cols = np.concatenate(cols)
        qp, qs = SP // 4, SS // 4
        pT = np.concatenate([pP[b, h * qp:(h + 1) * qp].T, pS[b, h * qs:(h + 1) * qs].T], 1)
        m = dict(common)
        m.update(xp=xP[b], xs=xS[b], xqp=xP[b, h * qp:(h + 1) * qp], xqs=xS[b, h * qs:(h + 1) * qs], wh=np.ascontiguousarray(w_in[:, cols]), pT=np.ascontiguousarray(pT))
        maps.append(m)
    return maps


def kernel(_dbg=False, _upto=5, **inputs):
    SP = inputs["x_prompt"].shape[1]
    SS = inputs["x_sample"].shape[1]
    key = (SP, SS, _dbg, _upto)
    if key not in _CACHE:
        _CACHE[key] = build(SP, SS, _dbg, _upto)
    nc = _CACHE[key]
    maps = _prep_inputs(inputs, SP, SS)
    res = run_bass_kernel_spmd(nc, maps, core_ids=list(range(8)))
    if _dbg:
        return res
    yp = np.zeros((2, SP, D), np.float32)
    ys = np.zeros((2, SS, D), np.float32)
    qp, qs = SP // 4, SS // 4
    for c in range(8):
        b, h = c // 4, c % 4
        yp[b, h * qp:(h + 1) * qp] = res.results[c]["out_p"]
        ys[b, h * qs:(h + 1) * qs] = res.results[c]["out_s"]
    return (yp, ys)
```

```python
import numpy as np
import ml_dtypes
from contextlib import ExitStack
import concourse.bass as bass
import concourse.mybir as mybir
from concourse.bass_utils import run_bass_kernel_spmd

F32 = mybir.dt.float32
BF16 = mybir.dt.bfloat16
AF = mybir.ActivationFunctionType
ALU = mybir.AluOpType
AX = mybir.AxisListType
bf16 = ml_dtypes.bfloat16

D = 2048
NCH = 14
PAD = 1024
EPS = 1e-6
DILS = (1, 4, 16)
C_AQ, C_AK, C_AV, C_AG = 0, 1, 2, 3
C_BQ, C_BK, C_BV, C_BG = 4, 7, 10, 13
ROPE_DIFF = (C_AQ, C_AK)
ROPE_DIL = (4, 5, 6, 7, 8, 9)
COPY_CH = (C_AV, 10, 11, 12)
GATE_CH = (C_AG, C_BG)
NEGM = -30000.0


class Res:
    __slots__ = ("lw", "rd")

    def __init__(self):
        self.lw = None
        self.rd = {}


class Op:
    __slots__ = ("eng", "fn", "deps", "dma", "need", "sem", "sigval", "inc", "dwaits", "done")


class Sched:
    ENGS = ("pe", "act", "dve", "pool", "sp")

    def __init__(self, nc, es):
        self.nc = nc
        self.es = es
        self.sem = {e: es.enter_context(nc.semaphore("s_" + e)) for e in self.ENGS}
        self.cnt = {e: 0 for e in self.ENGS}
        self.gsem = {}
        self.gcnt = {}
        self.ops = []
        self.waited = {e: {} for e in self.ENGS}
        self.nres = 0

    def res(self, n=None):
        if n is None:
            return Res()
        return [Res() for _ in range(n)]

    def _dep(self, o, p, kind):
        if p is o or p.done:
            return
        if p.dma is None and o.dma is None and p.eng == o.eng:
            if o.eng == "pe":
                return
        o.deps.add(p)

    def op(self, eng, fn, reads=(), writes=(), dma=None, inc=None):
        o = Op()
        o.eng, o.fn, o.dma, o.deps, o.need = eng, fn, dma, set(), False
        o.sem = o.sigval = None
        o.inc = inc
        o.done = False
        for r in reads:
            if r.lw is not None:
                self._dep(o, r.lw, "raw")
        for w in writes:
            if w.lw is not None:
                self._dep(o, w.lw, "waw")
            for rr in w.rd.values():
                self._dep(o, rr, "war")
        for w in writes:
            w.lw = o
            w.rd = {}
        for r in reads:
            r.rd[eng if dma is None else ("d", dma)] = o
        o.dwaits = {}
        for p in o.deps:
            if p.dma is not None:
                o.dwaits[p.dma] = self.gcnt[p.dma]
        if dma is not None:
            if dma not in self.gsem:
                self.gsem[dma] = self.es.enter_context(self.nc.semaphore("g_" + dma))
                self.gcnt[dma] = 0
            o.inc = 16 if inc is None else inc
            self.gcnt[dma] += o.inc
            o.sem, o.sigval = self.gsem[dma], self.gcnt[dma]
        self.ops.append(o)
        return o

    def flush(self, final=False, drain_cc=True):
        nc = self.nc
        ops = self.ops
        self.ops = []
        for o in ops:
            o.done = True
            for p in o.deps:
                p.need = True
        for o in ops:
            if o.dma is None and o.need:
                self.cnt[o.eng] += 1
                o.sem, o.sigval, o.inc = self.sem[o.eng], self.cnt[o.eng], 1
        per = {e: [] for e in self.ENGS}
        for o in ops:
            per[o.eng].append(o)
        gs = [(self.gsem[g], self.gcnt[g]) for g in self.gsem]

        def mk(ename):
            def body(e):
                wd = self.waited[ename]
                for o in per[ename]:
                    ws = {}
                    for p in o.deps:
                        if p.dma is None:
                            k = id(p.sem)
                            if k not in ws or ws[k][1] < p.sigval:
                                ws[k] = (p.sem, p.sigval)
                    for g, v in o.dwaits.items():
                        ws[id(self.gsem[g])] = (self.gsem[g], v)
                    for k, (s, v) in ws.items():
                        if wd.get(k, 0) < v:
                            e.wait_ge(s, v)
                            wd[k] = v
                    ins = o.fn(e)
                    if o.sigval is not None:
                        ins.then_inc(o.sem, o.inc)
                if ename in ("sp", "pool"):
                    for g_, (s, v) in zip(list(self.gsem), gs):
                        if (g_ == "cc") != (ename == "pool"):
                            continue
                        if g_ == "cc" and not drain_cc:
                            continue
                        if v > 0 and wd.get(id(s), 0) < v:
                            e.wait_ge(s, v)
                            wd[id(s)] = v
            return body

        with nc.Block() as block:
            block.tensor(mk("pe"))
            block.scalar(mk("act"))
            block.vector(mk("dve"))
            block.gpsimd(mk("pool"))
            block.sync(mk("sp"))


def _consts(smax):
    ident = np.eye(128, dtype=np.float32)
    rdil = np.zeros((128, 128), np.float32)
    for f in range(64):
        rdil[f + 64, f] = -1.0
        rdil[f, f + 64] = 1.0
    rdiff = np.zeros((128, 128), np.float32)
    for b0 in (0, 64):
        for f in range(32):
            rdiff[b0 + f + 32, b0 + f] = -1.0
            rdiff[b0 + f, b0 + f + 32] = 1.0
    kp = np.arange(128)[:, None]
    qf = np.arange(128)[None, :]
    lo = qf >= kp
    hi = qf <= kp
    masks = np.stack([lo, hi, lo & (kp < 64), hi & (kp >= 64)], 1)
    masks = np.where(masks, 0.0, NEGM).astype(np.float32)
    cm = np.concatenate([ident[:, None, :], rdil[:, None, :], rdiff[:, None, :],
                         np.ones((128, 1, 128), np.float32), masks], 1)
    pos = np.arange(smax, dtype=np.float32)
    inv_dil = (10000.0 ** (-np.arange(0, 128, 2, dtype=np.float32) / 128)).astype(np.float32)
    inv_dif = (10000.0 ** (-np.arange(0, 64, 2, dtype=np.float32) / 64)).astype(np.float32)
    a_dil = (pos[None, :] * inv_dil[np.arange(128) % 64][:, None]).astype(np.float32)
    a_dif = (pos[None, :] * inv_dif[(np.arange(128) % 64) % 32][:, None]).astype(np.float32)
    tabs = np.stack([np.cos(a_dif), np.sin(a_dif), np.cos(a_dil), np.sin(a_dil)], 1)
    return cm.astype(bf16), np.ascontiguousarray(tabs.astype(np.float32))


def build(SP, SS, dbg=False, upto=5):
    nc = bass.Bass("TRN2", target_bir_lowering=False)
    seqs = [("p", SP), ("s", SS)]
    SMAX = max(SP, SS)
    TT = SP // 4 + SS // 4
    TOT = SP + SS
    din = {}

    def inp(name, shape, dt=F32):
        din[name] = nc.dram_tensor(name, list(shape), dt, kind="ExternalInput").ap()
        return din[name]

    x_in = {"p": inp("xp", [SP, D]), "s": inp("xs", [SS, D])}
    xq_in = {"p": inp("xqp", [SP // 4, D]), "s": inp("xqs", [SS // 4, D])}
    wh = inp("wh", [D, NCH * 128])
    nmix = inp("nmix", [128, 16])
    lamv = inp("lamv", [4, 64])
    subln = inp("subln", [1, 128])
    wout = inp("wout", [1024, D])
    pnorm = inp("pnorm", [128, 16])
    wgate = inp("wgate", [D, D])
    wproj = inp("wproj", [256, D])
    fnorm = inp("fnorm", [1, D])
    pT = inp("pT", [256, TT])
    cmat = inp("cmat", [128, 8, 128], BF16)
    tabs = inp("tabs", [128, 4, SMAX])
    out_p = nc.dram_tensor("out_p", [SP // 4, D], F32, kind="ExternalOutput").ap()
    out_s = nc.dram_tensor("out_s", [SS // 4, D], F32, kind="ExternalOutput").ap()
    Z = {k: nc.dram_tensor("Z" + k, [NCH, 128, S + 2 * PAD], BF16).ap() for k, S in seqs}
    CW = 1024
    NCK = TOT // CW
    ysend = nc.dram_tensor("ysend", [NCK * 256, CW], BF16)
    yall = nc.dram_tensor("yall", [NCK * 1024, CW], BF16)
    if dbg:
        dbg_z = nc.dram_tensor("dbg_z", [NCH, 128, SS + 2 * PAD], BF16, kind="ExternalOutput").ap()
        dbg_y = nc.dram_tensor("dbg_y", [(TOT // 1024) * 256, 1024], BF16, kind="ExternalOutput").ap()

    es = ExitStack()
    with es:
        sc = Sched(nc, es)
        op = sc.op
        cm = es.enter_context(nc.sbuf_tensor("cm", [128, 8, 128], BF16))
        ident, rdil, rdiff, ones_b = cm[:, 0, :], cm[:, 1, :], cm[:, 2, :], cm[:, 3, :]
        lam_t = es.enter_context(nc.sbuf_tensor("lam_t", [128, 8], F32))
        one_c = es.enter_context(nc.sbuf_tensor("one_c", [128, 1], F32))
        r_one = sc.res()
        r_cm = sc.res()
        r_lam = sc.res()

        with ExitStack() as ps:
            sb = lambda n, s, d: ps.enter_context(nc.sbuf_tensor(n, s, d))
            pp = lambda n, s, d: ps.enter_context(nc.psum_tensor(n, s, d))
            Wb = sb("Wb", [128, 16, NCH * 128], BF16)
            xs_t = [sb("xs%d" % i, [128, D], F32) for i in range(4)]
            junk = sb("junk", [128, D], BF16)
            st_t = [sb("st%d" % i, [128, 8], F32) for i in range(2)]
            xb_t = [sb("xb%d" % i, [128, D], BF16) for i in range(2)]
            uT_t = [sb("uT%d" % i, [128, 16, 512], BF16) for i in range(2)]
            tab_t = [sb("tab%d" % i, [128, 4, 512], F32) for i in range(2)]
            stg_t = [sb("stg%d" % i, [128, NCH, 512], BF16) for i in range(2)]
            zb_t = [sb("zb%d" % i, [128, 512], BF16) for i in range(2)]
            t1_t = [sb("t1%d" % i, [128, 512], F32) for i in range(2)]
            t2_t = [sb("t2%d" % i, [128, 512], F32) for i in range(2)]
            ge_t = [sb("ge%d" % i, [128, 512], F32) for i in range(2)]
            ones_f = sb("ones_f", [128, 512], F32)
            nm_t = sb("nm_t", [128, 16], F32)
            lv_t = sb("lv_t", [128, 4, 64], F32)
            zero_t = sb("zero_t", [128, PAD], BF16)
            pT_ps = [pp("pT%d" % i, [128, D], BF16) for i in range(2)]
            pZ = [pp("pZ%d" % i, [128, 512], F32) for i in range(2)]
            pR = [pp("pR%d" % i, [128, 512], F32) for i in range(2)]
            r_W, r_nm, r_ones, r_zero, r_lv = sc.res(), sc.res(), sc.res(), sc.res(), sc.res()
            r_xs, r_st, r_xb = sc.res(4), sc.res(2), sc.res(2)
            r_uT = [[[sc.res(), sc.res()] for _ in range(4)] for _ in range(2)]
            r_tab, r_zb, r_t1, r_t2, r_ge = sc.res(2), sc.res(2), sc.res(2), sc.res(2), sc.res(2)
            r_stg = [sc.res(NCH) for _ in range(2)]
            r_pT, r_pZ, r_pR = sc.res(2), sc.res(2), sc.res(2)

            op("sp", lambda e: e.dma_start(out=cm[:], in_=cmat), writes=[r_cm], dma="cm")
            op("sp", lambda e: e.dma_start(out=nm_t[:], in_=nmix), writes=[r_nm], dma="cm")
            op("sp", lambda e: e.dma_start(out=lv_t[:].rearrange("p a b -> p (a b)"),
                                           in_=lamv.rearrange("a b -> (a b)").partition_broadcast(128)),
               writes=[r_lv], dma="cm")
            op("dve", lambda e: e.memset(ones_f[:], 1.0), writes=[r_ones])
            op("dve", lambda e: e.memset(one_c[:], 1.0), writes=[r_one])
            op("dve", lambda e: e.memset(zero_t[:], 0.0), writes=[r_zero])
            for k, S in seqs:
                for c in range(7, 13):
                    for off in (0, PAD + S):
                        op("pool", lambda e, k=k, c=c, off=off: e.dma_start(
                            out=Z[k][c, :, off:off + PAD], in_=zero_t[:]), reads=[r_zero], dma="zp")
            op("dve", lambda e: e.tensor_tensor(out=lv_t[:, 0, :], in0=lv_t[:, 0, :], in1=lv_t[:, 1, :], op=ALU.mult),
               reads=[r_lv], writes=[r_lv])
            op("dve", lambda e: e.tensor_tensor(out=lv_t[:, 2, :], in0=lv_t[:, 2, :], in1=lv_t[:, 3, :], op=ALU.mult),
               reads=[r_lv], writes=[r_lv])
            op("dve", lambda e: e.reduce_sum(out=lam_t[:, 2:3], in_=lv_t[:, 0, :], axis=AX.X), reads=[r_lv], writes=[r_lam])
            op("dve", lambda e: e.reduce_sum(out=lam_t[:, 3:4], in_=lv_t[:, 2, :], axis=AX.X), reads=[r_lv, r_lam], writes=[r_lam])
            op("act", lambda e: e.activation(out=lam_t[:, 4:6], in_=lam_t[:, 2:4], func=AF.Exp), reads=[r_lam], writes=[r_lam])
            op("dve", lambda e: e.tensor_tensor(out=lam_t[:, 6:7], in0=lam_t[:, 4:5], in1=lam_t[:, 5:6], op=ALU.subtract),
               reads=[r_lam], writes=[r_lam])
            op("dve", lambda e: e.tensor_scalar(out=lam_t[:, 0:1], in0=lam_t[:, 6:7], scalar1=0.2, scalar2=None, op0=ALU.add),
               reads=[r_lam], writes=[r_lam])
            op("dve", lambda e: e.tensor_scalar(out=lam_t[:, 1:2], in0=lam_t[:, 0:1], scalar1=-1.0, scalar2=None, op0=ALU.mult),
               reads=[r_lam], writes=[r_lam])
            for kc in range(16):
                sl = kc % 3
                for hf in range(2):
                    cs = slice(hf * 896, (hf + 1) * 896)
                    op("sp", lambda e, kc=kc, sl=sl, cs=cs: e.dma_start(out=xs_t[sl][:, 0:896], in_=wh[kc * 128:(kc + 1) * 128, cs]),
                       writes=[r_xs[sl]], dma="x%d" % sl)
                    op("dve", lambda e, kc=kc, sl=sl, cs=cs: e.tensor_scalar(out=Wb[:, kc, cs], in0=xs_t[sl][:, 0:896],
                                                                            scalar1=nm_t[:, kc:kc + 1], scalar2=None, op0=ALU.mult),
                       reads=[r_xs[sl], r_nm], writes=[r_W])

            blocks = [(k, S, b) for k, S in seqs for b in range(S // 512)]
            import os as _os
            if _os.environ.get("KDBG_NBLK"):
                blocks = blocks[:int(_os.environ["KDBG_NBLK"])]
            gtile = [0]

            NX = 4
            NT = len(blocks) * 4

            def tile_info(g):
                bi, ti = g // 4, g % 4
                k, S, b = blocks[bi]
                return bi, ti, k, b * 512 + ti * 128

            def prep_load(g):
                if g >= NT:
                    return
                bi, ti, k, t0 = tile_info(g)
                xsl = g % NX
                op("sp", lambda e: e.dma_start(out=xs_t[xsl][:], in_=x_in[k][t0:t0 + 128, :]), writes=[r_xs[xsl]], dma="x%d" % xsl)

            def prep_A(g):
                if g >= NT:
                    return
                xsl, s2 = g % NX, g % 2
                op("act", lambda e: e.activation(out=junk[:], in_=xs_t[xsl][:], func=AF.Square, accum_out=st_t[s2][:, 0:1]),
                   reads=[r_xs[xsl]], writes=[r_st[s2]])
                op("dve", lambda e: e.tensor_scalar(out=st_t[s2][:, 1:2], in0=st_t[s2][:, 0:1], scalar1=1.0 / D, scalar2=EPS,
                                                    op0=ALU.mult, op1=ALU.add), reads=[r_st[s2]], writes=[r_st[s2]])
                op("act", lambda e: e.activation(out=st_t[s2][:, 2:3], in_=st_t[s2][:, 1:2], func=AF.Ln), reads=[r_st[s2]], writes=[r_st[s2]])
                op("act", lambda e: e.activation(out=st_t[s2][:, 3:4], in_=st_t[s2][:, 2:3], func=AF.Exp, scale=-0.5),
                   reads=[r_st[s2]], writes=[r_st[s2]])
                op("dve", lambda e: e.tensor_scalar(out=xb_t[s2][:], in0=xs_t[xsl][:], scalar1=st_t[s2][:, 3:4], scalar2=None, op0=ALU.mult),
                   reads=[r_xs[xsl], r_st[s2]], writes=[r_xb[s2]])
                prep_load(g + 2)

            def prep_B(g):
                if g >= NT:
                    return
                bi, ti, k, t0 = tile_info(g)
                s2 = g % 2
                us = bi % 2
                for kc in range(16):
                    op("pe", lambda e, kc=kc: e.transpose(pT_ps[s2][:, kc * 128:(kc + 1) * 128], xb_t[s2][:, kc * 128:(kc + 1) * 128], ident),
                       reads=[r_xb[s2], r_cm], writes=[r_pT[s2]])
                src = pT_ps[s2][:].rearrange("p (a b) -> p a b", b=128)
                op("act", lambda e: e.activation(out=uT_t[us][:, 0:8, ti * 128:(ti + 1) * 128], in_=src[:, 0:8, :], func=AF.Copy),
                   reads=[r_pT[s2]], writes=[r_uT[us][ti][0]])
                op("dve", lambda e: e.tensor_copy(out=uT_t[us][:, 8:16, ti * 128:(ti + 1) * 128], in_=src[:, 8:16, :]),
                   reads=[r_pT[s2]], writes=[r_uT[us][ti][1]])

            def rot_part(bi, c, zs):
                us = bi % 2
                isdil = c in ROPE_DIL
                rm = rdil if isdil else rdiff
                ti = 2 if isdil else 0
                op("pe", lambda e: e.matmul(pR[zs][:], lhsT=rm, rhs=zb_t[zs][:], start=True, stop=True),
                   reads=[r_zb[zs], r_cm], writes=[r_pR[zs]])
                op("dve", lambda e: e.tensor_tensor(out=t2_t[zs][:], in0=pR[zs][:], in1=tab_t[us][:, ti + 1, :], op=ALU.mult),
                   reads=[r_pR[zs], r_tab[us]], writes=[r_t2[zs]])
                op("dve", lambda e: e.tensor_tensor(out=stg_t[us][:, c, :], in0=t1_t[zs][:], in1=t2_t[zs][:], op=ALU.add),
                   reads=[r_t1[zs], r_t2[zs]], writes=[r_stg[us][c]])

            if blocks:
                prep_load(0)
                prep_load(1)
                prep_A(0)
                prep_A(1)
                prep_B(0)
                prep_A(2)
                prep_B(1)
                prep_A(3)
                prep_B(2)
                prep_B(3)
            zc = 0
            for bi, (k, S, b) in enumerate(blocks):
                us = bi % 2
                t0 = b * 512
                op("sp", lambda e, us=us, t0=t0: e.dma_start(out=tab_t[us][:], in_=tabs[:, :, t0:t0 + 512]), writes=[r_tab[us]], dma="tab%d" % us)
                pend = None
                _cut = _os.environ.get("KDBG_CUT", "")
                for c in range(NCH):
                    if _cut == "prep":
                        break
                    if _cut == "rope" and c not in (0,):
                        continue
                    if _cut == "copy" and c not in (2,):
                        continue
                    if _cut == "gate" and c not in (3,):
                        continue
                    zs = zc % 2
                    zc += 1
                    uall = [r for t in r_uT[us] for r in t]
                    for kc in range(16):
                        op("pe", lambda e, c=c, kc=kc, zs=zs, us=us: e.matmul(pZ[zs][:], lhsT=Wb[:, kc, c * 128:(c + 1) * 128],
                                                                              rhs=uT_t[us][:, kc, :], start=(kc == 0), stop=(kc == 15)),
                           reads=[r_W] + uall, writes=[r_pZ[zs]])
                    if c in ROPE_DIFF or c in ROPE_DIL:
                        ti_ = 2 if c in ROPE_DIL else 0
                        op("act", lambda e, zs=zs: e.activation(out=zb_t[zs][:], in_=pZ[zs][:], func=AF.Copy),
                           reads=[r_pZ[zs]], writes=[r_zb[zs], r_pZ[zs]])
                        op("dve", lambda e, zs=zs, us=us, ti_=ti_: e.tensor_tensor(out=t1_t[zs][:], in0=pZ[zs][:], in1=tab_t[us][:, ti_, :], op=ALU.mult),
                           reads=[r_pZ[zs], r_tab[us]], writes=[r_t1[zs], r_pZ[zs]])
                    elif c in COPY_CH:
                        op("act", lambda e, zs=zs, us=us, c=c: e.activation(out=stg_t[us][:, c, :], in_=pZ[zs][:], func=AF.Copy),
                           reads=[r_pZ[zs]], writes=[r_stg[us][c]])
                    else:
                        op("act", lambda e, zs=zs: e.activation(out=ge_t[zs][:], in_=pZ[zs][:], func=AF.Exp, scale=-1.0),
                           reads=[r_pZ[zs]], writes=[r_ge[zs]])
                        op("act", lambda e, zs=zs: e.activation(out=ge_t[zs][:], in_=ge_t[zs][:], func=AF.Ln, bias=one_c[:, 0:1]),
                           reads=[r_ge[zs], r_one], writes=[r_ge[zs]])
                        op("act", lambda e, zs=zs: e.activation(out=ge_t[zs][:], in_=ge_t[zs][:], func=AF.Exp, scale=-1.0),
                           reads=[r_ge[zs]], writes=[r_ge[zs]])
                        op("dve", lambda e, zs=zs, us=us, c=c: e.tensor_tensor(out=stg_t[us][:, c, :], in0=pZ[zs][:], in1=ge_t[zs][:], op=ALU.mult),
                           reads=[r_pZ[zs], r_ge[zs]], writes=[r_stg[us][c]])
                    if pend is not None:
                        rot_part(bi, *pend)
                        pend = None
                    if c in ROPE_DIFF or c in ROPE_DIL:
                        pend = (c, zs)
                    if c in (0, 3, 6, 9):
                        prep_A((bi + 1) * 4 + c // 3)
                    if c in (2, 5, 8, 11):
                        prep_B((bi + 1) * 4 + (c - 2) // 3)
                if pend is not None:
                    rot_part(bi, *pend)
                    pend = None
                if _cut:
                    continue
                op("pool", lambda e, k=k, us=us, t0=t0: e.dma_start(
                    out=Z[k][:, :, PAD + t0:PAD + t0 + 512].rearrange("c p t -> p c t"), in_=stg_t[us][:]),
                   reads=r_stg[us], dma="stg%d" % us)
            sc.flush()
            if dbg:
                op("sp", lambda e: e.dma_start(out=dbg_z, in_=Z["s"]), dma="dbg")
                sc.flush()

        if upto < 2:
            return nc
        T2 = 2048
        scale_b = 128.0 ** -0.5
        with ExitStack() as ps:
            sb = lambda n, s, d: ps.enter_context(nc.sbuf_tensor(n, s, d))
            pp = lambda n, s, d: ps.enter_context(nc.psum_tensor(n, s, d))
            qb_t = [[sb("q%d_%d" % (i, g), [128, T2], BF16) for g in range(3)] for i in range(2)]
            kb_t = [[sb("k%d_%d" % (i, g), [128, T2 + 128 * DILS[g]], BF16) for g in range(3)] for i in range(2)]
            vb_t = [[sb("v%d_%d" % (i, g), [128, T2 + 128 * DILS[g]], BF16) for g in range(3)] for i in range(2)]
            gb_t = [sb("gb%d" % i, [128, T2], BF16) for i in range(2)]
            accO = [sb("accO%d" % i, [128, T2], F32) for i in range(2)]
            accD = [sb("accD%d" % i, [128, T2], F32) for i in range(2)]
            PT_t = [sb("PT%d" % i, [128, 256], BF16) for i in range(3)]
            Vm_t = [sb("Vm%d" % i, [128, 128], BF16) for i in range(3)]
            yst = [sb("yst%d" % i, [128, T2], BF16) for i in range(2)]
            pS = [pp("pS%d" % i, [128, 512], F32) for i in range(2)]
            pO = [pp("pO%d" % i, [128, 512], F32) for i in range(2)]
            pD = [pp("pD%d" % i, [128, 512], F32) for i in range(2)]
            pV_ = [pp("pV%d" % i, [128, 1024], BF16) for i in range(2)]
            r_q = [sc.res(3) for _ in range(2)]
            r_k = [sc.res(3) for _ in range(2)]
            r_v = [sc.res(3) for _ in range(2)]
            r_gb, r_aO, r_aD, r_PT, r_Vm, r_yst = sc.res(2), sc.res(2), sc.res(2), sc.res(3), sc.res(3), sc.res(2)
            r_pS, r_pO, r_pD, r_pV = sc.res(2), sc.res(2), sc.res(2), sc.res(2)
            masks = cm[:, 4:8, :]

            def strided(t, start, n, r):
                if r == 1:
                    return t[:, start:start + n]
                return t[:, start:start + (n - 1) * r + 1:r]

            units = [(k, S, u) for k, S in seqs for u in range(S // T2)]
            tcnt = [0]
            sucnt = [0]
            pend_norm = []
            tokoff = {"p": 0, "s": SP}
            for ui, (k, S, u) in enumerate(units):
                us = ui % 2
                t0 = u * T2
                for g in range(3):
                    r = DILS[g]
                    op("sp", lambda e, g=g, us=us, k=k, t0=t0: e.dma_start(out=qb_t[us][g][:], in_=Z[k][C_BQ + g, :, PAD + t0:PAD + t0 + T2]),
                       writes=[r_q[us][g]], dma="q%d_%d" % (us, g))
                    op("sp", lambda e, g=g, r=r, us=us, k=k, t0=t0: e.dma_start(out=kb_t[us][g][:], in_=Z[k][C_BK + g, :, PAD + t0 - 64 * r:PAD + t0 + T2 + 64 * r]),
                       writes=[r_k[us][g]], dma="k%d_%d" % (us, g))
                    op("sp", lambda e, g=g, r=r, us=us, k=k, t0=t0: e.dma_start(out=vb_t[us][g][:], in_=Z[k][C_BV + g, :, PAD + t0 - 64 * r:PAD + t0 + T2 + 64 * r]),
                       writes=[r_v[us][g]], dma="v%d_%d" % (us, g))
                op("sp", lambda e, us=us, k=k, t0=t0: e.dma_start(out=gb_t[us][:], in_=Z[k][C_BG, :, PAD + t0:PAD + t0 + T2]), writes=[r_gb[us]], dma="gb%d" % us)
                tasks = []
                for g in range(3):
                    r = DILS[g]
                    Lu = T2 // r
                    nq = min(4, Lu // 128)
                    nsub = Lu // (nq * 128)
                    for rho in range(r):
                        for su in range(nsub):
                            l0 = su * nq * 128
                            first = (u == 0 and l0 == 0)
                            last = (u == S // T2 - 1 and l0 + nq * 128 == Lu)
                            sid = sucnt[0]
                            sucnt[0] += 1
                            for m in range(nq + 1):
                                tasks.append(dict(g=g, r=r, rho=rho, l0=l0, m=m, nq=nq, first=first, last=last, sid=sid))

                def front(t):
                    i = tcnt[0]
                    tcnt[0] += 1
                    t["i"] = i
                    g, r, rho, l0, m, nq = t["g"], t["r"], t["rho"], t["l0"], t["m"], t["nq"]
                    s2, s3, vh = i % 2, i % 3, i % 2
                    kcol = rho + r * (l0 + 128 * m)
                    ktile = strided(kb_t[us][g], kcol, 128, r)
                    vtile = strided(vb_t[us][g], kcol, 128, r)
                    op("pe", lambda e: e.transpose(pV_[vh][:, 0:128], vtile, ident), reads=[r_v[us][g], r_cm], writes=[r_pV[vh]])
                    if i % 2 == 0:
                        op("dve", lambda e: e.tensor_copy(out=Vm_t[s3][:], in_=pV_[vh][:, 0:128]), reads=[r_pV[vh]], writes=[r_Vm[s3]])
                    else:
                        op("act", lambda e: e.activation(out=Vm_t[s3][:], in_=pV_[vh][:, 0:128], func=AF.Copy), reads=[r_pV[vh]], writes=[r_Vm[s3]])
                    jlo = m - 1 if m >= 1 else None
                    jhi = m if m < nq else None
                    if jlo is not None and jhi is not None:
                        qap = strided(qb_t[us][g], rho + r * (l0 + 128 * jlo), 256, r)
                        mk_ = masks[:, 0:2, :].rearrange("p a b -> p (a b)")
                        N = 256
                    elif jhi is not None:
                        qap = strided(qb_t[us][g], rho + r * (l0 + 128 * jhi), 128, r)
                        mk_ = masks[:, 3, :] if t["first"] else masks[:, 1, :]
                        N = 128
                    else:
                        qap = strided(qb_t[us][g], rho + r * (l0 + 128 * jlo), 128, r)
                        mk_ = masks[:, 2, :] if t["last"] else masks[:, 0, :]
                        N = 128
                    t["N"], t["jlo"], t["jhi"] = N, jlo, jhi
                    op("pe", lambda e: e.matmul(pS[s2][:, 0:N], lhsT=ktile, rhs=qap, start=True, stop=False),
                       reads=[r_k[us][g], r_q[us][g]], writes=[r_pS[s2]])
                    op("pe", lambda e: e.matmul(pS[s2][:, 0:N], lhsT=ident, rhs=mk_, start=False, stop=True),
                       reads=[r_cm], writes=[r_pS[s2]])
                    op("act", lambda e: e.activation(out=PT_t[s3][:, 0:N], in_=pS[s2][:, 0:N], func=AF.Exp, scale=scale_b),
                       reads=[r_pS[s2]], writes=[r_PT[s3]])

                def back(t):
                    i = t["i"]
                    g, r, rho, l0, m, nq = t["g"], t["r"], t["rho"], t["l0"], t["m"], t["nq"]
                    s3 = i % 3
                    so = t["sid"] % 2
                    halves = []
                    if t["jlo"] is not None:
                        halves.append((t["jlo"], 0, False, True))
                    if t["jhi"] is not None:
                        halves.append((t["jhi"], t["N"] - 128, True, False))
                    for (j, c0, st_, sp_) in halves:
                        op("pe", lambda e, j=j, c0=c0, st_=st_, sp_=sp_: e.matmul(pO[so][:, j * 128:(j + 1) * 128], lhsT=Vm_t[s3][:],
                                                                                    rhs=PT_t[s3][:, c0:c0 + 128], start=st_, stop=sp_),
                           reads=[r_Vm[s3], r_PT[s3]], writes=[r_pO[so]])
                        op("pe", lambda e, j=j, c0=c0, st_=st_, sp_=sp_: e.matmul(pD[so][:, j * 128:(j + 1) * 128], lhsT=ones_b,
                                                                                    rhs=PT_t[s3][:, c0:c0 + 128], start=st_, stop=sp_),
                           reads=[r_cm, r_PT[s3]], writes=[r_pD[so]])
                    if m == nq:
                        n = nq * 128
                        dO = strided(accO[us], rho + r * l0, n, r)
                        dD = strided(accD[us], rho + r * l0, n, r)
                        if g == 0:
                            op("dve", lambda e: e.tensor_copy(out=dO, in_=pO[so][:, 0:n]), reads=[r_pO[so]], writes=[r_aO[us]])
                            op("act", lambda e: e.activation(out=dD, in_=pD[so][:, 0:n], func=AF.Copy), reads=[r_pD[so]], writes=[r_aD[us]])
                        else:
                            op("dve", lambda e: e.tensor_tensor(out=dO, in0=dO, in1=pO[so][:, 0:n], op=ALU.add),
                               reads=[r_pO[so], r_aO[us]], writes=[r_aO[us]])
                            op("dve", lambda e: e.tensor_tensor(out=dD, in0=dD, in1=pD[so][:, 0:n], op=ALU.add),
                               reads=[r_pD[so], r_aD[us]], writes=[r_aD[us]])

                front(tasks[0])
                for i_ in range(len(tasks)):
                    if i_ + 1 < len(tasks):
                        front(tasks[i_ + 1])
                    back(tasks[i_])
                    if i_ in (24, 40, 64) and pend_norm and pend_norm[0]:
                        pend_norm[0].pop(0)()
                while pend_norm and pend_norm[0]:
                    pend_norm[0].pop(0)()
                pend_norm.clear()
                c0_ = tokoff[k] + t0

                def norm_a(us=us):
                    op("act", lambda e: e.activation(out=accD[us][:], in_=accD[us][:], func=AF.Ln), reads=[r_aD[us]], writes=[r_aD[us]])
                    op("act", lambda e: e.activation(out=accD[us][:], in_=accD[us][:], func=AF.Exp, scale=-1.0), reads=[r_aD[us]], writes=[r_aD[us]])

                def norm_b(us=us):
                    op("pool", lambda e: e.tensor_tensor(out=accO[us][:], in0=accO[us][:], in1=accD[us][:], op=ALU.mult),
                       reads=[r_aD[us], r_aO[us]], writes=[r_aO[us]])

                def norm_c(us=us, c0_=c0_):
                    op("dve", lambda e: e.tensor_tensor(out=yst[us][:], in0=accO[us][:], in1=gb_t[us][:], op=ALU.mult),
                       reads=[r_aO[us], r_gb[us]], writes=[r_yst[us]])
                    op("pool", lambda e: e.dma_start(out=ysend.ap()[(c0_ // CW) * 256:(c0_ // CW + T2 // CW) * 256, :].rearrange("(j r) t -> r j t", r=256)[128:256, :, :],
                                                     in_=yst[us][:].rearrange("p (j t) -> p j t", t=CW)),
                       reads=[r_yst[us]], dma="yst%d" % us)

                pend_norm.append([norm_a, norm_b, norm_c])
            while pend_norm and pend_norm[0]:
                pend_norm[0].pop(0)()
            sc.flush()

        if upto < 3:
            return nc
        sb = lambda n, s, d: es.enter_context(nc.sbuf_tensor(n, s, d))
        Wo = sb("Wo", [128, 8, D], BF16)
        Wg = sb("Wg", [128, 16, D], BF16)
        pn_t = sb("pn_t", [128, 16], F32)
        wst = [sb("wst%d" % i, [128, D], F32) for i in range(1)]
        r_Wo, r_Wg, r_Wp, r_fn, r_pn = sc.res(), sc.res(), sc.res(), sc.res(), sc.res()
        r_wst = sc.res(1)
        with ExitStack() as ps:
            sb = lambda n, s, d: ps.enter_context(nc.sbuf_tensor(n, s, d))
            pp = lambda n, s, d: ps.enter_context(nc.psum_tensor(n, s, d))
            Kt = sb("Kt", [128, SMAX], BF16)
            Vt = sb("Vt", [128, SMAX // 128, 130], BF16)
            vld = [sb("vld0", [128, 2048], BF16)] * 2
            Qb = [sb("Qb%d" % i, [128, 512], BF16) for i in range(2)]
            Gb = [sb("Gb%d" % i, [128, 512], BF16) for i in range(2)]
            PT3 = [sb("PT3_%d" % i, [128, 1024], BF16) for i in range(3)]
            Osb = [sb("Osb0", [128, 3, 512], F32)] * 2
            fin = [sb("fin%d" % i, [128, 16], F32) for i in range(2)]
            o_t = [sb("o_t%d" % i, [128, 4, 128], F32) for i in range(2)]
            sq_t = sb("sq_t", [128, 4, 128], F32)
            on_t = [sb("on_t%d" % i, [128, 4, 128], BF16) for i in range(2)]
            yst3 = [sb("yst3_%d" % i, [128, 512], BF16) for i in range(2)]
            sl_t = sb("sl_t", [128, 128], F32)
            pS3 = [pp("pS3_%d" % i, [128, 1024], F32) for i in range(2)]
            pO3 = pp("pO3", [128, 3, 512], F32)
            pY = pp("pY", [128, 512], BF16)
            r_Kt, r_Vt, r_sl = sc.res(), sc.res(), sc.res()
            r_vld, r_Qb, r_Gb, r_PT3, r_Osb, r_fin, r_o, r_on, r_yst3 = ([sc.res()] * 2, sc.res(2), sc.res(2), sc.res(3), [sc.res()] * 2,
                                                                          sc.res(2), sc.res(2), sc.res(2), sc.res(2))
            r_sq = sc.res()
            r_pS3, r_pO3, r_pY = sc.res(2), sc.res(), sc.res()

            def oacc(c, j):
                idx = c * 4 + j
                return idx // 3, (idx % 3) * 129

            op("sp", lambda e: e.dma_start(out=sl_t[:], in_=subln.rearrange("a b -> (a b)").partition_broadcast(128)), writes=[r_sl], dma="sl")
            op("dve", lambda e: e.tensor_scalar(out=sl_t[:], in0=sl_t[:], scalar1=0.8, scalar2=None, op0=ALU.mult), reads=[r_sl], writes=[r_sl])
            def load_tail_weights():
                op("sp", lambda e: e.dma_start(out=pn_t[:], in_=pnorm), writes=[r_pn], dma="c5")
                wl = 0
                for kc in range(8):
                    hh, ab = kc // 2, kc % 2
                    r0 = ab * 512 + hh * 128
                    sl = 0
                    wl += 1
                    op("sp", lambda e, r0=r0, sl=sl: e.dma_start(out=wst[sl][:], in_=wout[r0:r0 + 128, :]), writes=[r_wst[sl]], dma="wst%d" % sl)
                    op("dve", lambda e, kc=kc, sl=sl: e.tensor_copy(out=Wo[:, kc, :], in_=wst[sl][:]), reads=[r_wst[sl]], writes=[r_Wo])
                for kc in range(16):
                    sl = 0
                    wl += 1
                    op("sp", lambda e, kc=kc, sl=sl: e.dma_start(out=wst[sl][:], in_=wgate[kc * 128:(kc + 1) * 128, :]), writes=[r_wst[sl]], dma="wst%d" % sl)
                    op("dve", lambda e, kc=kc, sl=sl: e.tensor_scalar(out=Wg[:, kc, :], in0=wst[sl][:], scalar1=pn_t[:, kc:kc + 1], scalar2=None, op0=ALU.mult),
                       reads=[r_wst[sl], r_pn], writes=[r_Wg])
            vcnt = 0
            qcnt = 0
            pend_fin = []
            r_ys = sc.res(NCK)
            r_cc = sc.res()
            for k, S in seqs:
                nkt = S // 128
                op("sp", lambda e, k=k, S=S: e.dma_start(out=Kt[:, 0:S], in_=Z[k][C_AK, :, PAD:PAD + S]), writes=[r_Kt], dma="kt")
                op("dve", lambda e: e.memset(Vt[:, :, 128:130], 1.0), writes=[r_Vt])
                for vb in range(S // 2048):
                    vs = vcnt % 2
                    vcnt += 1
                    op("sp", lambda e, k=k, vb=vb, vs=vs: e.dma_start(out=vld[vs][:], in_=Z[k][C_AV, :, PAD + vb * 2048:PAD + (vb + 1) * 2048]),
                       writes=[r_vld[vs]], dma="vld0")
                    for q4 in range(4):
                        for jj in range(4):
                            tt = q4 * 4 + jj
                            op("pe", lambda e, vs=vs, tt=tt, jj=jj: e.transpose(pY[:, jj * 128:(jj + 1) * 128], vld[vs][:, tt * 128:(tt + 1) * 128], ident),
                               reads=[r_vld[vs], r_cm], writes=[r_pY])
                        kt0 = vb * 16 + q4 * 4
                        op("dve", lambda e, kt0=kt0: e.tensor_copy(out=Vt[:, kt0:kt0 + 4, 0:128], in_=pY[:].rearrange("p (a b) -> p a b", b=128)),
                           reads=[r_pY], writes=[r_Vt])
                for qb in range(S // 512):
                    qs = qcnt % 2
                    qcnt += 1
                    q0 = qb * 512
                    op("sp", lambda e, k=k, q0=q0, qs=qs: e.dma_start(out=Qb[qs][:], in_=Z[k][C_AQ, :, PAD + q0:PAD + q0 + 512]),
                       writes=[r_Qb[qs]], dma="qb%d" % qs)
                    op("sp", lambda e, k=k, q0=q0, qs=qs: e.dma_start(out=Gb[qs][:], in_=Z[k][C_AG, :, PAD + q0:PAD + q0 + 512]),
                       writes=[r_Gb[qs]], dma="qb%d" % qs)

                    def front3(kt, qs=qs):
                        s2, s3 = kt % 2, kt % 3
                        for c in range(2):
                            op("pe", lambda e, c=c: e.matmul(pS3[s2][:, c * 512:(c + 1) * 512], lhsT=Kt[c * 64:(c + 1) * 64, kt * 128:(kt + 1) * 128],
                                                             rhs=Qb[qs][c * 64:(c + 1) * 64, :], start=True, stop=True),
                               reads=[r_Kt, r_Qb[qs]], writes=[r_pS3[s2]])
                        op("act", lambda e: e.activation(out=PT3[s3][:], in_=pS3[s2][:], func=AF.Exp, scale=0.125),
                           reads=[r_pS3[s2]], writes=[r_PT3[s3]])

                    def back3(kt, nkt=nkt):
                        s3 = kt % 3
                        for c in range(2):
                            for j in range(4):
                                bk, off = oacc(c, j)
                                op("pe", lambda e, c=c, j=j, bk=bk, off=off: e.matmul(pO3[:, bk, off:off + 129],
                                                                                       lhsT=PT3[s3][:, c * 512 + j * 128:c * 512 + (j + 1) * 128],
                                                                                       rhs=Vt[:, kt, 0:129], start=(kt == 0 and off == 0), stop=(kt == nkt - 1),
                                                                                       skip_group_check=True),
                                   reads=[r_PT3[s3], r_Vt], writes=[r_pO3])

                    front3(0)
                    front3(1)
                    for kt in range(nkt):
                        if kt + 2 < nkt:
                            front3(kt + 2)
                        back3(kt)
                        if kt == 1 and pend_fin and len(pend_fin[0]) == 2:
                            pend_fin[0].pop(0)()
                        if kt == 14 and pend_fin:
                            for f_ in pend_fin.pop():
                                f_()
                    if pend_fin:
                        for f_ in pend_fin.pop():
                            f_()
                    if qcnt == 1:
                        load_tail_weights()
                    fs = qs
                    O = Osb[fs]
                    op("dve", lambda e, O=O: e.tensor_copy(out=O[:, 0:2, 0:387], in_=pO3[:, 0:2, 0:387]), reads=[r_pO3], writes=[r_Osb[fs]])
                    op("dve", lambda e, O=O: e.tensor_copy(out=O[:, 2, 0:258], in_=pO3[:, 2, 0:258]), reads=[r_pO3], writes=[r_Osb[fs]])
                    c0_ = tokoff[k] + q0

                    def finB(fs=fs, O=O):
                        f = fin[fs]
                        for c in range(2):
                            for j in range(4):
                                bk, off = oacc(c, j)
                                op("dve", lambda e, c=c, j=j, bk=bk, off=off: e.reciprocal(out=f[:, c * 4 + j:c * 4 + j + 1], in_=O[:, bk, off + 128:off + 129]),
                                   reads=[r_Osb[fs]], writes=[r_fin[fs]])
                        op("dve", lambda e: e.tensor_scalar(out=f[:, 4:8], in0=f[:, 4:8], scalar1=lam_t[:, 1:2], scalar2=None, op0=ALU.mult),
                           reads=[r_fin[fs], r_lam], writes=[r_fin[fs]])
                        ot = o_t[fs]
                        for j in range(4):
                            b0, o0 = oacc(0, j)
                            b1, o1 = oacc(1, j)
                            op("dve", lambda e, j=j, b0=b0, o0=o0: e.tensor_scalar(out=ot[:, j, :], in0=O[:, b0, o0:o0 + 128], scalar1=f[:, j:j + 1],
                                                                                 scalar2=None, op0=ALU.mult),
                               reads=[r_Osb[fs], r_fin[fs]], writes=[r_o[fs]])
                            op("dve", lambda e, j=j, b1=b1, o1=o1: e.scalar_tensor_tensor(out=ot[:, j, :], in0=O[:, b1, o1:o1 + 128], scalar=f[:, 4 + j:5 + j],
                                                                                        in1=ot[:, j, :], op0=ALU.mult, op1=ALU.add),
                               reads=[r_Osb[fs], r_fin[fs], r_o[fs]], writes=[r_o[fs]])
                        op("pool", lambda e: e.tensor_tensor(out=sq_t[:], in0=ot[:], in1=ot[:], op=ALU.mult), reads=[r_o[fs]], writes=[r_sq])
                        op("dve", lambda e: e.reduce_sum(out=f[:, 8:12], in_=sq_t[:], axis=AX.X), reads=[r_sq], writes=[r_fin[fs]])
                        op("dve", lambda e: e.tensor_scalar(out=f[:, 8:12], in0=f[:, 8:12], scalar1=1.0 / 128, scalar2=EPS, op0=ALU.mult, op1=ALU.add),
                           reads=[r_fin[fs]], writes=[r_fin[fs]])

                    def finC(fs=fs, qs=qs, c0_=c0_):
                        f = fin[fs]
                        ot = o_t[fs]
                        ont = on_t[fs]
                        op("act", lambda e: e.activation(out=f[:, 8:12], in_=f[:, 8:12], func=AF.Ln), reads=[r_fin[fs]], writes=[r_fin[fs]])
                        op("act", lambda e: e.activation(out=f[:, 12:16], in_=f[:, 8:12], func=AF.Exp, scale=-0.5), reads=[r_fin[fs]], writes=[r_fin[fs]])
                        for j in range(4):
                            op("dve", lambda e, j=j: e.scalar_tensor_tensor(out=ont[:, j, :], in0=ot[:, j, :], scalar=f[:, 12 + j:13 + j], in1=sl_t[:],
                                                                          op0=ALU.mult, op1=ALU.mult),
                               reads=[r_o[fs], r_fin[fs], r_sl], writes=[r_on[fs]])
                        for j in range(4):
                            op("pe", lambda e, j=j: e.transpose(pY[:, j * 128:(j + 1) * 128], ont[:, j, :], ident),
                               reads=[r_on[fs], r_cm], writes=[r_pY])
                        op("dve", lambda e: e.tensor_tensor(out=yst3[fs][:], in0=pY[:], in1=Gb[qs][:], op=ALU.mult),
                           reads=[r_pY, r_Gb[qs]], writes=[r_yst3[fs]])
                        jc = c0_ // CW
                        op("pool", lambda e: e.dma_start(out=ysend.ap()[(c0_ // CW) * 256:(c0_ // CW) * 256 + 128, c0_ % CW:c0_ % CW + 512], in_=yst3[fs][:]),
                           reads=[r_yst3[fs]], writes=[r_ys[jc]], dma="y3_%d" % fs)
                        if (c0_ + 512) % CW == 0:
                            op("pool", lambda e, j=jc: e.collective_compute("AllGather", ALU.bypass, replica_groups=[[0, 1, 2, 3], [4, 5, 6, 7]],
                                                                            ins=[ysend.ap()[j * 256:(j + 1) * 256, :].opt()],
                                                                            outs=[yall.ap()[j * 1024:(j + 1) * 1024, :].opt()]),
                               reads=[r_ys[jc]], writes=[r_cc], dma="cc", inc=1)

                    pend_fin.append([finB, finC])
            if pend_fin:
                for f_ in pend_fin.pop():
                    f_()
            sc.flush()

        if upto < 4:
            return nc
        if dbg:
            op("sp", lambda e: e.dma_start(out=dbg_y, in_=ysend.ap()), dma="dbg")
            sc.flush()

        if upto < 5:
            return nc
        with ExitStack() as ps:
            sb = lambda n, s, d: ps.enter_context(nc.sbuf_tensor(n, s, d))
            pp = lambda n, s, d: ps.enter_context(nc.psum_tensor(n, s, d))
            Wp = sb("Wp", [128, 2, D], BF16)
            fn_t = sb("fn_t", [128, D], F32)
            x5 = [wst[0]] + [sb("x5_%d" % i, [128, D], F32) for i in range(1, 3)]
            h1 = [sb("h1_%d" % i, [128, D], F32) for i in range(2)]
            hb = [sb("hb%d" % i, [128, D], BF16) for i in range(2)]
            hT = [sb("hT%d" % i, [128, 16, 128], BF16) for i in range(2)]
            yT = [sb("yT%d" % i, [128, 8, 512], BF16) for i in range(2)]
            pf = [sb("pf%d" % i, [128, 2, 512], F32) for i in range(2)]
            pb = [sb("pb%d" % i, [128, 2, 512], BF16) for i in range(2)]
            e5 = [sb("e5_%d" % i, [128, 512], F32) for i in range(2)]
            st5 = [sb("st5_%d" % i, [128, 8], F32) for i in range(2)]
            st5b = [sb("st5b_%d" % i, [128, 8], F32) for i in range(2)]
            r_st5b = sc.res(2)
            junk5 = sb("junk5", [128, D], BF16)
            pH = [pp("pH%d" % i, [128, 512], F32) for i in range(2)]
            pT5 = pp("pT5", [128, D], BF16)
            pG = [pp("pG%d" % i, [128, 512], F32) for i in range(2)]
            pP = [pp("pP%d" % i, [128, 512], F32) for i in range(2)]
            r_x5, r_h1, r_hb, r_yT, r_pf, r_pb, r_e5, r_st5 = (sc.res(3), sc.res(2), sc.res(2), sc.res(2), sc.res(2), sc.res(2), sc.res(2), sc.res(2))
            r_hT = [sc.res(2) for _ in range(2)]
            r_pH, r_pT5, r_pG, r_pP = sc.res(2), sc.res(), sc.res(2), sc.res(2)

            op("sp", lambda e: e.dma_start(out=fn_t[:], in_=fnorm.rearrange("a b -> (a b)").partition_broadcast(128)), writes=[r_fn], dma="c5")
            wl = 0
            for kc in range(2):
                sl = wl % 2
                wl += 1
                op("sp", lambda e, kc=kc, sl=sl: e.dma_start(out=x5[1 + sl][:], in_=wproj[kc * 128:(kc + 1) * 128, :]), writes=[r_x5[1 + sl]], dma="x5_%d" % (1 + sl))
                op("dve", lambda e, kc=kc, sl=sl: e.tensor_copy(out=Wp[:, kc, :], in_=x5[1 + sl][:]), reads=[r_x5[1 + sl]], writes=[r_Wp])

            rk = {}

            def rank_of(e):
                if "r" not in rk:
                    rk["r"] = e.partition_id() % 4
                return rk["r"]

            def ycols(e, nq4, cst):
                assert nq4 % CW == 0
                ck = rank_of(e) * ((nq4 // CW) * 1024) + (cst // CW) * 1024
                off = cst % CW
                return yall.ap()[bass.ds(ck, 1024), off:off + 512].rearrange("(c p) t -> p c t", p=128)

            tails = [("p", SP, out_p, 0), ("s", SS, out_s, SP // 4)]
            cnt = dict(g=0, h=0)
            tiles = []
            blks = []
            for k, S, outp, poff in tails:
                nq4 = S // 4
                ybase = 0 if k == "p" else SP
                for b4 in range(nq4 // 512):
                    lc0 = b4 * 512
                    bidx = len(blks)
                    blks.append(dict(ys=bidx % 2, lc0=lc0, nq4=nq4, ybase=ybase, poff=poff))
                    for ti in range(4):
                        t = len(tiles)
                        tiles.append(dict(k=k, outp=outp, l0=lc0 + ti * 128, ys=bidx % 2, ti=ti, s2=t % 2, xsl=t % 3, bidx=bidx))

            def blockload(bidx):
                if bidx >= len(blks):
                    return
                B_ = blks[bidx]
                ys, lc0, nq4, ybase, poff = B_["ys"], B_["lc0"], B_["nq4"], B_["ybase"], B_["poff"]
                need_cc = (SP // CW) if ybase == 0 else NCK

                def ld_y(e):
                    return e.dma_start(out=yT[ys][:], in_=ycols(e, nq4, ybase + lc0))

                op("sp", ld_y, writes=[r_yT[ys]], dma="yT%d" % ys)
                op("sp", lambda e: e.dma_start(out=pf[ys][:], in_=pT[:, poff + lc0:poff + lc0 + 512].rearrange("(c p) t -> p c t", p=128)),
                   writes=[r_pf[ys]], dma="yT%d" % ys)
                op("pool", lambda e: e.tensor_copy(out=pb[ys][:], in_=pf[ys][:]), reads=[r_pf[ys]], writes=[r_pb[ys]])

            def S1(t):
                T_ = tiles[t]
                k, l0, ys, ti, s2, xsl = T_["k"], T_["l0"], T_["ys"], T_["ti"], T_["s2"], T_["xsl"]
                if ti == 1:
                    blockload(T_["bidx"] + 1)
                op("sp", lambda e: e.dma_start(out=x5[xsl][:], in_=xq_in[k][l0:l0 + 128, :]), writes=[r_x5[xsl]], dma="x5_%d" % xsl)
                for cc in range(4):
                    hs = cnt["h"] % 2
                    cnt["h"] += 1
                    for kc in range(8):
                        op("pe", lambda e, kc=kc, cc=cc, hs=hs: e.matmul(pH[hs][:], lhsT=yT[ys][:, kc, ti * 128:(ti + 1) * 128],
                                                                        rhs=Wo[:, kc, cc * 512:(cc + 1) * 512], start=(kc == 0), stop=(kc == 7)),
                           reads=[r_yT[ys], r_Wo], writes=[r_pH[hs]])
                    op("dve", lambda e, cc=cc, hs=hs: e.tensor_tensor(out=h1[s2][:, cc * 512:(cc + 1) * 512], in0=pH[hs][:],
                                                                      in1=x5[xsl][:, cc * 512:(cc + 1) * 512], op=ALU.add),
                       reads=[r_pH[hs], r_x5[xsl]], writes=[r_h1[s2]])

            def S2(t):
                s2 = tiles[t]["s2"]
                st = st5[s2]
                op("act", lambda e: e.activation(out=junk5[:], in_=h1[s2][:], func=AF.Square, accum_out=st[:, 0:1]),
                   reads=[r_h1[s2]], writes=[r_st5[s2]])
                op("dve", lambda e: e.tensor_scalar(out=st[:, 1:2], in0=st[:, 0:1], scalar1=1.0 / D, scalar2=EPS, op0=ALU.mult, op1=ALU.add),
                   reads=[r_st5[s2]], writes=[r_st5[s2]])
                op("act", lambda e: e.activation(out=st[:, 2:3], in_=st[:, 1:2], func=AF.Ln), reads=[r_st5[s2]], writes=[r_st5[s2]])
                op("act", lambda e: e.activation(out=st[:, 3:4], in_=st[:, 2:3], func=AF.Exp, scale=-0.5), reads=[r_st5[s2]], writes=[r_st5[s2]])
                op("act", lambda e: e.activation(out=hb[s2][:], in_=h1[s2][:], func=AF.Copy, scale=st[:, 3:4]),
                   reads=[r_h1[s2], r_st5[s2]], writes=[r_hb[s2]])

            def S3(t):
                s2 = tiles[t]["s2"]
                for kc in range(16):
                    op("pe", lambda e, kc=kc: e.transpose(pT5[:, kc * 128:(kc + 1) * 128], hb[s2][:, kc * 128:(kc + 1) * 128], ident),
                       reads=[r_hb[s2], r_cm], writes=[r_pT5])
                src5 = pT5[:].rearrange("p (a b) -> p a b", b=128)
                op("dve", lambda e: e.tensor_copy(out=hT[s2][:, 0:8, :], in_=src5[:, 0:8, :]), reads=[r_pT5], writes=[r_hT[s2][0]])
                op("act", lambda e: e.activation(out=hT[s2][:, 8:16, :], in_=src5[:, 8:16, :], func=AF.Copy), reads=[r_pT5], writes=[r_hT[s2][1]])

            def S4(t):
                T_ = tiles[t]
                ys, ti, s2, xsl = T_["ys"], T_["ti"], T_["s2"], T_["xsl"]
                for cc in range(4):
                    gs = cnt["g"] % 2
                    cnt["g"] += 1
                    cs = slice(cc * 512, (cc + 1) * 512)
                    for kc in range(16):
                        op("pe", lambda e, kc=kc, cs=cs, gs=gs: e.matmul(pG[gs][:], lhsT=hT[s2][:, kc, :], rhs=Wg[:, kc, cs],
                                                                        start=(kc == 0), stop=(kc == 15)),
                           reads=r_hT[s2] + [r_Wg], writes=[r_pG[gs]])
                    for kc in range(2):
                        op("pe", lambda e, kc=kc, cs=cs, gs=gs: e.matmul(pP[gs][:], lhsT=pb[ys][:, kc, ti * 128:(ti + 1) * 128], rhs=Wp[:, kc, cs],
                                                                        start=(kc == 0), stop=(kc == 1)),
                           reads=[r_pb[ys], r_Wp], writes=[r_pP[gs]])
                    op("act", lambda e, gs=gs: e.activation(out=e5[gs][:], in_=pG[gs][:], func=AF.Exp, scale=-1.0), reads=[r_pG[gs]], writes=[r_e5[gs]])
                    op("act", lambda e, gs=gs: e.activation(out=e5[gs][:], in_=e5[gs][:], func=AF.Ln, bias=one_c[:, 0:1]),
                       reads=[r_e5[gs], r_one], writes=[r_e5[gs]])
                    op("act", lambda e, gs=gs: e.activation(out=e5[gs][:], in_=e5[gs][:], func=AF.Exp, scale=-1.0),
                       reads=[r_e5[gs]], writes=[r_e5[gs]])
                    op("dve", lambda e, gs=gs: e.tensor_tensor(out=e5[gs][:], in0=e5[gs][:], in1=pP[gs][:], op=ALU.mult),
                       reads=[r_e5[gs], r_pP[gs]], writes=[r_e5[gs]])
                    op("pool", lambda e, gs=gs, cs=cs: e.tensor_tensor(out=x5[xsl][:, cs], in0=h1[s2][:, cs], in1=e5[gs][:], op=ALU.add),
                       reads=[r_e5[gs], r_h1[s2], r_x5[xsl]], writes=[r_x5[xsl]])

            def S5(t):
                T_ = tiles[t]
                s2, xsl, outp, l0 = T_["s2"], T_["xsl"], T_["outp"], T_["l0"]
                st = st5b[s2]
                op("act", lambda e: e.activation(out=junk5[:], in_=x5[xsl][:], func=AF.Square, accum_out=st[:, 4:5]),
                   reads=[r_x5[xsl]], writes=[r_st5b[s2]])
                op("dve", lambda e: e.tensor_scalar(out=st[:, 5:6], in0=st[:, 4:5], scalar1=1.0 / D, scalar2=EPS, op0=ALU.mult, op1=ALU.add),
                   reads=[r_st5b[s2]], writes=[r_st5b[s2]])
                op("act", lambda e: e.activation(out=st[:, 6:7], in_=st[:, 5:6], func=AF.Ln), reads=[r_st5b[s2]], writes=[r_st5b[s2]])
                op("act", lambda e: e.activation(out=st[:, 7:8], in_=st[:, 6:7], func=AF.Exp, scale=-0.5), reads=[r_st5b[s2]], writes=[r_st5b[s2]])
                op("dve", lambda e: e.scalar_tensor_tensor(out=x5[xsl][:], in0=x5[xsl][:], scalar=st[:, 7:8], in1=fn_t[:],
                                                           op0=ALU.mult, op1=ALU.mult),
                   reads=[r_st5b[s2], r_fn, r_x5[xsl]], writes=[r_x5[xsl]])
                op("pool", lambda e: e.dma_start(out=outp[l0:l0 + 128, :], in_=x5[xsl][:]),
                   reads=[r_x5[xsl]], writes=[r_x5[xsl]], dma="o5_%d" % xsl)

            NTL = len(tiles)
            blockload(0)
            S1(0)
            S2(0)
            for t in range(NTL):
                if t + 1 < NTL:
                    S1(t + 1)
                S3(t)
                if t >= 1:
                    S5(t - 1)
                if t + 1 < NTL:
                    S2(t + 1)
                S4(t)
            S5(NTL - 1)
            sc.flush(final=True)
    return nc


_CACHE = {}


def _prep_inputs(inputs, SP, SS):
    f = lambda a: np.ascontiguousarray(np.asarray(a, dtype=np.float32))
    xP, xS = f(inputs["x_prompt"]), f(inputs["x_sample"])
    pP, pS = f(inputs["p_prompt"])[0], f(inputs["p_sample"])[0]
    w_in = f(inputs["w_in"])[0]
    cm, tabs = _consts(max(SP, SS))
    to_pk = lambda v: np.ascontiguousarray(v.reshape(16, 128).T)
    lamv = np.stack([f(inputs[n])[0] for n in ("lam_q1", "lam_k1", "lam_q2", "lam_k2")], 0)
    common = dict(nmix=to_pk(f(inputs["norm_mix"])[0]), lamv=lamv, subln=f(inputs["subln"]).reshape(1, 128),
                  wout=f(inputs["w_out"])[0], pnorm=to_pk(f(inputs["ple_norm"])[0]), wgate=f(inputs["w_ple_gate"])[0],
                  wproj=f(inputs["w_ple_proj"])[0], fnorm=f(inputs["final_norm"]).reshape(1, D), cmat=cm, tabs=tabs)
    maps = []
    for c in range(8):
        b, h = c // 4, c % 4
        cols = []
        for base in (0, 512, 1024, 1536):
            cols.append(np.arange(base + h * 128, base + (h + 1) * 128))
        for base in (2048, 3584, 5120):
            for g in range(3):
                cols.append(np.arange(base + g * 512 + h * 128, base + g * 512 + (h + 1) * 128))
        cols.append(np.arange(6656 + h * 128, 6656 + (h + 1) * 128))
        cols = np.concatenate(cols)
        qp, qs = SP // 4, SS // 4
        pT = np.concatenate([pP[b, h * qp:(h + 1) * qp].T, pS[b, h * qs:(h + 1) * qs].T], 1)
        m = dict(common)
        m.update(xp=xP[b], xs=xS[b], xqp=xP[b, h * qp:(h + 1) * qp], xqs=xS[b, h * qs:(h + 1) * qs], wh=np.ascontiguousarray(w_in[:, cols]), pT=np.ascontiguousarray(pT))
        maps.append(m)
    return maps


def kernel(_dbg=False, _upto=5, **inputs):
    SP = inputs["x_prompt"].shape[1]
    SS = inputs["x_sample"].shape[1]
    key = (SP, SS, _dbg, _upto)
    if key not in _CACHE:
        _CACHE[key] = build(SP, SS, _dbg, _upto)
    nc = _CACHE[key]
    maps = _prep_inputs(inputs, SP, SS)
    res = run_bass_kernel_spmd(nc, maps, core_ids=list(range(8)))
    if _dbg:
        return res
    yp = np.zeros((2, SP, D), np.float32)
    ys = np.zeros((2, SS, D), np.float32)
    qp, qs = SP // 4, SS // 4
    for c in range(8):
        b, h = c // 4, c % 4
        yp[b, h * qp:(h + 1) * qp] = res.results[c]["out_p"]
        ys[b, h * qs:(h + 1) * qs] = res.results[c]["out_s"]
    return (yp, ys)
```
